# Optimizing a Trainium2 kernel written in Bass

```python
import math
import jax, jax.numpy as jnp
from jax import lax
import numpy as np

D_MODEL = 1024
BATCH = 4
SEQ = 4096
DEPTH = 1

EPS = 1e-6
ROPE_THETA = 10000.0
GLA_HEADS = 4
GLA_DK = 128
GLA_DV = 256
GLA_RANK = 16
GLA_TAU = 16.0
GLA_CHUNK = 64
GLA_QK = GLA_HEADS * GLA_DK
GLA_V = GLA_HEADS * GLA_DV
DIL_GROUPS = ((128, 1), (512, 4), (2048, 16))
DIL_HEADS = 4
DIL_HD = 128
DIL_BLOCK = 128
DIL_QK = len(DIL_GROUPS) * DIL_HEADS * DIL_HD
DIL_OUT = DIL_HEADS * DIL_HD
IN_SPLIT_SIZES = (GLA_QK, GLA_QK, GLA_V, GLA_V, GLA_RANK,
                  DIL_QK, DIL_QK, DIL_QK, DIL_OUT,
                  D_MODEL, D_MODEL)
IN_WIDTH = sum(IN_SPLIT_SIZES)

kernel_name = 'hybrid_gla_dilated_gated_merge'


def rms_norm(x, gain):
    xf = x.astype(jnp.float32)
    y = xf * lax.rsqrt(jnp.mean(xf * xf, axis=-1, keepdims=True) + EPS)
    return (y * gain.astype(jnp.float32)).astype(x.dtype)


def apply_rope(x, positions):
    half = x.shape[-1] // 2
    inv_freq = ROPE_THETA ** (-jnp.arange(half, dtype=jnp.float32) / half)
    ang = positions.astype(jnp.float32)[..., None] * inv_freq
    cos = jnp.cos(ang)[:, :, None, :]
    sin = jnp.sin(ang)[:, :, None, :]
    xf = x.astype(jnp.float32)
    x1, x2 = xf[..., :half], xf[..., half:]
    return jnp.concatenate([x1 * cos - x2 * sin, x2 * cos + x1 * sin], axis=-1).astype(x.dtype)


def gla_chunked(q, k, v, log_a):
    B, S, H, DK = q.shape
    DV = v.shape[-1]
    C = GLA_CHUNK
    N = S // C

    def chunks(t):
        return t.astype(jnp.float32).reshape(B, N, C, H, t.shape[-1]).transpose(0, 3, 1, 2, 4)

    qc = chunks(q) * (DK ** -0.5)
    kc, vc, gc = chunks(k), chunks(v), chunks(log_a)
    b = jnp.cumsum(gc, axis=3)
    b_last = b[:, :, :, -1:, :]
    q_t = qc * jnp.exp(b)
    k_t = kc * jnp.exp(-b)
    k_end = kc * jnp.exp(b_last - b)
    causal = jnp.tril(jnp.ones((C, C), dtype=bool))
    attn = jnp.where(causal, jnp.einsum('bhnid,bhnjd->bhnij', q_t, k_t), 0.0)
    o_intra = jnp.einsum('bhnij,bhnjv->bhniv', attn, vc)
    kv = jnp.einsum('bhnjd,bhnjv->bhndv', k_end, vc)
    decay = jnp.exp(b_last[:, :, :, 0, :])

    def step(state, inp):
        d, kvn = inp
        return d[..., None] * state + kvn, state

    s0 = jnp.zeros((B, H, DK, DV), jnp.float32)
    _, s_in = lax.scan(step, s0, (jnp.moveaxis(decay, 2, 0), jnp.moveaxis(kv, 2, 0)))
    o_inter = jnp.einsum('bhnid,nbhdv->bhniv', q_t, s_in)
    o = (o_intra + o_inter).transpose(0, 2, 3, 1, 4).reshape(B, S, H, DV)
    return o


def banded_window_attn(q, k, v, win, blk):
    N, L, H, D = q.shape
    nb = L // blk
    nprev = -(-win // blk)
    K = (nprev + 1) * blk
    pad = ((0, 0), (nprev * blk, 0), (0, 0), (0, 0))
    kb = jnp.pad(k, pad).reshape(N, nb + nprev, blk, H, D)
    vb = jnp.pad(v, pad).reshape(N, nb + nprev, blk, H, D)
    kw = jnp.concatenate([kb[:, j:j + nb] for j in range(nprev + 1)], axis=2)
    vw = jnp.concatenate([vb[:, j:j + nb] for j in range(nprev + 1)], axis=2)
    qb = q.reshape(N, nb, blk, H, D)
    s = jnp.einsum('nbqhd,nbkhd->nbhqk', qb, kw).astype(jnp.float32) * (D ** -0.5)
    qi = jnp.arange(blk)
    kj = jnp.arange(K)
    rel = qi[:, None] + nprev * blk - kj[None, :]
    kabs = jnp.arange(nb)[:, None] * blk - nprev * blk + kj[None, :]
    mask = ((rel >= 0) & (rel <= win))[None, :, :] & (kabs >= 0)[:, None, :]
    s = jnp.where(mask[None, :, None], s, -jnp.inf)
    m = jnp.max(s, axis=-1, keepdims=True)
    p = jnp.exp(s - m)
    l = jnp.sum(p, axis=-1, keepdims=True)
    o = jnp.einsum('nbhqk,nbkhd->nbqhd', p, vw.astype(jnp.float32))
    o = o / jnp.swapaxes(l, 2, 3)
    lse = jnp.swapaxes((m + jnp.log(l))[..., 0], 2, 3)
    return o.reshape(N, L, H, D), lse.reshape(N, L, H)


def dilated_group_attn(q, k, v, win, dil):
    B, S, H, D = q.shape
    L = S // dil
    Lp = -(-L // DIL_BLOCK) * DIL_BLOCK

    def to_sub(t):
        t = t.reshape(B, L, dil, H, D).transpose(0, 2, 1, 3, 4).reshape(B * dil, L, H, D)
        return jnp.pad(t, ((0, 0), (0, Lp - L), (0, 0), (0, 0)))

    o, lse = banded_window_attn(to_sub(q), to_sub(k), to_sub(v), win // dil, DIL_BLOCK)
    o = o[:, :L].reshape(B, dil, L, H, D).transpose(0, 2, 1, 3, 4).reshape(B, S, H, D)
    lse = lse[:, :L].reshape(B, dil, L, H).transpose(0, 2, 1, 3).reshape(B, S, H)
    return o, lse


def setup_inputs(seed: int = 0) -> dict:
    key = jax.random.key(seed)
    ks = jax.random.split(key, 12)
    f32 = jnp.float32
    x = jax.random.normal(ks[0], (BATCH, SEQ, D_MODEL), f32)
    positions = jnp.broadcast_to(jnp.arange(SEQ, dtype=jnp.int32), (BATCH, SEQ))
    norm_gain = 1.0 + 0.02 * jax.random.normal(ks[1], (DEPTH, D_MODEL), f32)
    w_in = jax.random.normal(ks[2], (DEPTH, D_MODEL, IN_WIDTH), f32) * D_MODEL ** -0.5
    gla_w_a2 = jax.random.normal(ks[3], (DEPTH, GLA_RANK, GLA_QK), f32) * GLA_RANK ** -0.5
    gla_b_a = 0.1 * jax.random.normal(ks[4], (DEPTH, GLA_QK), f32)
    gla_out_gain = 1.0 + 0.02 * jax.random.normal(ks[5], (DEPTH, GLA_DV), f32)
    dil_q_gain = 1.0 + 0.02 * jax.random.normal(ks[6], (DEPTH, DIL_HD), f32)
    dil_k_gain = 1.0 + 0.02 * jax.random.normal(ks[7], (DEPTH, DIL_HD), f32)
    w_gla_out = jax.random.normal(ks[8], (DEPTH, GLA_V, D_MODEL), f32) * GLA_V ** -0.5
    w_dil_out = jax.random.normal(ks[9], (DEPTH, DIL_OUT, D_MODEL), f32) * DIL_OUT ** -0.5
    w_o = jax.random.normal(ks[10], (DEPTH, D_MODEL, D_MODEL), f32) * D_MODEL ** -0.5
    return {'x': x, 'positions': positions, 'norm_gain': norm_gain, 'w_in': w_in,
            'gla_w_a2': gla_w_a2, 'gla_b_a': gla_b_a, 'gla_out_gain': gla_out_gain,
            'dil_q_gain': dil_q_gain, 'dil_k_gain': dil_k_gain,
            'w_gla_out': w_gla_out, 'w_dil_out': w_dil_out, 'w_o': w_o}


def reference(x, positions, norm_gain, w_in, gla_w_a2, gla_b_a, gla_out_gain,
              dil_q_gain, dil_k_gain, w_gla_out, w_dil_out, w_o):
    B, S, _ = x.shape
    offsets = np.cumsum(IN_SPLIT_SIZES)[:-1].tolist()
    n_groups = len(DIL_GROUPS)
    for layer in range(DEPTH):
        h = rms_norm(x, norm_gain[layer])
        proj = h @ w_in[layer]
        (q_a, k_a, v_a, r_a, a_lr, q_d, k_d, v_d, z_d, g_a, g_d) = jnp.split(proj, offsets, axis=-1)

        log_a = jax.nn.log_sigmoid((a_lr @ gla_w_a2[layer] + gla_b_a[layer]).astype(jnp.float32)) / GLA_TAU
        o_a = gla_chunked(q_a.reshape(B, S, GLA_HEADS, GLA_DK),
                          k_a.reshape(B, S, GLA_HEADS, GLA_DK),
                          v_a.reshape(B, S, GLA_HEADS, GLA_DV),
                          log_a.reshape(B, S, GLA_HEADS, GLA_DK))
        o_a = rms_norm(o_a, gla_out_gain[layer]).reshape(B, S, GLA_V)
        o_a = (o_a * jax.nn.silu(r_a.astype(jnp.float32))).astype(x.dtype)
        y_a = o_a @ w_gla_out[layer]

        n_heads_d = n_groups * DIL_HEADS
        qd = apply_rope(rms_norm(q_d.reshape(B, S, n_heads_d, DIL_HD), dil_q_gain[layer]), positions)
        kd = apply_rope(rms_norm(k_d.reshape(B, S, n_heads_d, DIL_HD), dil_k_gain[layer]), positions)
        vd = v_d.reshape(B, S, n_heads_d, DIL_HD)
        outs, lses = [], []
        for g, (win, dil) in enumerate(DIL_GROUPS):
            hs = slice(g * DIL_HEADS, (g + 1) * DIL_HEADS)
            o_g, lse_g = dilated_group_attn(qd[:, :, hs], kd[:, :, hs], vd[:, :, hs], win, dil)
            outs.append(o_g)
            lses.append(lse_g)
        wts = jax.nn.softmax(jnp.stack(lses, axis=0), axis=0)
        o_d = jnp.sum(wts[..., None] * jnp.stack(outs, axis=0), axis=0).reshape(B, S, DIL_OUT)
        o_d = (o_d * jax.nn.silu(z_d.astype(jnp.float32))).astype(x.dtype)
        y_d = o_d @ w_dil_out[layer]

        y = jax.nn.sigmoid(g_a) * y_a + jax.nn.sigmoid(g_d) * y_d
        x = x + (y @ w_o[layer]).astype(x.dtype)
    return x
```

```python
import math
import numpy as np
import concourse.bass as bass
import concourse.mybir as mybir
from concourse.bass_utils import run_bass_kernel_spmd

F32 = mybir.dt.float32
BF16 = mybir.dt.bfloat16
I32 = mybir.dt.int32
AF = mybir.ActivationFunctionType
ALU = mybir.AluOpType
AX = mybir.AxisListType

D = 1024
SEQ = 4096
NB = 4
TOWN = 2048
TALL = 4096
INW = 10256
EPS = 1e-6
OFF_QA, OFF_KA, OFF_VA, OFF_RA, OFF_ALR = 0, 512, 1024, 2048, 3072
OFF_QD, OFF_KD, OFF_VD, OFF_ZD, OFF_GA, OFF_GD = 3088, 4624, 6160, 7696, 8208, 9232
NDSEM = 24
TWO_PI = 2.0 * math.pi
CW1 = 6.28125
CW2 = TWO_PI - CW1
MAGIC = 12582912.0


class Rec:
    def __init__(self):
        self.ops = []
        self.lastw = {}
        self.readers = {}
        self.bar = set()

    def barrier(self):
        last = {}
        dmas = []
        for i, o in enumerate(self.ops):
            if o["dma"]:
                dmas.append(i)
            elif o["fn"] is not None:
                last[o["eng"]] = i
        self.bar = set(last.values()) | set(dmas[-NDSEM:])

    def add(self, eng, fn, r=(), w=(), dma=False):
        i = len(self.ops)
        deps = set(self.bar)
        for k in r:
            j = self.lastw.get(k)
            if j is not None:
                deps.add(j)
        for k in w:
            j = self.lastw.get(k)
            if j is not None:
                deps.add(j)
            for j in self.readers.get(k, ()):
                deps.add(j)
        wset = set(w)
        for k in w:
            self.lastw[k] = i
            self.readers[k] = []
        for k in r:
            if k not in wset:
                self.readers.setdefault(k, []).append(i)
        self.ops.append(dict(eng=eng, fn=fn, deps=deps, dma=dma, sig=False))
        return i

    def emit(self, nc):
        ops = self.ops
        sems = {e: nc.alloc_semaphore("sem_" + e) for e in ("pe", "act", "dve", "pool")}
        dsems = [nc.alloc_semaphore("dsem%d" % i) for i in range(NDSEM)]
        for o in ops:
            for j in o["deps"]:
                pj = ops[j]
                if pj["dma"]:
                    continue
                if pj["eng"] == "pe" and o["eng"] == "pe" and not o["dma"]:
                    continue
                pj["sig"] = True
        cnt = {e: 0 for e in sems}
        nd = 0
        for o in ops:
            if o["dma"]:
                o["dsem"] = nd % NDSEM
                o["dval"] = 16 * (nd // NDSEM + 1)
                nd += 1
            elif o["sig"]:
                cnt[o["eng"]] += 1
                o["sval"] = cnt[o["eng"]]

        def make(engname):
            def body(e):
                waited = {}
                for o in ops:
                    if o["eng"] != engname:
                        continue
                    need = {}
                    for j in o["deps"]:
                        pj = ops[j]
                        if pj["dma"]:
                            key = ("d", pj["dsem"])
                            val = pj["dval"]
                        else:
                            if pj["eng"] == "pe" and engname == "pe" and not o["dma"]:
                                continue
                            key = ("e", pj["eng"])
                            val = pj["sval"]
                        if need.get(key, 0) < val:
                            need[key] = val
                    if o["dma"] and o["dval"] > 16:
                        key = ("d", o["dsem"])
                        if need.get(key, 0) < o["dval"] - 16:
                            need[key] = o["dval"] - 16
                    for key, val in need.items():
                        if waited.get(key, 0) >= val:
                            continue
                        waited[key] = val
                        sem = dsems[key[1]] if key[0] == "d" else sems[key[1]]
                        e.wait_ge(sem, val)
                    ins = o["fn"](e) if o["fn"] is not None else None
                    if o["dma"]:
                        ins.then_inc(dsems[o["dsem"]], 16)
                    elif o["sig"]:
                        ins.then_inc(sems[o["eng"]], 1)

            return body

        with nc.Block() as block:
            block.tensor(make("pe"))
            block.scalar(make("act"))
            block.vector(make("dve"))
            block.gpsimd(make("pool"))
            block.sync(make("sp"))


class Arena:
    def __init__(self, nc, base, top):
        self.nc, self.base, self.top, self.cur = nc, base, top, base
        self.n = 0

    def alloc(self, name, shape, dtype):
        nbytes = int(np.prod(shape[1:])) * (4 if dtype in (F32, I32) else 2)
        nbytes = (nbytes + 31) // 32 * 32
        off = self.cur
        self.cur += nbytes
        assert self.cur <= self.top, ("SBUF arena overflow", name, self.cur, self.top)
        self.n += 1
        return self.nc.alloc_sbuf_tensor_at("%s_%d" % (name, self.n), list(shape), dtype, offset=off)

    def mark(self):
        return self.cur

    def reset(self, m):
        self.cur = m


class PsumPool:
    def __init__(self, ps):
        self.ps = ps
        self.free = list(range(8))

    def alloc(self, n=1):
        if n == 1:
            assert self.free, "out of PSUM banks"
            return self.free.pop(0)
        for b in list(self.free):
            if all((b + i) in self.free for i in range(n)):
                for i in range(n):
                    self.free.remove(b + i)
                return b
        raise AssertionError("out of adjacent PSUM banks")

    def release(self, b, n=1):
        for i in range(n):
            self.free.append(b + i)


def build_program(debug=None):
    nc = bass.Bass("TRN2", target_bir_lowering=False)
    R = Rec()

    def dram(name, shape, dt=F32, kind="ExternalInput"):
        return nc.dram_tensor(name, list(shape), dt, kind=kind).ap()

    xT = dram("xT", [D, TALL])
    pos = dram("pos", [1, TALL], I32)
    flag_d = dram("flag", [128, 1])
    w_in = dram("w_in", [D, INW])
    w_a2_d = dram("w_a2", [16, 512])
    b_a_d = dram("b_a", [1, 512])
    b_aT_d = dram("b_aT", [128, 4])
    gx_d = dram("gx", [128, 8])
    go_d = dram("go", [128, 2])
    gqk_d = dram("gqk", [128, 2])
    gqk_row_d = dram("gqk_row", [1, 256])
    w_go_d = dram("w_gla_out", [1024, 1024])
    w_do_d = dram("w_dil_out", [512, 1024])
    w_o_d = dram("w_o", [1024, 1024])
    consts_d = dram("consts", [128, 648])
    outT = dram("outT", [D, TOWN], F32, kind="ExternalOutput")
    dbg_out = {}

    A = Arena(nc, 16512, 229344)
    ps_t = nc.alloc_psum_tensor("ps", [128, 4096], F32)
    PS = PsumPool(ps_t)

    def bank(b, n=1):
        return ps_t[:, b * 512:(b + n) * 512]

    def bankbf(b):
        return ps_t[:, b * 512:(b + 1) * 512].bitcast(BF16)

    def pk(b, n=1):
        return [("ps", b + i) for i in range(n)]

    hTo = A.alloc("hTo", [128, 8, TOWN], BF16)
    HTP_OFF = A.mark()
    hTp = A.alloc("hTp", [128, 8, TOWN], BF16)
    cst32 = A.alloc("cst32", [128, 648], F32)
    cstb = A.alloc("cstb", [128, 512], BF16)
    ones_b = A.alloc("ones_b", [128, 128], BF16)
    maskP = A.alloc("maskP", [128, 128], BF16)
    gx = A.alloc("gx", [128, 8], F32)
    go = A.alloc("go", [128, 2], F32)
    gqk = A.alloc("gqk", [128, 2], F32)
    flag = A.alloc("flag", [128, 1], F32)
    negC = A.alloc("negC", [128, 2], F32)
    ones32 = A.alloc("ones32", [1, 128], F32)
    wa2s = A.alloc("wa2s", [16, 512], F32)
    gqkrow = A.alloc("gqkrow", [1, 256], F32)
    mx = A.alloc("mx", [1, 4], F32)
    epsc = A.alloc("epsc", [128, 1], F32)
    posi = A.alloc("posi", [128, 512], I32)
    scanm = A.alloc("scanm", [128, 512], F32)
    onec = A.alloc("onec", [128, 1], F32)
    M_PP = A.alloc("M_PP", [128, 512], BF16)
    M_PL = A.alloc("M_PL", [128, 512], BF16)
    M_LL = A.alloc("M_LL", [128, 512], BF16)
    odT = A.alloc("odT", [128, 4, TOWN], BF16)
    ARENA_B = A.mark()
    oaT = A.alloc("oaT", [128, 8, TOWN], BF16)
    ARENA_A = A.mark()
    A.reset(ARENA_B)
    cosT = A.alloc("cosT", [128, TALL], BF16)
    sinT = A.alloc("sinT", [128, TALL], BF16)
    ARENA_B2 = A.mark()

    ident32 = cst32[:, 0:128]
    triU32 = cst32[:, 128:256]
    triL32 = cst32[:, 256:384]
    triN32 = cst32[:, 512:640]
    invf = cst32[:, 640:641]
    ident_b = cstb[:, 0:128]
    triU_b = cstb[:, 128:256]
    triL_b = cstb[:, 256:384]
    RT_b = cstb[:, 384:512]

    def hT(k, t0, n, step=1):
        last = t0 + (n - 1) * step
        if t0 >= TOWN:
            a = t0 - TOWN
            return hTo[:, k, a:a + (n - 1) * step + 1:step] if step > 1 else hTo[:, k, a:a + n]
        assert last < TOWN
        return hTp[:, k, t0:t0 + (n - 1) * step + 1:step] if step > 1 else hTp[:, k, t0:t0 + n]

    def hkey(t0):
        return ("hT", t0 // 512)

    def dma(out, in_, r, w, eng="sp"):
        return R.add(eng, lambda e: e.dma_start(out=out, in_=in_), r, w, dma=True)

    def mm_group(out, pairs, r, w):
        n = len(pairs)

        def fn(e):
            ins = None
            for i, (l, rr) in enumerate(pairs):
                ins = e.matmul(out, l, rr, start=(i == 0), stop=(i == n - 1))
            return ins

        return R.add("pe", fn, r, w)

    def act(out, in_, func, r, w, bias=None, scale=None):
        kw = {}
        if bias is not None:
            kw["bias"] = bias
        if scale is not None:
            kw["scale"] = scale
        return R.add("act", lambda e: e.activation(out, in_, func, **kw), r, w)

    def tt(eng, out, in0, in1, op, r, w):
        return R.add(eng, lambda e: e.tensor_tensor(out, in0, in1, op), r, w)

    def ts(eng, out, in0, s1, op0, r, w, s2=None, op1=None):
        if op1 is None:
            return R.add(eng, lambda e: e.tensor_scalar(out, in0, s1, None, op0), r, w)
        return R.add(eng, lambda e: e.tensor_scalar(out, in0, s1, s2, op0, op1), r, w)

    def stt(out, in0, scalar, in1, op0, op1, r, w):
        return R.add("dve", lambda e: e.scalar_tensor_tensor(out, in0, scalar, in1, op0, op1), r, w)

    def cp(eng, out, in_, r, w):
        if eng == "act":
            return R.add(eng, lambda e: e.activation(out, in_, AF.Copy), r, w)
        return R.add(eng, lambda e: e.tensor_copy(out, in_), r, w)

    dma(cst32[:, :], consts_d[:, :], [], ["cst32"])
    dma(gx[:, :], gx_d[:, :], [], ["gx"])
    dma(go[:, :], go_d[:, :], [], ["go"])
    dma(gqk[:, :], gqk_d[:, :], [], ["gqk"])
    dma(flag[:, :], flag_d[:, :], [], ["flag"])
    dma(wa2s[:, :], w_a2_d[:, :], [], ["wa2s"])
    dma(gqkrow[:, :], gqk_row_d[:, :], [], ["gqkrow"])
    cp("dve", cstb[:, :], cst32[:, 0:512], ["cst32"], ["cstb"])
    R.add("dve", lambda e: e.memset(ones_b[:, :], 1.0), [], ["ones_b"])
    R.add("dve", lambda e: e.memset(ones32[:, :], 1.0), [], ["ones32"])
    R.add("dve", lambda e: e.memset(epsc[:, :], EPS), [], ["epsc"])
    R.add("dve", lambda e: e.memset(scanm[:, :], 1.0), [], ["scanm"])
    for j in range(4):
        R.add("dve", lambda e, j=j: e.memset(scanm[:, j * 128:j * 128 + 1], 0.0), ["scanm"], ["scanm"])
    R.add("dve", lambda e: e.memset(onec[:, :], 1.0), [], ["onec"])
    ts("dve", maskP[:, :], triL32, flag[:, 0:1], ALU.mult, ["cst32", "flag"], ["maskP"])
    for (M, first, second) in ((M_PP, "P", "P"), (M_PL, "P", "L"), (M_LL, "L", "L")):
        for u, kind in enumerate((first, second)):
            if kind == "P":
                ts("dve", M[:, u * 256:u * 256 + 128], triL32, flag[:, 0:1], ALU.mult, ["cst32", "flag"], ["masks"])
            else:
                cp("dve", M[:, u * 256:u * 256 + 128], triL32, ["cst32"], ["masks"])
            cp("dve", M[:, u * 256 + 128:u * 256 + 256], triU32, ["cst32"], ["masks"])
    R.add("dve", lambda e: e.tensor_reduce(mx[:, 0:1], gqkrow[:, 0:128], AX.X, ALU.max,
                                           apply_absolute_value=True), ["gqkrow"], ["mx0"])
    R.add("dve", lambda e: e.tensor_reduce(mx[:, 1:2], gqkrow[:, 128:256], AX.X, ALU.max,
                                           apply_absolute_value=True), ["gqkrow"], ["mx1"])
    ts("dve", mx[:, 2:3], mx[:, 0:1], mx[:, 1:2], ALU.mult, ["mx0", "mx1"], ["mx2"],
       s2=-math.sqrt(128.0), op1=ALU.mult)
    cp("dve", mx[:, 3:4], mx[:, 2:3], ["mx2"], ["mx3"])
    b0 = PS.alloc()
    mm_group(bank(b0)[:, 0:2], [(ones32[0:1, 0:128], mx[0:1, 2:4])], ["ones32", "mx2", "mx3"], pk(b0))
    cp("dve", negC[:, :], bank(b0)[:, 0:2], pk(b0), ["negC"])
    PS.release(b0)


    out_keys = []

    def dump(nm, t, keys):
        dd = dram("dbg_" + nm, list(t.shape), t.dtype, kind="ExternalOutput")
        dbg_out[nm] = dd
        dma(dd, t.ap(), keys, [("dbg", nm)])

    def finish():
        R.add("sp", None, [("dbg", nm) for nm in dbg_out] + out_keys, [])
        R.emit(nc)
        return nc, dbg_out

    def recip(t_ap, key):
        R.add("dve", lambda e: e.reciprocal(t_ap, t_ap), [key], [key])

    def mk(name, shape, dt, n=2):
        return [A.alloc(name, shape, dt) for _ in range(n)]

    class Loader:
        def __init__(self, stg_):
            self.stg, self.q, self.nd, self.ncast = stg_, [], 0, 0

        def enqueue(self, dst, src, dkey, extra=()):
            self.q.append((dst, src, dkey, tuple(extra)))

        def tick(self, n=1):
            for _ in range(n):
                if self.nd < len(self.q) and self.nd - self.ncast < 2:
                    dst, src, dkey, extra = self.q[self.nd]
                    sl = self.nd % 2
                    K_, n_ = dst.shape[1], dst.shape[2]
                    dma(self.stg[sl][:, 0:K_, 0:n_], src, [], [("stg", sl)])
                    self.nd += 1
                elif self.ncast < self.nd:
                    dst, src, dkey, extra = self.q[self.ncast]
                    sl = self.ncast % 2
                    K_, n_ = dst.shape[1], dst.shape[2]
                    cp("act", dst, self.stg[sl][:, 0:K_, 0:n_], [("stg", sl)], [dkey] + list(extra))
                    self.ncast += 1

        def flush(self):
            while self.ncast < len(self.q):
                self.tick()

    xTv = xT.rearrange("(k p) t -> p k t", p=128)
    w_in_v = w_in.rearrange("(k p) c -> p k c", p=128)
    w_go_v = w_go_d.rearrange("(k p) c -> p k c", p=128)
    w_do_v = w_do_d.rearrange("(k p) c -> p k c", p=128)
    w_o_v = w_o_d.rearrange("(k p) c -> p k c", p=128)

    A.reset(ARENA_B2)
    stgB = mk("stg", [128, 8, 128], F32)
    wB0_p = A.alloc("wB", [128, 8, 384], BF16)
    wz_p = A.alloc("wz", [128, 8, 128], BF16)
    B_PRE_END = A.mark()
    WL0 = Loader(stgB)
    WL0.enqueue(wz_p.ap(), w_in_v[:, :, OFF_ZD:OFF_ZD + 128], "wz")
    for (d0, s0, nm) in ((0, OFF_QD, "q"), (128, OFF_KD, "k"), (256, OFF_VD, "v")):
        WL0.enqueue(wB0_p[:, :, d0:d0 + 128], w_in_v[:, :, s0:s0 + 128], ("wB", 0, nm))
    ang_r = mk("ang", [128, 512], F32)
    kk_r = mk("kk", [128, 512], F32)
    xs_r = mk("xs", [128, 8, 512], F32, 3)
    sq_r = mk("sq", [128, 8, 512], BF16)
    rb_r = mk("rb", [128, 512], F32)

    def tab1(qq):
        ang, kk = ang_r[qq % 2], kk_r[qq % 2]
        ka, kkk = ("ang", qq % 2), ("kk", qq % 2)
        cs_ = slice(qq * 512, (qq + 1) * 512)
        dma(posi[:, :], pos[:, cs_].partition_broadcast(128)[:, 0, :], [], ["posi"])
        ts("dve", ang[:, :], posi[:, :], invf, ALU.mult, ["posi"], [ka])
        ts("dve", kk[:, :], ang[:, :], 1.0 / TWO_PI, ALU.mult, [ka], [kkk], s2=MAGIC, op1=ALU.add)
        ts("dve", kk[:, :], kk[:, :], MAGIC, ALU.subtract, [kkk], [kkk])
        stt(ang[:, :], kk[:, :], -CW1, ang[:, :], ALU.mult, ALU.add, [kkk, ka], [ka])
        stt(ang[:, :], kk[:, :], -CW2, ang[:, :], ALU.mult, ALU.add, [kkk, ka], [ka])
        ts("dve", ang[:, :], ang[:, :], math.pi, ALU.min, [ka], [ka], s2=-math.pi, op1=ALU.max)

    def tab2(qq):
        ang, kk = ang_r[qq % 2], kk_r[qq % 2]
        ka, kkk = ("ang", qq % 2), ("kk", qq % 2)
        cs_ = slice(qq * 512, (qq + 1) * 512)
        act(sinT[:, cs_], ang[:, :], AF.Sin, [ka], [("sinT", qq)])
        act(kk[:, :], ang[:, :], AF.Sin, [ka], [kkk], scale=0.5)

    def tab3(qq):
        kk = kk_r[qq % 2]
        kkk = ("kk", qq % 2)
        cs_ = slice(qq * 512, (qq + 1) * 512)
        tt("dve", kk[:, :], kk[:, :], kk[:, :], ALU.mult, [kkk], [kkk])
        ts("dve", cosT[:, cs_], kk[:, :], -2.0, ALU.mult, [kkk], [("cosT", qq)], s2=1.0, op1=ALU.add)

    def p0_x1(G):
        s, s3 = G % 2, G % 3
        xs, sq = xs_r[s3], sq_r[s]
        if G >= 2:
            dma(xs.ap(), xTv[:, :, G * 512:(G + 1) * 512], [("xs", (G - 2) % 3)], [("xs", s3)])
        act(sq.ap(), xs.ap(), AF.Square, [("xs", s3)], [("sq", s)])

    def p0_x2(G):
        s = G % 2
        sq, rb = sq_r[s], rb_r[s]
        b = PS.alloc()
        mm_group(bank(b), [(ones_b.ap(), sq[:, k, :]) for k in range(8)], [("sq", s)], pk(b))
        act(rb.ap(), bank(b), AF.Ln, pk(b), [("rb", s)], bias=epsc[:, 0:1], scale=1.0 / D)
        PS.release(b)
        act(rb.ap(), rb.ap(), AF.Exp, [("rb", s)], [("rb", s)], scale=-0.5)

    xg_r = mk("xg", [128, 512], F32)

    def p0_y(G):
        s, s3 = G % 2, G % 3
        xs, rb = xs_r[s3], rb_r[s]
        for k in range(6):
            stt(hT(k, G * 512, 512), xs[:, k, :], gx[:, k:k + 1], rb.ap(), ALU.mult, ALU.mult,
                [("xs", s3), ("rb", s)], [hkey(G * 512)])
        for k in (6, 7):
            xg = xg_r[k % 2]
            act(xg.ap(), xs[:, k, :], AF.Copy, [("xs", s3)], [("xg", k % 2)], scale=gx[:, k:k + 1])
            tt("pool", hT(k, G * 512, 512), xg.ap(), rb.ap(), ALU.mult, [("xg", k % 2), ("rb", s)], [hkey(G * 512)])

    dma(xs_r[0].ap(), xTv[:, :, 0:512], [], [("xs", 0)])
    dma(xs_r[1].ap(), xTv[:, :, 512:1024], [("xs", 0)], [("xs", 1)])
    R.barrier()
    tab1(0)
    p0_x1(0)
    p0_x2(0)
    tab1(1)
    for G in range(8):
        if G + 1 < 8:
            p0_x1(G + 1)
        tab2(G)
        p0_y(G)
        if G + 1 < 8:
            p0_x2(G + 1)
        if G == 1:
            WL0.tick(2)
        if G == 6:
            WL0.tick(4)
        tab3(G)
        if G + 2 < 8:
            tab1(G + 2)
    WL0.flush()
    if debug == "p0":
        dump("sinT", sinT, [("sinT", q) for q in range(8)])
        dump("cosT", cosT, [("cosT", q) for q in range(8)])
        dump("hTo", hTo, [hkey(t) for t in range(0, TALL, 512)])
        dump("hTp", hTp, [hkey(t) for t in range(0, TALL, 512)])
        return finish()

    R.barrier()
    A.reset(B_PRE_END)
    stg = stgB
    WL = Loader(stg)
    load_block = WL.enqueue

    wB0 = wB0_p
    wz = wz_p
    wB = [wB0, A.alloc("wB", [128, 8, 384], BF16)]
    qr = A.alloc("qr", [128, TOWN], BF16)
    kr = A.alloc("kr", [128, TALL], BF16)
    vt = A.alloc("vt", [128, 32 * 128], BF16)
    Oacc = A.alloc("Oacc", [128, TOWN], F32)
    Lacc = A.alloc("Lacc", [128, TOWN], F32)
    silz_r = mk("silz", [128, TOWN], BF16)
    nsq = mk("nsq", [128, 512], BF16, 4)
    nrs = mk("nrs", [128, 512], F32)
    kn = mk("kn", [128, 512], BF16, 3)
    t1 = mk("t1", [128, 512], F32)
    t2 = mk("t2", [128, 512], F32)
    pT = mk("pT", [128, 512], BF16, 6)

    def load_B(hd, g, st):
        n = g * 4 + hd
        for (d0, s0, nm) in ((0, OFF_QD + n * 128, "q"), (128, OFF_KD + n * 128, "k"), (256, OFF_VD + n * 128, "v")):
            load_block(wB[st][:, :, d0:d0 + 128], w_in_v[:, :, s0:s0 + 128], ("wB", st, nm))

    allh = [hkey(t) for t in range(0, TALL, 512)]
    wcnt = [0]
    pcnt = [0]

    def combine(hd):
        silz = silz_r[hd % 2]
        for G in range(4):
            gs = slice(G * 512, (G + 1) * 512)
            i2 = G % 2
            act(nrs[i2].ap(), Lacc[:, gs], AF.Ln, ["Lacc", ("nrs", i2)], [("nrs", i2)])
            act(nrs[i2].ap(), nrs[i2].ap(), AF.Exp, [("nrs", i2)], [("nrs", i2)], scale=-1.0)
            tt("pool", t1[i2].ap(), Oacc[:, gs], nrs[i2].ap(), ALU.mult, ["Oacc", ("nrs", i2)], [("t1", i2)])
            tt("dve", odT[:, hd, gs], t1[i2].ap(), silz[:, gs], ALU.mult, [("t1", i2), ("silz", hd % 2, G)], [("odT", hd, G)])

    for hd in range(4):
        silz = silz_r[hd % 2]
        WL.flush()
        for G in range(4):
            b = PS.alloc()
            mm_group(bank(b), [(wz[:, k, :], hT(k, TOWN + G * 512, 512)) for k in range(8)], ["wz", hkey(TOWN + G * 512)], pk(b))
            act(silz[:, G * 512:(G + 1) * 512], bank(b), AF.Silu, pk(b), [("silz", hd % 2, G)])
            PS.release(b)
        for g in range(3):
            st = wcnt[0] % 2
            wcnt[0] += 1
            W = wB[st]
            dil = 4 ** g
            Pn = 128 * dil
            nb = 16 // dil
            nxt = (hd, g + 1) if g < 2 else ((hd + 1, 0) if hd < 3 else None)
            WL.flush()
            if nxt is not None:
                load_B(nxt[0], nxt[1], wcnt[0] % 2)
            if g == 2 and hd < 3:
                load_block(wz.ap(), w_in_v[:, :, OFF_ZD + (hd + 1) * 128:OFF_ZD + (hd + 2) * 128], "wz")
            kp = [(TOWN - Pn, 128)] if Pn == 128 else [(TOWN - Pn + i * 512, 512) for i in range(Pn // 512)]
            kp += [(TOWN + i * 512, 512) for i in range(4)]
            pieces = [dict(t0=t0, n=n, dst=kr[:, t0:t0 + n], dkey=("kr", t0 // 128), c0=128, wk=("wB", st, "k"),
                           gcol=gqk[:, 1:2]) for (t0, n) in kp]
            pieces += [dict(t0=TOWN + i * 512, n=512, dst=qr[:, i * 512:(i + 1) * 512], dkey=("qr", i), c0=0,
                            wk=("wB", st, "q"), gcol=gqk[:, 0:1]) for i in range(4)]
            krkeys = [("kr", t // 128) for (t, n) in kp]
            qrkeys = [("qr", i) for i in range(4)]

            def st_a(p):
                p["id"] = pcnt[0]
                pcnt[0] += 1
                i3, n, t0 = p["id"] % 4, p["n"], p["t0"]
                p["bp"] = PS.alloc()
                mm_group(bank(p["bp"])[:, 0:n], [(W[:, k, p["c0"]:p["c0"] + 128], hT(k, t0, n)) for k in range(8)],
                         [p["wk"], hkey(t0)], pk(p["bp"]))
                act(nsq[i3][:, 0:n], bank(p["bp"])[:, 0:n], AF.Square, pk(p["bp"]), [("nsq", i3)])

            def st_b(p):
                i2, i3, ik, n, t0, bp = p["id"] % 2, p["id"] % 4, p["id"] % 3, p["n"], p["t0"], p["bp"]
                bs = PS.alloc()
                mm_group(bank(bs)[:, 0:n], [(ones_b.ap(), nsq[i3][:, 0:n])], [("nsq", i3)], pk(bs))
                act(nrs[i2][:, 0:n], bank(bs)[:, 0:n], AF.Ln, pk(bs), [("nrs", i2)], bias=epsc[:, 0:1], scale=1.0 / 128.0)
                PS.release(bs)
                act(nrs[i2][:, 0:n], nrs[i2][:, 0:n], AF.Exp, [("nrs", i2)], [("nrs", i2)], scale=-0.5)
                stt(kn[ik][:, 0:n], bank(bp)[:, 0:n], p["gcol"], nrs[i2][:, 0:n], ALU.mult, ALU.mult,
                    pk(bp) + [("nrs", i2)], [("kn", ik)])
                PS.release(bp)

            def st_c(p):
                i2, ik, n, t0 = p["id"] % 2, p["id"] % 3, p["n"], p["t0"]
                br = PS.alloc()
                mm_group(bank(br)[:, 0:n], [(RT_b, kn[ik][:, 0:n])], [("kn", ik)], pk(br))
                tt("pool", t1[i2][:, 0:n], kn[ik][:, 0:n], cosT[:, t0:t0 + n], ALU.mult, [("kn", ik)], [("t1", i2)])
                tt("dve", t2[i2][:, 0:n], bank(br)[:, 0:n], sinT[:, t0:t0 + n], ALU.mult, pk(br), [("t2", i2)])
                PS.release(br)
                tt("dve", p["dst"], t1[i2][:, 0:n], t2[i2][:, 0:n], ALU.add,
                   [("t1", i2), ("t2", i2)], [p["dkey"]])

            blocks = [(r, b) for r in range(dil) for b in range(-1, nb)]

            def v_chunk(i0):
                chunk = blocks[i0:i0 + 4]
                bv = PS.alloc()
                for qi, (r, b) in enumerate(chunk):
                    tau0 = TOWN + 128 * b * dil + r
                    mm_group(bank(bv)[:, qi * 128:(qi + 1) * 128],
                             [(hT(k, tau0, 128, dil), W[:, k, 256:384]) for k in range(8)],
                             [("wB", st, "v")] + allh, pk(bv))
                cp("act", vt[:, i0 * 128:(i0 + len(chunk)) * 128],
                   bank(bv)[:, 0:len(chunk) * 128], pk(bv), [("vt", i0 // 4)])
                PS.release(bv)

            v_list = list(range(0, len(blocks), 4))
            v_pos = [0]

            def v_some(n, keep=0):
                for _ in range(n):
                    if v_pos[0] < len(v_list) - keep:
                        v_chunk(v_list[v_pos[0]])
                        v_pos[0] += 1

            if g == 1:
                qblocks = [(r, b) for b in range(nb) for r in range(dil)]
            else:
                qblocks = [(r, b) for r in range(dil) for b in range(nb)]
            npair = len(qblocks) // 2
            sc = float(128.0 ** -0.5)
            state = {}

            def scores(i):
                bsT = PS.alloc()
                for u in range(2):
                    r, b = qblocks[2 * i + u]
                    q0 = 128 * b * dil + r
                    qs = qr[:, q0:q0 + 127 * dil + 1:dil]
                    tp = TOWN + 128 * (b - 1) * dil + r
                    tc = TOWN + 128 * b * dil + r
                    qk = [("qr", t // 512) for t in range(q0 - q0 % 512, q0 + 127 * dil + 1, 512)]
                    kk1 = [("kr", t0_ // 128) for (t0_, n_) in kp if t0_ < tp + 127 * dil + 1 and t0_ + n_ > tp]
                    kk2 = [("kr", t0_ // 128) for (t0_, n_) in kp if t0_ < tc + 127 * dil + 1 and t0_ + n_ > tc]
                    mm_group(bank(bsT)[:, u * 256:u * 256 + 128], [(kr[:, tp:tp + 127 * dil + 1:dil], qs)],
                             kk1 + qk, pk(bsT))
                    mm_group(bank(bsT)[:, u * 256 + 128:u * 256 + 256], [(kr[:, tc:tc + 127 * dil + 1:dil], qs)],
                             kk2 + qk, pk(bsT))
                p = pT[i % 6]
                act(p.ap(), bank(bsT), AF.Exp, pk(bsT), [("pT", i % 6)], bias=negC[:, 0:1], scale=sc)
                PS.release(bsT)
                f0 = qblocks[2 * i][1] == 0
                f1 = qblocks[2 * i + 1][1] == 0
                assert not (f1 and not f0)
                M = M_PP if (f0 and f1) else (M_PL if f0 else M_LL)
                tt("dve" if i % 4 else "pool", p.ap(), p.ap(), M.ap(), ALU.mult, [("pT", i % 6)], [("pT", i % 6)])

            def pv(i):
                p = pT[i % 6]
                for u in range(2):
                    j = 2 * i + u
                    r, b = qblocks[j]
                    qi = j % 4
                    if qi == 0:
                        state["bO"] = PS.alloc()
                        state["bL"] = PS.alloc()
                    bO, bL = state["bO"], state["bL"]
                    vp = (r * (nb + 1) + b) * 128
                    vc = (r * (nb + 1) + b + 1) * 128
                    mm_group(bank(bO)[:, qi * 128:(qi + 1) * 128],
                             [(vt[:, vp:vp + 128], p[:, u * 256:u * 256 + 128]),
                              (vt[:, vc:vc + 128], p[:, u * 256 + 128:u * 256 + 256])],
                             [("vt", (vp // 128) // 4), ("vt", (vc // 128) // 4), ("pT", i % 6)], pk(bO))
                    mm_group(bank(bL)[:, qi * 128:(qi + 1) * 128],
                             [(ones_b.ap(), p[:, u * 256:u * 256 + 128]), (ones_b.ap(), p[:, u * 256 + 128:u * 256 + 256])],
                             [("pT", i % 6)], pk(bL))
                    if qi == 3:
                        r0, b0_ = qblocks[j - 3]
                        if g == 0:
                            ov = Oacc[:, b0_ * 128:b0_ * 128 + 512]
                            lv = Lacc[:, b0_ * 128:b0_ * 128 + 512]
                            so, sl = bank(bO), bank(bL)
                        elif g == 1:
                            ov = Oacc[:, b0_ * 512:(b0_ + 1) * 512].rearrange("p (i r) -> p r i", r=4)
                            lv = Lacc[:, b0_ * 512:(b0_ + 1) * 512].rearrange("p (i r) -> p r i", r=4)
                            so = bank(bO).rearrange("p (r i) -> p r i", r=4)
                            sl = bank(bL).rearrange("p (r i) -> p r i", r=4)
                        else:
                            ov = Oacc.ap().rearrange("p (i r) -> p r i", r=16)[:, r0:r0 + 4, :]
                            lv = Lacc.ap().rearrange("p (i r) -> p r i", r=16)[:, r0:r0 + 4, :]
                            so = bank(bO).rearrange("p (r i) -> p r i", r=4)
                            sl = bank(bL).rearrange("p (r i) -> p r i", r=4)
                        if g == 0:
                            cp("dve", ov, so, pk(bO), ["Oacc"])
                            cp("act", lv, sl, pk(bL), ["Lacc"])
                        else:
                            tt("dve", ov, so, ov, ALU.add, pk(bO) + ["Oacc"], ["Oacc"])
                            tt("dve", lv, sl, lv, ALU.add, pk(bL) + ["Lacc"], ["Lacc"])
                        PS.release(bO)
                        PS.release(bL)

            LOOK = 4
            early = [0]
            NP = len(pieces)
            for i in range(NP + 4):
                WL.tick(2)
                if i < NP:
                    st_a(pieces[i])
                else:
                    v_some(1, keep=1)
                    if g < 2:
                        while early[0] < min(LOOK, npair) and early[0] < 2 * (i - NP):
                            scores(early[0])
                            early[0] += 1
                if 0 <= i - 2 < NP:
                    st_b(pieces[i - 2])
                if 0 <= i - 4 < NP:
                    st_c(pieces[i - 4])
            v_some(len(v_list), keep=2)
            for i in range(early[0], min(LOOK, npair)):
                scores(i)
            v_some(2)
            for i in range(npair):
                if i + LOOK < npair:
                    scores(i + LOOK)
                pv(i)
                if i == 0 and g == 0 and hd >= 1:
                    combine(hd - 1)
    combine(3)
    if debug == "pB":
        dump("odT", odT, [("odT", c, g) for c in range(4) for g in range(4)])
        return finish()

    R.barrier()
    A.reset(ARENA_A)
    stg = mk("stg", [128, 8, 128], F32)
    WL = Loader(stg)
    load_block = WL.enqueue

    WA0_OFF = A.mark()
    wA = mk("wA", [128, 8, 768], BF16)
    wC0 = nc.alloc_sbuf_tensor_at("wC0_alias", [128, 28, 128], BF16, offset=WA0_OFF)
    alrT = A.alloc("alrT", [128, TALL], BF16)
    walr = A.alloc("walr", [128, 8, 128], BF16)
    wa2p = A.alloc("wa2p", [128, 512], BF16)
    nba = A.alloc("nba", [128, 4], F32)
    e32 = A.alloc("e32", [128, 512], F32)
    cum = A.alloc("cum", [128, 512], F32)
    eb = A.alloc("eb", [128, 512], F32)
    enb = A.alloc("enb", [128, 512], F32)
    decs = A.alloc("decs", [128, 4 * 32], F32)
    kTt = mk("kTt", [128, 512], BF16)
    qTt = mk("qTt", [128, 512], BF16)
    ktok = mk("ktok", [128, 512], BF16)
    vtok = mk("vtok", [128, 1024], BF16)
    silr = mk("silr", [128, 1024], BF16)
    attn_sb = mk("attn", [128, 128], BF16, 4)
    osq = A.alloc("osq", [128, 1024], BF16)
    rso = A.alloc("rso", [128, 512], F32)
    rs = A.alloc("rs", [128, 1024], F32)
    Tst = mk("Tst", [128, 256], F32)
    Sbf = mk("Sbf", [128, 256], BF16, 5)

    R.add("pool", lambda e: e.memset(walr.ap(), 0.0), [], ["walr"])
    R.add("pool", lambda e: e.memset(wa2p.ap(), 0.0), [], ["wa2p"])
    cp("dve", wa2p[0:16, :], wa2s[0:16, :], ["wa2p"], ["wa2p"])
    dma(nba.ap(), b_aT_d[:, :], [], ["nba"])
    ts("dve", nba.ap(), nba.ap(), -1.0, ALU.mult, ["nba"], ["nba"])
    load_block(walr[:, :, 0:16], w_in_v[:, :, OFF_ALR:OFF_ALR + 16], "walr")
    WL.flush()
    for G in range(8):
        b = PS.alloc()
        mm_group(bank(b), [(walr[:, k, :], hT(k, G * 512, 512)) for k in range(8)],
                 ["walr", hkey(G * 512)], pk(b))
        act(alrT[:, G * 512:(G + 1) * 512], bank(b), AF.Copy, pk(b), [("alrT", G)])
        PS.release(b)

    def load_head(h):
        st = h % 2
        for (d0, s0, nm) in ((0, OFF_QA + h * 128, "q"), (128, OFF_KA + h * 128, "k"),
                             (256, OFF_VA + h * 256, "v0"), (384, OFF_VA + h * 256 + 128, "v1"),
                             (512, OFF_RA + h * 256, "r0"), (640, OFF_RA + h * 256 + 128, "r1")):
            load_block(wA[st][:, :, d0:d0 + 128], w_in_v[:, :, s0:s0 + 128], ("wA", st, nm))

    def stage1(h, G):
        st, s, own, t0 = h % 2, G % 2, G >= 4, G * 512
        W = wA[st]
        hk = hkey(t0)
        bz = PS.alloc()
        mm_group(bank(bz), [(wa2p[:, h * 128:(h + 1) * 128], alrT[:, t0:t0 + 512])], [("alrT", G), "wa2p"], pk(bz))
        act(e32.ap(), bank(bz), AF.Exp, pk(bz) + ["nba"], ["e32"], scale=-1.0, bias=nba[:, h:h + 1])
        PS.release(bz)
        act(e32.ap(), e32.ap(), AF.Ln, ["e32"], ["e32"], bias=onec[:, 0:1])
        R.add("dve", lambda e: e.tensor_tensor_scan(cum.ap(), scanm.ap(), e32.ap(), 0.0, ALU.mult, ALU.add),
              ["e32"], ["cum"])
        bk = PS.alloc()
        mm_group(bank(bk), [(W[:, k, 128:256], hT(k, t0, 512)) for k in range(8)], [("wA", st, "k"), hk], pk(bk))
        act(enb.ap(), cum.ap(), AF.Exp, ["cum"], ["enb"], scale=1.0 / 16.0)
        if own:
            act(eb.ap(), cum.ap(), AF.Exp, ["cum"], ["eb"], scale=-1.0 / 16.0)
        act(decs[:, h * 32 + G * 4:h * 32 + G * 4 + 4], cum[:, 127:512:128], AF.Exp, ["cum"], [("dec", h, G)],
            scale=-1.0 / 16.0)
        tt("dve", kTt[s].ap(), bank(bk), enb.ap(), ALU.mult, pk(bk) + ["enb"], [("kTt", s)])
        PS.release(bk)
        yield
        if own:
            bq = PS.alloc()
            mm_group(bank(bq), [(W[:, k, 0:128], hT(k, t0, 512)) for k in range(8)], [("wA", st, "q"), hk], pk(bq))
            stt(qTt[s].ap(), bank(bq), float(128.0 ** -0.5), eb.ap(), ALU.mult, ALU.mult, pk(bq) + ["eb"], [("qTt", s)])
            PS.release(bq)
            for c in range(2):
                br = PS.alloc()
                mm_group(bank(br), [(W[:, k, 512 + c * 128:640 + c * 128], hT(k, t0, 512)) for k in range(8)],
                         [("wA", st, "r%d" % c), hk], pk(br))
                act(silr[s][:, c * 512:(c + 1) * 512], bank(br), AF.Silu, pk(br), [("silr", s, c)])
                PS.release(br)
                if c == 0:
                    yield
        else:
            yield
        yield
        for jj in range(2):
            bv = PS.alloc()
            for j in (2 * jj, 2 * jj + 1):
                mm_group(bank(bv)[:, (j % 2) * 256:(j % 2 + 1) * 256],
                         [(hT(k, t0 + j * 128, 128), W[:, k, 256:512]) for k in range(8)],
                         [("wA", st, "v0"), ("wA", st, "v1"), hk], pk(bv))
            act(vtok[s][:, jj * 512:(jj + 1) * 512], bank(bv), AF.Copy, pk(bv), [("vtok", s, jj)])
            PS.release(bv)
            if jj == 0:
                yield

    def front(h, G):
        s, own, t0 = G % 2, G >= 4, G * 512
        if G == 0:
            R.add("pool", lambda e: e.memset(Sbf[0].ap(), 0.0), [], [("Sbf", 0)])
        bt = PS.alloc()

        def ftr(e):
            ins = None
            for j in range(4):
                ins = e.transpose(bankbf(bt)[:, j * 128:(j + 1) * 128], kTt[s][:, j * 128:(j + 1) * 128], ident_b)
            return ins
        R.add("pe", ftr, [("kTt", s)], pk(bt))
        cp("dve", ktok[s].ap(), bankbf(bt)[:, 0:512], pk(bt), [("ktok", s)])
        PS.release(bt)
        if own:
            ba_ = PS.alloc()
            for j in range(4):
                js = slice(j * 128, (j + 1) * 128)
                mm_group(bank(ba_)[:, js], [(kTt[s][:, js], qTt[s][:, js])], [("kTt", s), ("qTt", s)], pk(ba_))
            for j in range(4):
                js = slice(j * 128, (j + 1) * 128)
                tt("dve", attn_sb[j].ap(), bank(ba_)[:, js], triU32, ALU.mult, pk(ba_), [("attn", j)])
            PS.release(ba_)
        bkv = PS.alloc(2)
        for j in range(4):
            js = slice(j * 128, (j + 1) * 128)
            mm_group(bank(bkv, 2)[:, j * 256:(j + 1) * 256], [(ktok[s][:, js], vtok[s][:, j * 256:(j + 1) * 256])],
                     [("ktok", s), ("vtok", s, j // 2)], pk(bkv + j // 2))
        for j in range(4):
            n = G * 4 + j
            kvs = bank(bkv, 2)[:, j * 256:(j + 1) * 256]
            if n == 0:
                cp("dve", Tst[0].ap(), kvs, pk(bkv + j // 2), [("Tst", 0)])
            else:
                stt(Tst[n % 2].ap(), Tst[(n - 1) % 2].ap(), decs[:, h * 32 + n - 1:h * 32 + n], kvs,
                    ALU.mult, ALU.add, [("Tst", (n - 1) % 2), ("dec", h, (n - 1) // 4)] + pk(bkv + j // 2), [("Tst", n % 2)])
            if n < 31:
                ts("pool", Sbf[(n + 1) % 5].ap(), Tst[n % 2].ap(), decs[:, h * 32 + n:h * 32 + n + 1], ALU.mult,
                   [("Tst", n % 2), ("dec", h, G)], [("Sbf", (n + 1) % 5)], s2=1.0, op1=ALU.mult)
        PS.release(bkv, 2)

    bo_of = {}

    def mid(h, G, gen):
        s, own = G % 2, G >= 4
        if own:
            bo_of[(h, G)] = PS.alloc(2)
            bo = bo_of[(h, G)]
        for j in range(4):
            n = G * 4 + j
            js = slice(j * 128, (j + 1) * 128)
            WL.tick()
            if gen is not None:
                next(gen, None)
            if own:
                for c in range(2):
                    mm_group(bank(bo + c)[:, js],
                             [(vtok[s][:, j * 256 + c * 128:j * 256 + (c + 1) * 128], attn_sb[j].ap()),
                              (Sbf[n % 5][:, c * 128:(c + 1) * 128], qTt[s][:, js])],
                             [("vtok", s, j // 2), ("attn", j), ("Sbf", n % 5), ("qTt", s)], pk(bo + c))
        if gen is not None:
            for _ in gen:
                pass

    def tail(h, G):
        s, own = G % 2, G >= 4
        if not own:
            return
        bo = bo_of.pop((h, G))
        act(osq.ap(), bank(bo, 2), AF.Square, pk(bo, 2), ["osq"])
        bs = PS.alloc()
        mm_group(bank(bs), [(ones_b.ap(), osq[:, 0:512]), (ones_b.ap(), osq[:, 512:1024])], ["osq"], pk(bs))
        act(rso.ap(), bank(bs), AF.Ln, pk(bs), ["rso"], bias=epsc[:, 0:1], scale=1.0 / 256.0)
        PS.release(bs)
        act(rso.ap(), rso.ap(), AF.Exp, ["rso"], ["rso"], scale=-0.5)
        for c in range(2):
            cs = slice(c * 512, (c + 1) * 512)
            tt("pool", rs[:, cs], silr[s][:, cs], rso.ap(), ALU.mult, [("silr", s, c), "rso"], [("rs", c)])
            stt(oaT[:, 2 * h + c, (G - 4) * 512:(G - 3) * 512], bank(bo + c), go[:, c:c + 1], rs[:, cs],
                ALU.mult, ALU.mult, pk(bo + c) + [("rs", c)], [("oaT", 2 * h + c, G - 4)])
        PS.release(bo, 2)

    load_head(0)
    WL.flush()
    order = [(h, G) for h in range(4) for G in range(8)]
    for _ in stage1(0, 0):
        pass
    front(0, 0)
    for idx, (h, G) in enumerate(order):
        if G == 0 and h + 1 < 4:
            load_head(h + 1)
        if h == 3 and G == 1:
            war = [("wA", 0, nm_) for nm_ in ("q", "k", "v0", "v1", "r0", "r1")]
            load_block(wC0[:, 0:8, :], w_in_v[:, :, OFF_GA:OFF_GA + 128], ("wC", 0, 0), war)
            load_block(wC0[:, 8:16, :], w_in_v[:, :, OFF_GD:OFF_GD + 128], ("wC", 0, 1), war)
            load_block(wC0[:, 16:24, :], w_go_v[:, :, 0:128], ("wC", 0, 2), war)
            load_block(wC0[:, 24:28, :], w_do_v[:, :, 0:128], ("wC", 0, 3), war)
        if G == 7:
            WL.flush()
        gen = stage1(*order[idx + 1]) if idx + 1 < len(order) else None
        mid(h, G, gen)
        if idx + 1 < len(order):
            front(*order[idx + 1])
        tail(h, G)
    WL.flush()
    if debug == "pA":
        dump("oaT", oaT, [("oaT", c, g) for c in range(8) for g in range(4)])
        return finish()

    R.barrier()
    A.reset(ARENA_A)
    yT = nc.alloc_sbuf_tensor_at("yT_alias", [128, 8, TOWN], BF16, offset=HTP_OFF)
    A.reset(WA0_OFF + 28 * 128 * 2)
    wC = [wC0, A.alloc("wC", [128, 28, 128], BF16)]
    sga = mk("sga", [128, 512], F32)
    sgd = mk("sgd", [128, 512], F32)
    u1 = mk("u1", [128, 512], F32)
    u2 = mk("u2", [128, 512], F32)
    wo = mk("wo", [128, 8, 128], BF16, 3)
    xr = mk("xr", [128, TOWN], F32, 3)
    ob = mk("ob", [128, 512], F32)
    it = 0

    def pre_c(c):
        load_block(wo[c % 3].ap(), w_o_v[:, :, c * 128:(c + 1) * 128], ("wo", c % 3))
        dma(xr[c % 3].ap(), xT[c * 128:(c + 1) * 128, TOWN:TALL], [], [("xr", c % 3)])

    def load_c(c):
        s = c % 2
        cs = slice(c * 128, (c + 1) * 128)
        load_block(wC[s][:, 0:8, :], w_in_v[:, :, OFF_GA + c * 128:OFF_GA + (c + 1) * 128], ("wC", s, 0))
        load_block(wC[s][:, 8:16, :], w_in_v[:, :, OFF_GD + c * 128:OFF_GD + (c + 1) * 128], ("wC", s, 1))
        load_block(wC[s][:, 16:24, :], w_go_v[:, :, cs], ("wC", s, 2))
        load_block(wC[s][:, 24:28, :], w_do_v[:, :, cs], ("wC", s, 3))

    for c in range(8):
        s = c % 2
        cs = slice(c * 128, (c + 1) * 128)
        WL.flush()
        if c + 1 < 8:
            load_c(c + 1)
        if c == 6:
            pre_c(0)
        if c == 7:
            pre_c(1)
        for G in range(4):
            WL.tick(2)
            gs = slice(G * 512, (G + 1) * 512)
            i2 = it % 2
            it += 1
            hk = hkey(TOWN + G * 512)
            bga = PS.alloc()
            mm_group(bank(bga), [(wC[s][:, k, :], hT(k, TOWN + G * 512, 512)) for k in range(8)], [("wC", s, 0), hk], pk(bga))
            bgd = PS.alloc()
            mm_group(bank(bgd), [(wC[s][:, 8 + k, :], hT(k, TOWN + G * 512, 512)) for k in range(8)], [("wC", s, 1), hk], pk(bgd))
            bya = PS.alloc()
            mm_group(bank(bya), [(wC[s][:, 16 + k, :], oaT[:, k, gs]) for k in range(8)],
                     [("wC", s, 2)] + [("oaT", k, G) for k in range(8)], pk(bya))
            byd = PS.alloc()
            mm_group(bank(byd), [(wC[s][:, 24 + k, :], odT[:, k, gs]) for k in range(4)],
                     [("wC", s, 3)] + [("odT", k, G) for k in range(4)], pk(byd))
            act(sga[i2].ap(), bank(bga), AF.Sigmoid, pk(bga), [("sga", i2)])
            act(sgd[i2].ap(), bank(bgd), AF.Sigmoid, pk(bgd), [("sgd", i2)])
            PS.release(bga)
            PS.release(bgd)
            tt("dve", u1[i2].ap(), bank(bya), sga[i2].ap(), ALU.mult, pk(bya) + [("sga", i2)], [("u1", i2)])
            tt("dve", u2[i2].ap(), bank(byd), sgd[i2].ap(), ALU.mult, pk(byd) + [("sgd", i2)], [("u2", i2)])
            PS.release(bya)
            PS.release(byd)
            tt("pool", yT[:, c, gs], u1[i2].ap(), u2[i2].ap(), ALU.add, [("u1", i2), ("u2", i2)], [("yT", c, G)])
    it = 0
    for c in range(8):
        s = c % 3
        cs = slice(c * 128, (c + 1) * 128)
        WL.flush()
        for G in range(4):
            WL.tick()
            gs = slice(G * 512, (G + 1) * 512)
            i2 = it % 2
            it += 1
            bo = PS.alloc()
            mm_group(bank(bo), [(wo[s][:, k, :], yT[:, k, gs]) for k in range(8)],
                     [("wo", s)] + [("yT", k, G) for k in range(8)], pk(bo))
            tt("dve", ob[i2].ap(), bank(bo), xr[s][:, gs], ALU.add, pk(bo) + [("xr", s)], [("ob", i2)])
            PS.release(bo)
            dma(outT[cs, gs], ob[i2].ap(), [("ob", i2)], [("out", c, G)], eng="act")
            out_keys.append(("out", c, G))
            if G == 0 and c + 2 < 8:
                pre_c(c + 2)
    return finish()


def make_consts():
    c = np.zeros((128, 648), np.float32)
    j = np.arange(128)[:, None]
    i = np.arange(128)[None, :]
    c[:, 0:128] = (j == i)
    c[:, 128:256] = (j <= i)
    c[:, 256:384] = (j >= i)
    rt = np.zeros((128, 128), np.float32)
    for m in range(64):
        rt[m + 64, m] = -1.0
        rt[m, m + 64] = 1.0
    c[:, 384:512] = rt
    c[:, 512:640] = (j <= i) * (-1.0 / 16.0)
    half = 64
    inv = (10000.0 ** (-(np.arange(half, dtype=np.float32)) / np.float32(half))).astype(np.float32)
    c[:, 640] = np.concatenate([inv, inv])
    return c


def prep_inputs(x, positions, norm_gain, w_in, gla_w_a2, gla_b_a, gla_out_gain,
                dil_q_gain, dil_k_gain, w_gla_out, w_dil_out, w_o):
    consts = make_consts()
    common = {
        "w_in": np.ascontiguousarray(w_in[0]),
        "w_a2": np.ascontiguousarray(gla_w_a2[0]),
        "b_a": np.ascontiguousarray(gla_b_a[0].reshape(1, 512)),
        "b_aT": np.ascontiguousarray(gla_b_a[0].reshape(4, 128).T),
        "gx": np.ascontiguousarray(norm_gain[0].reshape(8, 128).T),
        "go": np.ascontiguousarray(gla_out_gain[0].reshape(2, 128).T),
        "gqk": np.ascontiguousarray(np.stack([dil_q_gain[0], dil_k_gain[0]], axis=1)),
        "gqk_row": np.ascontiguousarray(np.concatenate([dil_q_gain[0], dil_k_gain[0]]).reshape(1, 256)),
        "w_gla_out": np.ascontiguousarray(w_gla_out[0]),
        "w_dil_out": np.ascontiguousarray(w_dil_out[0]),
        "w_o": np.ascontiguousarray(w_o[0]),
        "consts": consts,
    }
    in_maps = []
    for b in range(NB):
        xb = np.asarray(x[b], np.float32)
        pb = np.asarray(positions[b], np.int32)
        for half in range(2):
            xt = np.zeros((D, TALL), np.float32)
            pp = np.zeros((1, TALL), np.int32)
            if half == 0:
                xt[:, TOWN:] = xb[0:TOWN].T
                pp[0, TOWN:] = pb[0:TOWN]
            else:
                xt[:, :] = xb.T
                pp[0, :] = pb
            m = dict(common)
            m["xT"] = xt
            m["pos"] = pp
            m["flag"] = np.full((128, 1), float(half), np.float32)
            in_maps.append(m)
    return in_maps


_NC_CACHE = {}


def kernel(**inputs):
    inputs = {k: np.asarray(v) for k, v in inputs.items()}
    in_maps = prep_inputs(**inputs)
    if "nc" not in _NC_CACHE:
        _NC_CACHE["nc"] = build_program()[0]
    nc = _NC_CACHE["nc"]
    res = run_bass_kernel_spmd(nc, in_maps, core_ids=list(range(8)))
    out = np.zeros((NB, SEQ, D), np.float32)
    for b in range(NB):
        for half in range(2):
            o = np.asarray(res.results[b * 2 + half]["outT"], np.float32)
            out[b, half * TOWN:(half + 1) * TOWN, :] = o.T
    return out
```

```python
import math
import numpy as np
import concourse.bass as bass
import concourse.mybir as mybir
from concourse.bass_utils import run_bass_kernel_spmd

F32 = mybir.dt.float32
BF16 = mybir.dt.bfloat16
I32 = mybir.dt.int32
AF = mybir.ActivationFunctionType
ALU = mybir.AluOpType
AX = mybir.AxisListType

D = 1024
SEQ = 4096
NB = 4
TOWN = 2048
TALL = 4096
INW = 10256
EPS = 1e-6
OFF_QA, OFF_KA, OFF_VA, OFF_RA, OFF_ALR = 0, 512, 1024, 2048, 3072
OFF_QD, OFF_KD, OFF_VD, OFF_ZD, OFF_GA, OFF_GD = 3088, 4624, 6160, 7696, 8208, 9232
NDSEM = 24
TWO_PI = 2.0 * math.pi
CW1 = 6.28125
CW2 = TWO_PI - CW1
MAGIC = 12582912.0


class Rec:
    def __init__(self):
        self.ops = []
        self.lastw = {}
        self.readers = {}
        self.bar = set()

    def barrier(self):
        last = {}
        dmas = []
        for i, o in enumerate(self.ops):
            if o["dma"]:
                dmas.append(i)
            elif o["fn"] is not None:
                last[o["eng"]] = i
        self.bar = set(last.values()) | set(dmas[-NDSEM:])

    def add(self, eng, fn, r=(), w=(), dma=False):
        i = len(self.ops)
        deps = set(self.bar)
        for k in r:
            j = self.lastw.get(k)
            if j is not None:
                deps.add(j)
        for k in w:
            j = self.lastw.get(k)
            if j is not None:
                deps.add(j)
            for j in self.readers.get(k, ()):
                deps.add(j)
        wset = set(w)
        for k in w:
            self.lastw[k] = i
            self.readers[k] = []
        for k in r:
            if k not in wset:
                self.readers.setdefault(k, []).append(i)
        self.ops.append(dict(eng=eng, fn=fn, deps=deps, dma=dma, sig=False))
        return i

    def emit(self, nc):
        ops = self.ops
        sems = {e: nc.alloc_semaphore("sem_" + e) for e in ("pe", "act", "dve", "pool")}
        dsems = [nc.alloc_semaphore("dsem%d" % i) for i in range(NDSEM)]
        for o in ops:
            for j in o["deps"]:
                pj = ops[j]
                if pj["dma"]:
                    continue
                if pj["eng"] == "pe" and o["eng"] == "pe" and not o["dma"]:
                    continue
                pj["sig"] = True
        cnt = {e: 0 for e in sems}
        nd = 0
        for o in ops:
            if o["dma"]:
                o["dsem"] = nd % NDSEM
                o["dval"] = 16 * (nd // NDSEM + 1)
                nd += 1
            elif o["sig"]:
                cnt[o["eng"]] += 1
                o["sval"] = cnt[o["eng"]]

        def make(engname):
            def body(e):
                waited = {}
                for o in ops:
                    if o["eng"] != engname:
                        continue
                    need = {}
                    for j in o["deps"]:
                        pj = ops[j]
                        if pj["dma"]:
                            key = ("d", pj["dsem"])
                            val = pj["dval"]
                        else:
                            if pj["eng"] == "pe" and engname == "pe" and not o["dma"]:
                                continue
                            key = ("e", pj["eng"])
                            val = pj["sval"]
                        if need.get(key, 0) < val:
                            need[key] = val
                    if o["dma"] and o["dval"] > 16:
                        key = ("d", o["dsem"])
                        if need.get(key, 0) < o["dval"] - 16:
                            need[key] = o["dval"] - 16
                    for key, val in need.items():
                        if waited.get(key, 0) >= val:
                            continue
                        waited[key] = val
                        sem = dsems[key[1]] if key[0] == "d" else sems[key[1]]
                        e.wait_ge(sem, val)
                    ins = o["fn"](e) if o["fn"] is not None else None
                    if o["dma"]:
                        ins.then_inc(dsems[o["dsem"]], 16)
                    elif o["sig"]:
                        ins.then_inc(sems[o["eng"]], 1)

            return body

        with nc.Block() as block:
            block.tensor(make("pe"))
            block.scalar(make("act"))
            block.vector(make("dve"))
            block.gpsimd(make("pool"))
            block.sync(make("sp"))


class Arena:
    def __init__(self, nc, base, top):
        self.nc, self.base, self.top, self.cur = nc, base, top, base
        self.n = 0

    def alloc(self, name, shape, dtype):
        nbytes = int(np.prod(shape[1:])) * (4 if dtype in (F32, I32) else 2)
        nbytes = (nbytes + 31) // 32 * 32
        off = self.cur
        self.cur += nbytes
        assert self.cur <= self.top, ("SBUF arena overflow", name, self.cur, self.top)
        self.n += 1
        return self.nc.alloc_sbuf_tensor_at("%s_%d" % (name, self.n), list(shape), dtype, offset=off)

    def mark(self):
        return self.cur

    def reset(self, m):
        self.cur = m


class PsumPool:
    def __init__(self, ps):
        self.ps = ps
        self.free = list(range(8))

    def alloc(self, n=1):
        if n == 1:
            assert self.free, "out of PSUM banks"
            return self.free.pop(0)
        for b in list(self.free):
            if all((b + i) in self.free for i in range(n)):
                for i in range(n):
                    self.free.remove(b + i)
                return b
        raise AssertionError("out of adjacent PSUM banks")

    def release(self, b, n=1):
        for i in range(n):
            self.free.append(b + i)


def build_program(debug=None):
    nc = bass.Bass("TRN2", target_bir_lowering=False)
    R = Rec()

    def dram(name, shape, dt=F32, kind="ExternalInput"):
        return nc.dram_tensor(name, list(shape), dt, kind=kind).ap()

    xT = dram("xT", [D, TALL])
    pos = dram("pos", [1, TALL], I32)
    flag_d = dram("flag", [128, 1])
    w_in = dram("w_in", [D, INW])
    w_a2_d = dram("w_a2", [16, 512])
    b_a_d = dram("b_a", [1, 512])
    b_aT_d = dram("b_aT", [128, 4])
    gx_d = dram("gx", [128, 8])
    go_d = dram("go", [128, 2])
    gqk_d = dram("gqk", [128, 2])
    gqk_row_d = dram("gqk_row", [1, 256])
    w_go_d = dram("w_gla_out", [1024, 1024])
    w_do_d = dram("w_dil_out", [512, 1024])
    w_o_d = dram("w_o", [1024, 1024])
    consts_d = dram("consts", [128, 648])
    outT = dram("outT", [D, TOWN], F32, kind="ExternalOutput")
    dbg_out = {}

    A = Arena(nc, 16512, 229344)
    ps_t = nc.alloc_psum_tensor("ps", [128, 4096], F32)
    PS = PsumPool(ps_t)

    def bank(b, n=1):
        return ps_t[:, b * 512:(b + n) * 512]

    def bankbf(b):
        return ps_t[:, b * 512:(b + 1) * 512].bitcast(BF16)

    def pk(b, n=1):
        return [("ps", b + i) for i in range(n)]

    hTo = A.alloc("hTo", [128, 8, TOWN], BF16)
    HTP_OFF = A.mark()
    hTp = A.alloc("hTp", [128, 8, TOWN], BF16)
    cst32 = A.alloc("cst32", [128, 648], F32)
    cstb = A.alloc("cstb", [128, 512], BF16)
    ones_b = A.alloc("ones_b", [128, 128], BF16)
    maskP = A.alloc("maskP", [128, 128], BF16)
    gx = A.alloc("gx", [128, 8], F32)
    go = A.alloc("go", [128, 2], F32)
    gqk = A.alloc("gqk", [128, 2], F32)
    flag = A.alloc("flag", [128, 1], F32)
    negC = A.alloc("negC", [128, 2], F32)
    ones32 = A.alloc("ones32", [1, 128], F32)
    wa2s = A.alloc("wa2s", [16, 512], F32)
    gqkrow = A.alloc("gqkrow", [1, 256], F32)
    mx = A.alloc("mx", [1, 4], F32)
    epsc = A.alloc("epsc", [128, 1], F32)
    posi = A.alloc("posi", [128, 512], I32)
    scanm = A.alloc("scanm", [128, 512], F32)
    onec = A.alloc("onec", [128, 1], F32)
    M_PP = A.alloc("M_PP", [128, 512], BF16)
    M_PL = A.alloc("M_PL", [128, 512], BF16)
    M_LL = A.alloc("M_LL", [128, 512], BF16)
    odT = A.alloc("odT", [128, 4, TOWN], BF16)
    ARENA_B = A.mark()
    oaT = A.alloc("oaT", [128, 8, TOWN], BF16)
    ARENA_A = A.mark()
    A.reset(ARENA_B)
    cosT = A.alloc("cosT", [128, TALL], BF16)
    sinT = A.alloc("sinT", [128, TALL], BF16)
    ARENA_B2 = A.mark()

    ident32 = cst32[:, 0:128]
    triU32 = cst32[:, 128:256]
    triL32 = cst32[:, 256:384]
    triN32 = cst32[:, 512:640]
    invf = cst32[:, 640:641]
    ident_b = cstb[:, 0:128]
    triU_b = cstb[:, 128:256]
    triL_b = cstb[:, 256:384]
    RT_b = cstb[:, 384:512]

    def hT(k, t0, n, step=1):
        last = t0 + (n - 1) * step
        if t0 >= TOWN:
            a = t0 - TOWN
            return hTo[:, k, a:a + (n - 1) * step + 1:step] if step > 1 else hTo[:, k, a:a + n]
        assert last < TOWN
        return hTp[:, k, t0:t0 + (n - 1) * step + 1:step] if step > 1 else hTp[:, k, t0:t0 + n]

    def hkey(t0):
        return ("hT", t0 // 512)

    def dma(out, in_, r, w, eng="sp"):
        return R.add(eng, lambda e: e.dma_start(out=out, in_=in_), r, w, dma=True)

    def mm_group(out, pairs, r, w):
        n = len(pairs)

        def fn(e):
            ins = None
            for i, (l, rr) in enumerate(pairs):
                ins = e.matmul(out, l, rr, start=(i == 0), stop=(i == n - 1))
            return ins

        return R.add("pe", fn, r, w)

    def act(out, in_, func, r, w, bias=None, scale=None):
        kw = {}
        if bias is not None:
            kw["bias"] = bias
        if scale is not None:
            kw["scale"] = scale
        return R.add("act", lambda e: e.activation(out, in_, func, **kw), r, w)

    def tt(eng, out, in0, in1, op, r, w):
        return R.add(eng, lambda e: e.tensor_tensor(out, in0, in1, op), r, w)

    def ts(eng, out, in0, s1, op0, r, w, s2=None, op1=None):
        if op1 is None:
            return R.add(eng, lambda e: e.tensor_scalar(out, in0, s1, None, op0), r, w)
        return R.add(eng, lambda e: e.tensor_scalar(out, in0, s1, s2, op0, op1), r, w)

    def stt(out, in0, scalar, in1, op0, op1, r, w):
        return R.add("dve", lambda e: e.scalar_tensor_tensor(out, in0, scalar, in1, op0, op1), r, w)

    def cp(eng, out, in_, r, w):
        if eng == "act":
            return R.add(eng, lambda e: e.activation(out, in_, AF.Copy), r, w)
        return R.add(eng, lambda e: e.tensor_copy(out, in_), r, w)

    dma(cst32[:, :], consts_d[:, :], [], ["cst32"])
    dma(gx[:, :], gx_d[:, :], [], ["gx"])
    dma(go[:, :], go_d[:, :], [], ["go"])
    dma(gqk[:, :], gqk_d[:, :], [], ["gqk"])
    dma(flag[:, :], flag_d[:, :], [], ["flag"])
    dma(wa2s[:, :], w_a2_d[:, :], [], ["wa2s"])
    dma(gqkrow[:, :], gqk_row_d[:, :], [], ["gqkrow"])
    cp("dve", cstb[:, :], cst32[:, 0:512], ["cst32"], ["cstb"])
    R.add("dve", lambda e: e.memset(ones_b[:, :], 1.0), [], ["ones_b"])
    R.add("dve", lambda e: e.memset(ones32[:, :], 1.0), [], ["ones32"])
    R.add("dve", lambda e: e.memset(epsc[:, :], EPS), [], ["epsc"])
    R.add("dve", lambda e: e.memset(scanm[:, :], 1.0), [], ["scanm"])
    for j in range(4):
        R.add("dve", lambda e, j=j: e.memset(scanm[:, j * 128:j * 128 + 1], 0.0), ["scanm"], ["scanm"])
    R.add("dve", lambda e: e.memset(onec[:, :], 1.0), [], ["onec"])
    ts("dve", maskP[:, :], triL32, flag[:, 0:1], ALU.mult, ["cst32", "flag"], ["maskP"])
    for (M, first, second) in ((M_PP, "P", "P"), (M_PL, "P", "L"), (M_LL, "L", "L")):
        for u, kind in enumerate((first, second)):
            if kind == "P":
                ts("dve", M[:, u * 256:u * 256 + 128], triL32, flag[:, 0:1], ALU.mult, ["cst32", "flag"], ["masks"])
            else:
                cp("dve", M[:, u * 256:u * 256 + 128], triL32, ["cst32"], ["masks"])
            cp("dve", M[:, u * 256 + 128:u * 256 + 256], triU32, ["cst32"], ["masks"])
    R.add("dve", lambda e: e.tensor_reduce(mx[:, 0:1], gqkrow[:, 0:128], AX.X, ALU.max,
                                           apply_absolute_value=True), ["gqkrow"], ["mx0"])
    R.add("dve", lambda e: e.tensor_reduce(mx[:, 1:2], gqkrow[:, 128:256], AX.X, ALU.max,
                                           apply_absolute_value=True), ["gqkrow"], ["mx1"])
    ts("dve", mx[:, 2:3], mx[:, 0:1], mx[:, 1:2], ALU.mult, ["mx0", "mx1"], ["mx2"],
       s2=-math.sqrt(128.0), op1=ALU.mult)
    cp("dve", mx[:, 3:4], mx[:, 2:3], ["mx2"], ["mx3"])
    b0 = PS.alloc()
    mm_group(bank(b0)[:, 0:2], [(ones32[0:1, 0:128], mx[0:1, 2:4])], ["ones32", "mx2", "mx3"], pk(b0))
    cp("dve", negC[:, :], bank(b0)[:, 0:2], pk(b0), ["negC"])
    PS.release(b0)


    out_keys = []

    def dump(nm, t, keys):
        dd = dram("dbg_" + nm, list(t.shape), t.dtype, kind="ExternalOutput")
        dbg_out[nm] = dd
        dma(dd, t.ap(), keys, [("dbg", nm)])

    def finish():
        R.add("sp", None, [("dbg", nm) for nm in dbg_out] + out_keys, [])
        R.emit(nc)
        return nc, dbg_out

    def recip(t_ap, key):
        R.add("dve", lambda e: e.reciprocal(t_ap, t_ap), [key], [key])

    def mk(name, shape, dt, n=2):
        return [A.alloc(name, shape, dt) for _ in range(n)]

    class Loader:
        def __init__(self, stg_):
            self.stg, self.q, self.nd, self.ncast = stg_, [], 0, 0

        def enqueue(self, dst, src, dkey, extra=()):
            self.q.append((dst, src, dkey, tuple(extra)))

        def tick(self, n=1):
            for _ in range(n):
                if self.nd < len(self.q) and self.nd - self.ncast < 2:
                    dst, src, dkey, extra = self.q[self.nd]
                    sl = self.nd % 2
                    K_, n_ = dst.shape[1], dst.shape[2]
                    dma(self.stg[sl][:, 0:K_, 0:n_], src, [], [("stg", sl)])
                    self.nd += 1
                elif self.ncast < self.nd:
                    dst, src, dkey, extra = self.q[self.ncast]
                    sl = self.ncast % 2
                    K_, n_ = dst.shape[1], dst.shape[2]
                    cp("act", dst, self.stg[sl][:, 0:K_, 0:n_], [("stg", sl)], [dkey] + list(extra))
                    self.ncast += 1

        def flush(self):
            while self.ncast < len(self.q):
                self.tick()

    xTv = xT.rearrange("(k p) t -> p k t", p=128)
    w_in_v = w_in.rearrange("(k p) c -> p k c", p=128)
    w_go_v = w_go_d.rearrange("(k p) c -> p k c", p=128)
    w_do_v = w_do_d.rearrange("(k p) c -> p k c", p=128)
    w_o_v = w_o_d.rearrange("(k p) c -> p k c", p=128)

    A.reset(ARENA_B2)
    stgB = mk("stg", [128, 8, 128], F32)
    wB0_p = A.alloc("wB", [128, 8, 384], BF16)
    wz_p = A.alloc("wz", [128, 8, 128], BF16)
    B_PRE_END = A.mark()
    WL0 = Loader(stgB)
    WL0.enqueue(wz_p.ap(), w_in_v[:, :, OFF_ZD:OFF_ZD + 128], "wz")
    for (d0, s0, nm) in ((0, OFF_QD, "q"), (128, OFF_KD, "k"), (256, OFF_VD, "v")):
        WL0.enqueue(wB0_p[:, :, d0:d0 + 128], w_in_v[:, :, s0:s0 + 128], ("wB", 0, nm))
    ang_r = mk("ang", [128, 512], F32)
    kk_r = mk("kk", [128, 512], F32)
    xs_r = mk("xs", [128, 8, 512], F32, 3)
    sq_r = mk("sq", [128, 8, 512], BF16)
    rb_r = mk("rb", [128, 512], F32)

    def tab1(qq):
        ang, kk = ang_r[qq % 2], kk_r[qq % 2]
        ka, kkk = ("ang", qq % 2), ("kk", qq % 2)
        cs_ = slice(qq * 512, (qq + 1) * 512)
        dma(posi[:, :], pos[:, cs_].partition_broadcast(128)[:, 0, :], [], ["posi"])
        ts("dve", ang[:, :], posi[:, :], invf, ALU.mult, ["posi"], [ka])
        ts("dve", kk[:, :], ang[:, :], 1.0 / TWO_PI, ALU.mult, [ka], [kkk], s2=MAGIC, op1=ALU.add)
        ts("dve", kk[:, :], kk[:, :], MAGIC, ALU.subtract, [kkk], [kkk])
        stt(ang[:, :], kk[:, :], -CW1, ang[:, :], ALU.mult, ALU.add, [kkk, ka], [ka])
        stt(ang[:, :], kk[:, :], -CW2, ang[:, :], ALU.mult, ALU.add, [kkk, ka], [ka])
        ts("dve", ang[:, :], ang[:, :], math.pi, ALU.min, [ka], [ka], s2=-math.pi, op1=ALU.max)

    def tab2(qq):
        ang, kk = ang_r[qq % 2], kk_r[qq % 2]
        ka, kkk = ("ang", qq % 2), ("kk", qq % 2)
        cs_ = slice(qq * 512, (qq + 1) * 512)
        act(sinT[:, cs_], ang[:, :], AF.Sin, [ka], [("sinT", qq)])
        act(kk[:, :], ang[:, :], AF.Sin, [ka], [kkk], scale=0.5)

    def tab3(qq):
        kk = kk_r[qq % 2]
        kkk = ("kk", qq % 2)
        cs_ = slice(qq * 512, (qq + 1) * 512)
        tt("dve", kk[:, :], kk[:, :], kk[:, :], ALU.mult, [kkk], [kkk])
        ts("dve", cosT[:, cs_], kk[:, :], -2.0, ALU.mult, [kkk], [("cosT", qq)], s2=1.0, op1=ALU.add)

    def p0_x1(G):
        s, s3 = G % 2, G % 3
        xs, sq = xs_r[s3], sq_r[s]
        if G >= 2:
            dma(xs.ap(), xTv[:, :, G * 512:(G + 1) * 512], [("xs", (G - 2) % 3)], [("xs", s3)])
        act(sq.ap(), xs.ap(), AF.Square, [("xs", s3)], [("sq", s)])

    def p0_x2(G):
        s = G % 2
        sq, rb = sq_r[s], rb_r[s]
        b = PS.alloc()
        mm_group(bank(b), [(ones_b.ap(), sq[:, k, :]) for k in range(8)], [("sq", s)], pk(b))
        act(rb.ap(), bank(b), AF.Ln, pk(b), [("rb", s)], bias=epsc[:, 0:1], scale=1.0 / D)
        PS.release(b)
        act(rb.ap(), rb.ap(), AF.Exp, [("rb", s)], [("rb", s)], scale=-0.5)

    xg_r = mk("xg", [128, 512], F32)

    def p0_y(G):
        s, s3 = G % 2, G % 3
        xs, rb = xs_r[s3], rb_r[s]
        for k in range(6):
            stt(hT(k, G * 512, 512), xs[:, k, :], gx[:, k:k + 1], rb.ap(), ALU.mult, ALU.mult,
                [("xs", s3), ("rb", s)], [hkey(G * 512)])
        for k in (6, 7):
            xg = xg_r[k % 2]
            act(xg.ap(), xs[:, k, :], AF.Copy, [("xs", s3)], [("xg", k % 2)], scale=gx[:, k:k + 1])
            tt("pool", hT(k, G * 512, 512), xg.ap(), rb.ap(), ALU.mult, [("xg", k % 2), ("rb", s)], [hkey(G * 512)])

    dma(xs_r[0].ap(), xTv[:, :, 0:512], [], [("xs", 0)])
    dma(xs_r[1].ap(), xTv[:, :, 512:1024], [("xs", 0)], [("xs", 1)])
    R.barrier()
    tab1(0)
    p0_x1(0)
    p0_x2(0)
    tab1(1)
    for G in range(8):
        if G + 1 < 8:
            p0_x1(G + 1)
        tab2(G)
        p0_y(G)
        if G + 1 < 8:
            p0_x2(G + 1)
        if G == 1:
            WL0.tick(2)
        if G == 6:
            WL0.tick(4)
        tab3(G)
        if G + 2 < 8:
            tab1(G + 2)
    WL0.flush()
    if debug == "p0":
        dump("sinT", sinT, [("sinT", q) for q in range(8)])
        dump("cosT", cosT, [("cosT", q) for q in range(8)])
        dump("hTo", hTo, [hkey(t) for t in range(0, TALL, 512)])
        dump("hTp", hTp, [hkey(t) for t in range(0, TALL, 512)])
        return finish()

    R.barrier()
    A.reset(B_PRE_END)
    stg = stgB
    WL = Loader(stg)
    load_block = WL.enqueue

    wB0 = wB0_p
    wz = wz_p
    wB = [wB0, A.alloc("wB", [128, 8, 384], BF16)]
    qr = A.alloc("qr", [128, TOWN], BF16)
    kr = A.alloc("kr", [128, TALL], BF16)
    vt = A.alloc("vt", [128, 32 * 128], BF16)
    Oacc = A.alloc("Oacc", [128, TOWN], F32)
    Lacc = A.alloc("Lacc", [128, TOWN], F32)
    silz_r = mk("silz", [128, TOWN], BF16)
    nsq = mk("nsq", [128, 512], BF16, 4)
    nrs = mk("nrs", [128, 512], F32)
    kn = mk("kn", [128, 512], BF16, 3)
    t1 = mk("t1", [128, 512], F32)
    t2 = mk("t2", [128, 512], F32)
    pT = mk("pT", [128, 512], BF16, 6)

    def load_B(hd, g, st):
        n = g * 4 + hd
        for (d0, s0, nm) in ((0, OFF_QD + n * 128, "q"), (128, OFF_KD + n * 128, "k"), (256, OFF_VD + n * 128, "v")):
            load_block(wB[st][:, :, d0:d0 + 128], w_in_v[:, :, s0:s0 + 128], ("wB", st, nm))

    allh = [hkey(t) for t in range(0, TALL, 512)]
    wcnt = [0]
    pcnt = [0]

    def combine(hd):
        silz = silz_r[hd % 2]
        for G in range(4):
            gs = slice(G * 512, (G + 1) * 512)
            i2 = G % 2
            act(nrs[i2].ap(), Lacc[:, gs], AF.Ln, ["Lacc", ("nrs", i2)], [("nrs", i2)])
            act(nrs[i2].ap(), nrs[i2].ap(), AF.Exp, [("nrs", i2)], [("nrs", i2)], scale=-1.0)
            tt("pool", t1[i2].ap(), Oacc[:, gs], nrs[i2].ap(), ALU.mult, ["Oacc", ("nrs", i2)], [("t1", i2)])
            tt("dve", odT[:, hd, gs], t1[i2].ap(), silz[:, gs], ALU.mult, [("t1", i2), ("silz", hd % 2, G)], [("odT", hd, G)])

    for hd in range(4):
        silz = silz_r[hd % 2]
        WL.flush()
        for G in range(4):
            b = PS.alloc()
            mm_group(bank(b), [(wz[:, k, :], hT(k, TOWN + G * 512, 512)) for k in range(8)], ["wz", hkey(TOWN + G * 512)], pk(b))
            act(silz[:, G * 512:(G + 1) * 512], bank(b), AF.Silu, pk(b), [("silz", hd % 2, G)])
            PS.release(b)
        for g in range(3):
            st = wcnt[0] % 2
            wcnt[0] += 1
            W = wB[st]
            dil = 4 ** g
            Pn = 128 * dil
            nb = 16 // dil
            nxt = (hd, g + 1) if g < 2 else ((hd + 1, 0) if hd < 3 else None)
            WL.flush()
            if nxt is not None:
                load_B(nxt[0], nxt[1], wcnt[0] % 2)
            if g == 2 and hd < 3:
                load_block(wz.ap(), w_in_v[:, :, OFF_ZD + (hd + 1) * 128:OFF_ZD + (hd + 2) * 128], "wz")
            kp = [(TOWN - Pn, 128)] if Pn == 128 else [(TOWN - Pn + i * 512, 512) for i in range(Pn // 512)]
            kp += [(TOWN + i * 512, 512) for i in range(4)]
            pieces = [dict(t0=t0, n=n, dst=kr[:, t0:t0 + n], dkey=("kr", t0 // 128), c0=128, wk=("wB", st, "k"),
                           gcol=gqk[:, 1:2]) for (t0, n) in kp]
            pieces += [dict(t0=TOWN + i * 512, n=512, dst=qr[:, i * 512:(i + 1) * 512], dkey=("qr", i), c0=0,
                            wk=("wB", st, "q"), gcol=gqk[:, 0:1]) for i in range(4)]
            krkeys = [("kr", t // 128) for (t, n) in kp]
            qrkeys = [("qr", i) for i in range(4)]

            def st_a(p):
                p["id"] = pcnt[0]
                pcnt[0] += 1
                i3, n, t0 = p["id"] % 4, p["n"], p["t0"]
                p["bp"] = PS.alloc()
                mm_group(bank(p["bp"])[:, 0:n], [(W[:, k, p["c0"]:p["c0"] + 128], hT(k, t0, n)) for k in range(8)],
                         [p["wk"], hkey(t0)], pk(p["bp"]))
                act(nsq[i3][:, 0:n], bank(p["bp"])[:, 0:n], AF.Square, pk(p["bp"]), [("nsq", i3)])

            def st_b(p):
                i2, i3, ik, n, t0, bp = p["id"] % 2, p["id"] % 4, p["id"] % 3, p["n"], p["t0"], p["bp"]
                bs = PS.alloc()
                mm_group(bank(bs)[:, 0:n], [(ones_b.ap(), nsq[i3][:, 0:n])], [("nsq", i3)], pk(bs))
                act(nrs[i2][:, 0:n], bank(bs)[:, 0:n], AF.Ln, pk(bs), [("nrs", i2)], bias=epsc[:, 0:1], scale=1.0 / 128.0)
                PS.release(bs)
                act(nrs[i2][:, 0:n], nrs[i2][:, 0:n], AF.Exp, [("nrs", i2)], [("nrs", i2)], scale=-0.5)
                stt(kn[ik][:, 0:n], bank(bp)[:, 0:n], p["gcol"], nrs[i2][:, 0:n], ALU.mult, ALU.mult,
                    pk(bp) + [("nrs", i2)], [("kn", ik)])
                PS.release(bp)

            def st_c(p):
                i2, ik, n, t0 = p["id"] % 2, p["id"] % 3, p["n"], p["t0"]
                br = PS.alloc()
                mm_group(bank(br)[:, 0:n], [(RT_b, kn[ik][:, 0:n])], [("kn", ik)], pk(br))
                tt("pool", t1[i2][:, 0:n], kn[ik][:, 0:n], cosT[:, t0:t0 + n], ALU.mult, [("kn", ik)], [("t1", i2)])
                tt("dve", t2[i2][:, 0:n], bank(br)[:, 0:n], sinT[:, t0:t0 + n], ALU.mult, pk(br), [("t2", i2)])
                PS.release(br)
                tt("dve", p["dst"], t1[i2][:, 0:n], t2[i2][:, 0:n], ALU.add,
                   [("t1", i2), ("t2", i2)], [p["dkey"]])

            blocks = [(r, b) for r in range(dil) for b in range(-1, nb)]

            def v_chunk(i0):
                chunk = blocks[i0:i0 + 4]
                bv = PS.alloc()
                for qi, (r, b) in enumerate(chunk):
                    tau0 = TOWN + 128 * b * dil + r
                    mm_group(bank(bv)[:, qi * 128:(qi + 1) * 128],
                             [(hT(k, tau0, 128, dil), W[:, k, 256:384]) for k in range(8)],
                             [("wB", st, "v")] + allh, pk(bv))
                cp("act", vt[:, i0 * 128:(i0 + len(chunk)) * 128],
                   bank(bv)[:, 0:len(chunk) * 128], pk(bv), [("vt", i0 // 4)])
                PS.release(bv)

            v_list = list(range(0, len(blocks), 4))
            v_pos = [0]

            def v_some(n, keep=0):
                for _ in range(n):
                    if v_pos[0] < len(v_list) - keep:
                        v_chunk(v_list[v_pos[0]])
                        v_pos[0] += 1

            NP = len(pieces)
            for i in range(NP + 4):
                WL.tick(2)
                if i < NP:
                    st_a(pieces[i])
                else:
                    v_some(1, keep=1)
                if 0 <= i - 2 < NP:
                    st_b(pieces[i - 2])
                if 0 <= i - 4 < NP:
                    st_c(pieces[i - 4])
            v_some(len(v_list), keep=2)
            if g == 1:
                qblocks = [(r, b) for b in range(nb) for r in range(dil)]
            else:
                qblocks = [(r, b) for r in range(dil) for b in range(nb)]
            npair = len(qblocks) // 2
            sc = float(128.0 ** -0.5)
            state = {}

            def scores(i):
                bsT = PS.alloc()
                for u in range(2):
                    r, b = qblocks[2 * i + u]
                    q0 = 128 * b * dil + r
                    qs = qr[:, q0:q0 + 127 * dil + 1:dil]
                    tp = TOWN + 128 * (b - 1) * dil + r
                    tc = TOWN + 128 * b * dil + r
                    qk = [("qr", t // 512) for t in range(q0 - q0 % 512, q0 + 127 * dil + 1, 512)]
                    kk1 = [("kr", t0_ // 128) for (t0_, n_) in kp if t0_ < tp + 127 * dil + 1 and t0_ + n_ > tp]
                    kk2 = [("kr", t0_ // 128) for (t0_, n_) in kp if t0_ < tc + 127 * dil + 1 and t0_ + n_ > tc]
                    mm_group(bank(bsT)[:, u * 256:u * 256 + 128], [(kr[:, tp:tp + 127 * dil + 1:dil], qs)],
                             kk1 + qk, pk(bsT))
                    mm_group(bank(bsT)[:, u * 256 + 128:u * 256 + 256], [(kr[:, tc:tc + 127 * dil + 1:dil], qs)],
                             kk2 + qk, pk(bsT))
                p = pT[i % 6]
                act(p.ap(), bank(bsT), AF.Exp, pk(bsT), [("pT", i % 6)], bias=negC[:, 0:1], scale=sc)
                PS.release(bsT)
                f0 = qblocks[2 * i][1] == 0
                f1 = qblocks[2 * i + 1][1] == 0
                assert not (f1 and not f0)
                M = M_PP if (f0 and f1) else (M_PL if f0 else M_LL)
                tt("dve" if i % 4 else "pool", p.ap(), p.ap(), M.ap(), ALU.mult, [("pT", i % 6)], [("pT", i % 6)])

            def pv(i):
                p = pT[i % 6]
                for u in range(2):
                    j = 2 * i + u
                    r, b = qblocks[j]
                    qi = j % 4
                    if qi == 0:
                        state["bO"] = PS.alloc()
                        state["bL"] = PS.alloc()
                    bO, bL = state["bO"], state["bL"]
                    vp = (r * (nb + 1) + b) * 128
                    vc = (r * (nb + 1) + b + 1) * 128
                    mm_group(bank(bO)[:, qi * 128:(qi + 1) * 128],
                             [(vt[:, vp:vp + 128], p[:, u * 256:u * 256 + 128]),
                              (vt[:, vc:vc + 128], p[:, u * 256 + 128:u * 256 + 256])],
                             [("vt", (vp // 128) // 4), ("vt", (vc // 128) // 4), ("pT", i % 6)], pk(bO))
                    mm_group(bank(bL)[:, qi * 128:(qi + 1) * 128],
                             [(ones_b.ap(), p[:, u * 256:u * 256 + 128]), (ones_b.ap(), p[:, u * 256 + 128:u * 256 + 256])],
                             [("pT", i % 6)], pk(bL))
                    if qi == 3:
                        r0, b0_ = qblocks[j - 3]
                        if g == 0:
                            ov = Oacc[:, b0_ * 128:b0_ * 128 + 512]
                            lv = Lacc[:, b0_ * 128:b0_ * 128 + 512]
                            so, sl = bank(bO), bank(bL)
                        elif g == 1:
                            ov = Oacc[:, b0_ * 512:(b0_ + 1) * 512].rearrange("p (i r) -> p r i", r=4)
                            lv = Lacc[:, b0_ * 512:(b0_ + 1) * 512].rearrange("p (i r) -> p r i", r=4)
                            so = bank(bO).rearrange("p (r i) -> p r i", r=4)
                            sl = bank(bL).rearrange("p (r i) -> p r i", r=4)
                        else:
                            ov = Oacc.ap().rearrange("p (i r) -> p r i", r=16)[:, r0:r0 + 4, :]
                            lv = Lacc.ap().rearrange("p (i r) -> p r i", r=16)[:, r0:r0 + 4, :]
                            so = bank(bO).rearrange("p (r i) -> p r i", r=4)
                            sl = bank(bL).rearrange("p (r i) -> p r i", r=4)
                        if g == 0:
                            cp("dve", ov, so, pk(bO), ["Oacc"])
                            cp("act", lv, sl, pk(bL), ["Lacc"])
                        else:
                            tt("dve", ov, so, ov, ALU.add, pk(bO) + ["Oacc"], ["Oacc"])
                            tt("dve", lv, sl, lv, ALU.add, pk(bL) + ["Lacc"], ["Lacc"])
                        PS.release(bO)
                        PS.release(bL)

            LOOK = 4
            for i in range(min(LOOK, npair)):
                scores(i)
            v_some(2)
            for i in range(npair):
                if i + LOOK < npair:
                    scores(i + LOOK)
                pv(i)
                if i == 0 and g == 0 and hd >= 1:
                    combine(hd - 1)
    combine(3)
    if debug == "pB":
        dump("odT", odT, [("odT", c, g) for c in range(4) for g in range(4)])
        return finish()

    R.barrier()
    A.reset(ARENA_A)
    stg = mk("stg", [128, 8, 128], F32)
    WL = Loader(stg)
    load_block = WL.enqueue

    WA0_OFF = A.mark()
    wA = mk("wA", [128, 8, 768], BF16)
    wC0 = nc.alloc_sbuf_tensor_at("wC0_alias", [128, 28, 128], BF16, offset=WA0_OFF)
    alrT = A.alloc("alrT", [128, TALL], BF16)
    walr = A.alloc("walr", [128, 8, 128], BF16)
    wa2p = A.alloc("wa2p", [128, 512], BF16)
    nba = A.alloc("nba", [128, 4], F32)
    e32 = A.alloc("e32", [128, 512], F32)
    cum = A.alloc("cum", [128, 512], F32)
    eb = A.alloc("eb", [128, 512], F32)
    enb = A.alloc("enb", [128, 512], F32)
    decs = A.alloc("decs", [128, 4 * 32], F32)
    kTt = mk("kTt", [128, 512], BF16)
    qTt = mk("qTt", [128, 512], BF16)
    ktok = mk("ktok", [128, 512], BF16)
    vtok = mk("vtok", [128, 1024], BF16)
    silr = mk("silr", [128, 1024], BF16)
    attn_sb = mk("attn", [128, 128], BF16, 4)
    osq = A.alloc("osq", [128, 1024], BF16)
    rso = A.alloc("rso", [128, 512], F32)
    rs = A.alloc("rs", [128, 1024], F32)
    Tst = mk("Tst", [128, 256], F32)
    Sbf = mk("Sbf", [128, 256], BF16, 5)

    R.add("pool", lambda e: e.memset(walr.ap(), 0.0), [], ["walr"])
    R.add("pool", lambda e: e.memset(wa2p.ap(), 0.0), [], ["wa2p"])
    cp("dve", wa2p[0:16, :], wa2s[0:16, :], ["wa2p"], ["wa2p"])
    dma(nba.ap(), b_aT_d[:, :], [], ["nba"])
    ts("dve", nba.ap(), nba.ap(), -1.0, ALU.mult, ["nba"], ["nba"])
    def load_head(h):
        st = h % 2
        for (d0, s0, nm) in ((0, OFF_QA + h * 128, "q"), (128, OFF_KA + h * 128, "k"),
                             (256, OFF_VA + h * 256, "v0"), (384, OFF_VA + h * 256 + 128, "v1"),
                             (512, OFF_RA + h * 256, "r0"), (640, OFF_RA + h * 256 + 128, "r1")):
            load_block(wA[st][:, :, d0:d0 + 128], w_in_v[:, :, s0:s0 + 128], ("wA", st, nm))

    load_block(walr[:, :, 0:16], w_in_v[:, :, OFF_ALR:OFF_ALR + 16], "walr")
    load_head(0)
    WL.tick(3)
    for G in range(8):
        WL.tick(2)
        b = PS.alloc()
        mm_group(bank(b), [(walr[:, k, :], hT(k, G * 512, 512)) for k in range(8)],
                 ["walr", hkey(G * 512)], pk(b))
        act(alrT[:, G * 512:(G + 1) * 512], bank(b), AF.Copy, pk(b), [("alrT", G)])
        PS.release(b)
    WL.flush()

    def stage1(h, G):
        st, s, own, t0 = h % 2, G % 2, G >= 4, G * 512
        W = wA[st]
        hk = hkey(t0)
        bz = PS.alloc()
        mm_group(bank(bz), [(wa2p[:, h * 128:(h + 1) * 128], alrT[:, t0:t0 + 512])], [("alrT", G), "wa2p"], pk(bz))
        act(e32.ap(), bank(bz), AF.Exp, pk(bz) + ["nba"], ["e32"], scale=-1.0, bias=nba[:, h:h + 1])
        PS.release(bz)
        act(e32.ap(), e32.ap(), AF.Ln, ["e32"], ["e32"], bias=onec[:, 0:1])
        R.add("dve", lambda e: e.tensor_tensor_scan(cum.ap(), scanm.ap(), e32.ap(), 0.0, ALU.mult, ALU.add),
              ["e32"], ["cum"])
        bk = PS.alloc()
        mm_group(bank(bk), [(W[:, k, 128:256], hT(k, t0, 512)) for k in range(8)], [("wA", st, "k"), hk], pk(bk))
        act(enb.ap(), cum.ap(), AF.Exp, ["cum"], ["enb"], scale=1.0 / 16.0)
        if own:
            act(eb.ap(), cum.ap(), AF.Exp, ["cum"], ["eb"], scale=-1.0 / 16.0)
        act(decs[:, h * 32 + G * 4:h * 32 + G * 4 + 4], cum[:, 127:512:128], AF.Exp, ["cum"], [("dec", h, G)],
            scale=-1.0 / 16.0)
        tt("dve", kTt[s].ap(), bank(bk), enb.ap(), ALU.mult, pk(bk) + ["enb"], [("kTt", s)])
        PS.release(bk)
        yield
        if own:
            bq = PS.alloc()
            mm_group(bank(bq), [(W[:, k, 0:128], hT(k, t0, 512)) for k in range(8)], [("wA", st, "q"), hk], pk(bq))
            stt(qTt[s].ap(), bank(bq), float(128.0 ** -0.5), eb.ap(), ALU.mult, ALU.mult, pk(bq) + ["eb"], [("qTt", s)])
            PS.release(bq)
            for c in range(2):
                br = PS.alloc()
                mm_group(bank(br), [(W[:, k, 512 + c * 128:640 + c * 128], hT(k, t0, 512)) for k in range(8)],
                         [("wA", st, "r%d" % c), hk], pk(br))
                act(silr[s][:, c * 512:(c + 1) * 512], bank(br), AF.Silu, pk(br), [("silr", s, c)])
                PS.release(br)
                if c == 0:
                    yield
        else:
            yield
        yield
        for jj in range(2):
            bv = PS.alloc()
            for j in (2 * jj, 2 * jj + 1):
                mm_group(bank(bv)[:, (j % 2) * 256:(j % 2 + 1) * 256],
                         [(hT(k, t0 + j * 128, 128), W[:, k, 256:512]) for k in range(8)],
                         [("wA", st, "v0"), ("wA", st, "v1"), hk], pk(bv))
            act(vtok[s][:, jj * 512:(jj + 1) * 512], bank(bv), AF.Copy, pk(bv), [("vtok", s, jj)])
            PS.release(bv)
            if jj == 0:
                yield

    def front(h, G):
        s, own, t0 = G % 2, G >= 4, G * 512
        if G == 0:
            R.add("pool", lambda e: e.memset(Sbf[0].ap(), 0.0), [], [("Sbf", 0)])
        bt = PS.alloc()

        def ftr(e):
            ins = None
            for j in range(4):
                ins = e.transpose(bankbf(bt)[:, j * 128:(j + 1) * 128], kTt[s][:, j * 128:(j + 1) * 128], ident_b)
            return ins
        R.add("pe", ftr, [("kTt", s)], pk(bt))
        cp("dve", ktok[s].ap(), bankbf(bt)[:, 0:512], pk(bt), [("ktok", s)])
        PS.release(bt)
        if own:
            ba_ = PS.alloc()
            for j in range(4):
                js = slice(j * 128, (j + 1) * 128)
                mm_group(bank(ba_)[:, js], [(kTt[s][:, js], qTt[s][:, js])], [("kTt", s), ("qTt", s)], pk(ba_))
            for j in range(4):
                js = slice(j * 128, (j + 1) * 128)
                tt("dve", attn_sb[j].ap(), bank(ba_)[:, js], triU32, ALU.mult, pk(ba_), [("attn", j)])
            PS.release(ba_)
        bkv = PS.alloc(2)
        for j in range(4):
            js = slice(j * 128, (j + 1) * 128)
            mm_group(bank(bkv, 2)[:, j * 256:(j + 1) * 256], [(ktok[s][:, js], vtok[s][:, j * 256:(j + 1) * 256])],
                     [("ktok", s), ("vtok", s, j // 2)], pk(bkv + j // 2))
        for j in range(4):
            n = G * 4 + j
            kvs = bank(bkv, 2)[:, j * 256:(j + 1) * 256]
            if n == 0:
                cp("dve", Tst[0].ap(), kvs, pk(bkv + j // 2), [("Tst", 0)])
            else:
                stt(Tst[n % 2].ap(), Tst[(n - 1) % 2].ap(), decs[:, h * 32 + n - 1:h * 32 + n], kvs,
                    ALU.mult, ALU.add, [("Tst", (n - 1) % 2), ("dec", h, (n - 1) // 4)] + pk(bkv + j // 2), [("Tst", n % 2)])
            if n < 31:
                ts("pool", Sbf[(n + 1) % 5].ap(), Tst[n % 2].ap(), decs[:, h * 32 + n:h * 32 + n + 1], ALU.mult,
                   [("Tst", n % 2), ("dec", h, G)], [("Sbf", (n + 1) % 5)], s2=1.0, op1=ALU.mult)
        PS.release(bkv, 2)

    bo_of = {}

    def mid(h, G, gen):
        s, own = G % 2, G >= 4
        if own:
            bo_of[(h, G)] = PS.alloc(2)
            bo = bo_of[(h, G)]
        for j in range(4):
            n = G * 4 + j
            js = slice(j * 128, (j + 1) * 128)
            WL.tick()
            if gen is not None:
                next(gen, None)
            if own:
                for c in range(2):
                    mm_group(bank(bo + c)[:, js],
                             [(vtok[s][:, j * 256 + c * 128:j * 256 + (c + 1) * 128], attn_sb[j].ap()),
                              (Sbf[n % 5][:, c * 128:(c + 1) * 128], qTt[s][:, js])],
                             [("vtok", s, j // 2), ("attn", j), ("Sbf", n % 5), ("qTt", s)], pk(bo + c))
        if gen is not None:
            for _ in gen:
                pass

    def tail(h, G):
        s, own = G % 2, G >= 4
        if not own:
            return
        bo = bo_of.pop((h, G))
        act(osq.ap(), bank(bo, 2), AF.Square, pk(bo, 2), ["osq"])
        bs = PS.alloc()
        mm_group(bank(bs), [(ones_b.ap(), osq[:, 0:512]), (ones_b.ap(), osq[:, 512:1024])], ["osq"], pk(bs))
        act(rso.ap(), bank(bs), AF.Ln, pk(bs), ["rso"], bias=epsc[:, 0:1], scale=1.0 / 256.0)
        PS.release(bs)
        act(rso.ap(), rso.ap(), AF.Exp, ["rso"], ["rso"], scale=-0.5)
        for c in range(2):
            cs = slice(c * 512, (c + 1) * 512)
            tt("pool", rs[:, cs], silr[s][:, cs], rso.ap(), ALU.mult, [("silr", s, c), "rso"], [("rs", c)])
            stt(oaT[:, 2 * h + c, (G - 4) * 512:(G - 3) * 512], bank(bo + c), go[:, c:c + 1], rs[:, cs],
                ALU.mult, ALU.mult, pk(bo + c) + [("rs", c)], [("oaT", 2 * h + c, G - 4)])
        PS.release(bo, 2)

    order = [(h, G) for h in range(4) for G in range(8)]
    for _ in stage1(0, 0):
        pass
    front(0, 0)
    for idx, (h, G) in enumerate(order):
        if G == 0 and h + 1 < 4:
            load_head(h + 1)
        if h == 3 and G == 1:
            war = [("wA", 0, nm_) for nm_ in ("q", "k", "v0", "v1", "r0", "r1")]
            load_block(wC0[:, 0:8, :], w_in_v[:, :, OFF_GA:OFF_GA + 128], ("wC", 0, 0), war)
            load_block(wC0[:, 8:16, :], w_in_v[:, :, OFF_GD:OFF_GD + 128], ("wC", 0, 1), war)
            load_block(wC0[:, 16:24, :], w_go_v[:, :, 0:128], ("wC", 0, 2), war)
            load_block(wC0[:, 24:28, :], w_do_v[:, :, 0:128], ("wC", 0, 3), war)
        if G == 7:
            WL.flush()
        gen = stage1(*order[idx + 1]) if idx + 1 < len(order) else None
        mid(h, G, gen)
        if idx + 1 < len(order):
            front(*order[idx + 1])
        tail(h, G)
    WL.flush()
    if debug == "pA":
        dump("oaT", oaT, [("oaT", c, g) for c in range(8) for g in range(4)])
        return finish()

    R.barrier()
    A.reset(ARENA_A)
    yT = nc.alloc_sbuf_tensor_at("yT_alias", [128, 8, TOWN], BF16, offset=HTP_OFF)
    A.reset(WA0_OFF + 28 * 128 * 2)
    wC = [wC0, A.alloc("wC", [128, 28, 128], BF16)]
    sga = mk("sga", [128, 512], F32)
    sgd = mk("sgd", [128, 512], F32)
    u1 = mk("u1", [128, 512], F32)
    u2 = mk("u2", [128, 512], F32)
    wo = mk("wo", [128, 8, 128], BF16, 3)
    xr = mk("xr", [128, TOWN], F32, 3)
    ob = mk("ob", [128, 512], F32)
    it = 0

    def pre_c(c):
        load_block(wo[c % 3].ap(), w_o_v[:, :, c * 128:(c + 1) * 128], ("wo", c % 3))
        dma(xr[c % 3].ap(), xT[c * 128:(c + 1) * 128, TOWN:TALL], [], [("xr", c % 3)])

    def load_c(c):
        s = c % 2
        cs = slice(c * 128, (c + 1) * 128)
        load_block(wC[s][:, 0:8, :], w_in_v[:, :, OFF_GA + c * 128:OFF_GA + (c + 1) * 128], ("wC", s, 0))
        load_block(wC[s][:, 8:16, :], w_in_v[:, :, OFF_GD + c * 128:OFF_GD + (c + 1) * 128], ("wC", s, 1))
        load_block(wC[s][:, 16:24, :], w_go_v[:, :, cs], ("wC", s, 2))
        load_block(wC[s][:, 24:28, :], w_do_v[:, :, cs], ("wC", s, 3))

    for c in range(8):
        s = c % 2
        cs = slice(c * 128, (c + 1) * 128)
        WL.flush()
        if c + 1 < 8:
            load_c(c + 1)
        if c == 6:
            pre_c(0)
        if c == 7:
            pre_c(1)
        for G in range(4):
            WL.tick(2)
            gs = slice(G * 512, (G + 1) * 512)
            i2 = it % 2
            it += 1
            hk = hkey(TOWN + G * 512)
            bga = PS.alloc()
            mm_group(bank(bga), [(wC[s][:, k, :], hT(k, TOWN + G * 512, 512)) for k in range(8)], [("wC", s, 0), hk], pk(bga))
            bgd = PS.alloc()
            mm_group(bank(bgd), [(wC[s][:, 8 + k, :], hT(k, TOWN + G * 512, 512)) for k in range(8)], [("wC", s, 1), hk], pk(bgd))
            bya = PS.alloc()
            mm_group(bank(bya), [(wC[s][:, 16 + k, :], oaT[:, k, gs]) for k in range(8)],
                     [("wC", s, 2)] + [("oaT", k, G) for k in range(8)], pk(bya))
            byd = PS.alloc()
            mm_group(bank(byd), [(wC[s][:, 24 + k, :], odT[:, k, gs]) for k in range(4)],
                     [("wC", s, 3)] + [("odT", k, G) for k in range(4)], pk(byd))
            act(sga[i2].ap(), bank(bga), AF.Sigmoid, pk(bga), [("sga", i2)])
            act(sgd[i2].ap(), bank(bgd), AF.Sigmoid, pk(bgd), [("sgd", i2)])
            PS.release(bga)
            PS.release(bgd)
            tt("dve", u1[i2].ap(), bank(bya), sga[i2].ap(), ALU.mult, pk(bya) + [("sga", i2)], [("u1", i2)])
            tt("dve", u2[i2].ap(), bank(byd), sgd[i2].ap(), ALU.mult, pk(byd) + [("sgd", i2)], [("u2", i2)])
            PS.release(bya)
            PS.release(byd)
            tt("pool", yT[:, c, gs], u1[i2].ap(), u2[i2].ap(), ALU.add, [("u1", i2), ("u2", i2)], [("yT", c, G)])
    it = 0
    for c in range(8):
        s = c % 3
        cs = slice(c * 128, (c + 1) * 128)
        WL.flush()
        for G in range(4):
            WL.tick()
            gs = slice(G * 512, (G + 1) * 512)
            i2 = it % 2
            it += 1
            bo = PS.alloc()
            mm_group(bank(bo), [(wo[s][:, k, :], yT[:, k, gs]) for k in range(8)],
                     [("wo", s)] + [("yT", k, G) for k in range(8)], pk(bo))
            tt("dve", ob[i2].ap(), bank(bo), xr[s][:, gs], ALU.add, pk(bo) + [("xr", s)], [("ob", i2)])
            PS.release(bo)
            dma(outT[cs, gs], ob[i2].ap(), [("ob", i2)], [("out", c, G)], eng="act")
            out_keys.append(("out", c, G))
            if G == 0 and c + 2 < 8:
                pre_c(c + 2)
    return finish()


def make_consts():
    c = np.zeros((128, 648), np.float32)
    j = np.arange(128)[:, None]
    i = np.arange(128)[None, :]
    c[:, 0:128] = (j == i)
    c[:, 128:256] = (j <= i)
    c[:, 256:384] = (j >= i)
    rt = np.zeros((128, 128), np.float32)
    for m in range(64):
        rt[m + 64, m] = -1.0
        rt[m, m + 64] = 1.0
    c[:, 384:512] = rt
    c[:, 512:640] = (j <= i) * (-1.0 / 16.0)
    half = 64
    inv = (10000.0 ** (-(np.arange(half, dtype=np.float32)) / np.float32(half))).astype(np.float32)
    c[:, 640] = np.concatenate([inv, inv])
    return c


def prep_inputs(x, positions, norm_gain, w_in, gla_w_a2, gla_b_a, gla_out_gain,
                dil_q_gain, dil_k_gain, w_gla_out, w_dil_out, w_o):
    consts = make_consts()
    common = {
        "w_in": np.ascontiguousarray(w_in[0]),
        "w_a2": np.ascontiguousarray(gla_w_a2[0]),
        "b_a": np.ascontiguousarray(gla_b_a[0].reshape(1, 512)),
        "b_aT": np.ascontiguousarray(gla_b_a[0].reshape(4, 128).T),
        "gx": np.ascontiguousarray(norm_gain[0].reshape(8, 128).T),
        "go": np.ascontiguousarray(gla_out_gain[0].reshape(2, 128).T),
        "gqk": np.ascontiguousarray(np.stack([dil_q_gain[0], dil_k_gain[0]], axis=1)),
        "gqk_row": np.ascontiguousarray(np.concatenate([dil_q_gain[0], dil_k_gain[0]]).reshape(1, 256)),
        "w_gla_out": np.ascontiguousarray(w_gla_out[0]),
        "w_dil_out": np.ascontiguousarray(w_dil_out[0]),
        "w_o": np.ascontiguousarray(w_o[0]),
        "consts": consts,
    }
    in_maps = []
    for b in range(NB):
        xb = np.asarray(x[b], np.float32)
        pb = np.asarray(positions[b], np.int32)
        for half in range(2):
            xt = np.zeros((D, TALL), np.float32)
            pp = np.zeros((1, TALL), np.int32)
            if half == 0:
                xt[:, TOWN:] = xb[0:TOWN].T
                pp[0, TOWN:] = pb[0:TOWN]
            else:
                xt[:, :] = xb.T
                pp[0, :] = pb
            m = dict(common)
            m["xT"] = xt
            m["pos"] = pp
            m["flag"] = np.full((128, 1), float(half), np.float32)
            in_maps.append(m)
    return in_maps


_NC_CACHE = {}


def kernel(**inputs):
    inputs = {k: np.asarray(v) for k, v in inputs.items()}
    in_maps = prep_inputs(**inputs)
    if "nc" not in _NC_CACHE:
        _NC_CACHE["nc"] = build_program()[0]
    nc = _NC_CACHE["nc"]
    res = run_bass_kernel_spmd(nc, in_maps, core_ids=list(range(8)))
    out = np.zeros((NB, SEQ, D), np.float32)
    for b in range(NB):
        for half in range(2):
            o = np.asarray(res.results[b * 2 + half]["outT"], np.float32)
            out[b, half * TOWN:(half + 1) * TOWN, :] = o.T
    return out
```

```python
import math
import numpy as np
import concourse.bass as bass
import concourse.mybir as mybir
from concourse.bass_utils import run_bass_kernel_spmd

F32 = mybir.dt.float32
BF16 = mybir.dt.bfloat16
I32 = mybir.dt.int32
AF = mybir.ActivationFunctionType
ALU = mybir.AluOpType
AX = mybir.AxisListType

D = 1024
SEQ = 4096
NB = 4
TOWN = 2048
TALL = 4096
INW = 10256
EPS = 1e-6
OFF_QA, OFF_KA, OFF_VA, OFF_RA, OFF_ALR = 0, 512, 1024, 2048, 3072
OFF_QD, OFF_KD, OFF_VD, OFF_ZD, OFF_GA, OFF_GD = 3088, 4624, 6160, 7696, 8208, 9232
NDSEM = 24
TWO_PI = 2.0 * math.pi
CW1 = 6.28125
CW2 = TWO_PI - CW1
MAGIC = 12582912.0


class Rec:
    def __init__(self):
        self.ops = []
        self.lastw = {}
        self.readers = {}
        self.bar = set()

    def barrier(self):
        last = {}
        dmas = []
        for i, o in enumerate(self.ops):
            if o["dma"]:
                dmas.append(i)
            elif o["fn"] is not None:
                last[o["eng"]] = i
        self.bar = set(last.values()) | set(dmas[-NDSEM:])

    def add(self, eng, fn, r=(), w=(), dma=False):
        i = len(self.ops)
        deps = set(self.bar)
        for k in r:
            j = self.lastw.get(k)
            if j is not None:
                deps.add(j)
        for k in w:
            j = self.lastw.get(k)
            if j is not None:
                deps.add(j)
            for j in self.readers.get(k, ()):
                deps.add(j)
        wset = set(w)
        for k in w:
            self.lastw[k] = i
            self.readers[k] = []
        for k in r:
            if k not in wset:
                self.readers.setdefault(k, []).append(i)
        self.ops.append(dict(eng=eng, fn=fn, deps=deps, dma=dma, sig=False))
        return i

    def emit(self, nc):
        ops = self.ops
        sems = {e: nc.alloc_semaphore("sem_" + e) for e in ("pe", "act", "dve", "pool")}
        dsems = [nc.alloc_semaphore("dsem%d" % i) for i in range(NDSEM)]
        for o in ops:
            for j in o["deps"]:
                pj = ops[j]
                if pj["dma"]:
                    continue
                if pj["eng"] == "pe" and o["eng"] == "pe" and not o["dma"]:
                    continue
                pj["sig"] = True
        cnt = {e: 0 for e in sems}
        nd = 0
        for o in ops:
            if o["dma"]:
                o["dsem"] = nd % NDSEM
                o["dval"] = 16 * (nd // NDSEM + 1)
                nd += 1
            elif o["sig"]:
                cnt[o["eng"]] += 1
                o["sval"] = cnt[o["eng"]]

        def make(engname):
            def body(e):
                waited = {}
                for o in ops:
                    if o["eng"] != engname:
                        continue
                    need = {}
                    for j in o["deps"]:
                        pj = ops[j]
                        if pj["dma"]:
                            key = ("d", pj["dsem"])
                            val = pj["dval"]
                        else:
                            if pj["eng"] == "pe" and engname == "pe" and not o["dma"]:
                                continue
                            key = ("e", pj["eng"])
                            val = pj["sval"]
                        if need.get(key, 0) < val:
                            need[key] = val
                    if o["dma"] and o["dval"] > 16:
                        key = ("d", o["dsem"])
                        if need.get(key, 0) < o["dval"] - 16:
                            need[key] = o["dval"] - 16
                    for key, val in need.items():
                        if waited.get(key, 0) >= val:
                            continue
                        waited[key] = val
                        sem = dsems[key[1]] if key[0] == "d" else sems[key[1]]
                        e.wait_ge(sem, val)
                    ins = o["fn"](e) if o["fn"] is not None else None
                    if o["dma"]:
                        ins.then_inc(dsems[o["dsem"]], 16)
                    elif o["sig"]:
                        ins.then_inc(sems[o["eng"]], 1)

            return body

        with nc.Block() as block:
            block.tensor(make("pe"))
            block.scalar(make("act"))
            block.vector(make("dve"))
            block.gpsimd(make("pool"))
            block.sync(make("sp"))


class Arena:
    def __init__(self, nc, base, top):
        self.nc, self.base, self.top, self.cur = nc, base, top, base
        self.n = 0

    def alloc(self, name, shape, dtype):
        nbytes = int(np.prod(shape[1:])) * (4 if dtype in (F32, I32) else 2)
        nbytes = (nbytes + 31) // 32 * 32
        off = self.cur
        self.cur += nbytes
        assert self.cur <= self.top, ("SBUF arena overflow", name, self.cur, self.top)
        self.n += 1
        return self.nc.alloc_sbuf_tensor_at("%s_%d" % (name, self.n), list(shape), dtype, offset=off)

    def mark(self):
        return self.cur

    def reset(self, m):
        self.cur = m


class PsumPool:
    def __init__(self, ps):
        self.ps = ps
        self.free = list(range(8))

    def alloc(self, n=1):
        if n == 1:
            assert self.free, "out of PSUM banks"
            return self.free.pop(0)
        for b in list(self.free):
            if all((b + i) in self.free for i in range(n)):
                for i in range(n):
                    self.free.remove(b + i)
                return b
        raise AssertionError("out of adjacent PSUM banks")

    def release(self, b, n=1):
        for i in range(n):
            self.free.append(b + i)


def build_program(debug=None):
    nc = bass.Bass("TRN2", target_bir_lowering=False)
    R = Rec()

    def dram(name, shape, dt=F32, kind="ExternalInput"):
        return nc.dram_tensor(name, list(shape), dt, kind=kind).ap()

    xT = dram("xT", [D, TALL])
    pos = dram("pos", [1, TALL], I32)
    flag_d = dram("flag", [128, 1])
    w_in = dram("w_in", [D, INW])
    w_a2_d = dram("w_a2", [16, 512])
    b_a_d = dram("b_a", [1, 512])
    b_aT_d = dram("b_aT", [128, 4])
    gx_d = dram("gx", [128, 8])
    go_d = dram("go", [128, 2])
    gqk_d = dram("gqk", [128, 2])
    gqk_row_d = dram("gqk_row", [1, 256])
    w_go_d = dram("w_gla_out", [1024, 1024])
    w_do_d = dram("w_dil_out", [512, 1024])
    w_o_d = dram("w_o", [1024, 1024])
    consts_d = dram("consts", [128, 648])
    outT = dram("outT", [D, TOWN], F32, kind="ExternalOutput")
    dbg_out = {}

    A = Arena(nc, 16512, 229344)
    ps_t = nc.alloc_psum_tensor("ps", [128, 4096], F32)
    PS = PsumPool(ps_t)

    def bank(b, n=1):
        return ps_t[:, b * 512:(b + n) * 512]

    def bankbf(b):
        return ps_t[:, b * 512:(b + 1) * 512].bitcast(BF16)

    def pk(b, n=1):
        return [("ps", b + i) for i in range(n)]

    hTo = A.alloc("hTo", [128, 8, TOWN], BF16)
    HTP_OFF = A.mark()
    hTp = A.alloc("hTp", [128, 8, TOWN], BF16)
    cst32 = A.alloc("cst32", [128, 648], F32)
    cstb = A.alloc("cstb", [128, 512], BF16)
    ones_b = A.alloc("ones_b", [128, 128], BF16)
    maskP = A.alloc("maskP", [128, 128], BF16)
    gx = A.alloc("gx", [128, 8], F32)
    go = A.alloc("go", [128, 2], F32)
    gqk = A.alloc("gqk", [128, 2], F32)
    flag = A.alloc("flag", [128, 1], F32)
    negC = A.alloc("negC", [128, 2], F32)
    ones32 = A.alloc("ones32", [1, 128], F32)
    wa2s = A.alloc("wa2s", [16, 512], F32)
    gqkrow = A.alloc("gqkrow", [1, 256], F32)
    mx = A.alloc("mx", [1, 4], F32)
    epsc = A.alloc("epsc", [128, 1], F32)
    posi = A.alloc("posi", [128, 512], I32)
    scanm = A.alloc("scanm", [128, 512], F32)
    onec = A.alloc("onec", [128, 1], F32)
    M_PP = A.alloc("M_PP", [128, 512], BF16)
    M_PL = A.alloc("M_PL", [128, 512], BF16)
    M_LL = A.alloc("M_LL", [128, 512], BF16)
    odT = A.alloc("odT", [128, 4, TOWN], BF16)
    ARENA_B = A.mark()
    oaT = A.alloc("oaT", [128, 8, TOWN], BF16)
    ARENA_A = A.mark()
    A.reset(ARENA_B)
    cosT = A.alloc("cosT", [128, TALL], BF16)
    sinT = A.alloc("sinT", [128, TALL], BF16)
    ARENA_B2 = A.mark()

    ident32 = cst32[:, 0:128]
    triU32 = cst32[:, 128:256]
    triL32 = cst32[:, 256:384]
    triN32 = cst32[:, 512:640]
    invf = cst32[:, 640:641]
    ident_b = cstb[:, 0:128]
    triU_b = cstb[:, 128:256]
    triL_b = cstb[:, 256:384]
    RT_b = cstb[:, 384:512]

    def hT(k, t0, n, step=1):
        last = t0 + (n - 1) * step
        if t0 >= TOWN:
            a = t0 - TOWN
            return hTo[:, k, a:a + (n - 1) * step + 1:step] if step > 1 else hTo[:, k, a:a + n]
        assert last < TOWN
        return hTp[:, k, t0:t0 + (n - 1) * step + 1:step] if step > 1 else hTp[:, k, t0:t0 + n]

    def hkey(t0):
        return ("hT", t0 // 512)

    def dma(out, in_, r, w, eng="sp"):
        return R.add(eng, lambda e: e.dma_start(out=out, in_=in_), r, w, dma=True)

    def mm_group(out, pairs, r, w):
        n = len(pairs)

        def fn(e):
            ins = None
            for i, (l, rr) in enumerate(pairs):
                ins = e.matmul(out, l, rr, start=(i == 0), stop=(i == n - 1))
            return ins

        return R.add("pe", fn, r, w)

    def act(out, in_, func, r, w, bias=None, scale=None):
        kw = {}
        if bias is not None:
            kw["bias"] = bias
        if scale is not None:
            kw["scale"] = scale
        return R.add("act", lambda e: e.activation(out, in_, func, **kw), r, w)

    def tt(eng, out, in0, in1, op, r, w):
        return R.add(eng, lambda e: e.tensor_tensor(out, in0, in1, op), r, w)

    def ts(eng, out, in0, s1, op0, r, w, s2=None, op1=None):
        if op1 is None:
            return R.add(eng, lambda e: e.tensor_scalar(out, in0, s1, None, op0), r, w)
        return R.add(eng, lambda e: e.tensor_scalar(out, in0, s1, s2, op0, op1), r, w)

    def stt(out, in0, scalar, in1, op0, op1, r, w):
        return R.add("dve", lambda e: e.scalar_tensor_tensor(out, in0, scalar, in1, op0, op1), r, w)

    def cp(eng, out, in_, r, w):
        if eng == "act":
            return R.add(eng, lambda e: e.activation(out, in_, AF.Copy), r, w)
        return R.add(eng, lambda e: e.tensor_copy(out, in_), r, w)

    dma(cst32[:, :], consts_d[:, :], [], ["cst32"])
    dma(gx[:, :], gx_d[:, :], [], ["gx"])
    dma(go[:, :], go_d[:, :], [], ["go"])
    dma(gqk[:, :], gqk_d[:, :], [], ["gqk"])
    dma(flag[:, :], flag_d[:, :], [], ["flag"])
    dma(wa2s[:, :], w_a2_d[:, :], [], ["wa2s"])
    dma(gqkrow[:, :], gqk_row_d[:, :], [], ["gqkrow"])
    cp("dve", cstb[:, :], cst32[:, 0:512], ["cst32"], ["cstb"])
    R.add("dve", lambda e: e.memset(ones_b[:, :], 1.0), [], ["ones_b"])
    R.add("dve", lambda e: e.memset(ones32[:, :], 1.0), [], ["ones32"])
    R.add("dve", lambda e: e.memset(epsc[:, :], EPS), [], ["epsc"])
    R.add("dve", lambda e: e.memset(scanm[:, :], 1.0), [], ["scanm"])
    for j in range(4):
        R.add("dve", lambda e, j=j: e.memset(scanm[:, j * 128:j * 128 + 1], 0.0), ["scanm"], ["scanm"])
    R.add("dve", lambda e: e.memset(onec[:, :], 1.0), [], ["onec"])
    ts("dve", maskP[:, :], triL32, flag[:, 0:1], ALU.mult, ["cst32", "flag"], ["maskP"])
    for (M, first, second) in ((M_PP, "P", "P"), (M_PL, "P", "L"), (M_LL, "L", "L")):
        for u, kind in enumerate((first, second)):
            if kind == "P":
                ts("dve", M[:, u * 256:u * 256 + 128], triL32, flag[:, 0:1], ALU.mult, ["cst32", "flag"], ["masks"])
            else:
                cp("dve", M[:, u * 256:u * 256 + 128], triL32, ["cst32"], ["masks"])
            cp("dve", M[:, u * 256 + 128:u * 256 + 256], triU32, ["cst32"], ["masks"])
    R.add("dve", lambda e: e.tensor_reduce(mx[:, 0:1], gqkrow[:, 0:128], AX.X, ALU.max,
                                           apply_absolute_value=True), ["gqkrow"], ["mx0"])
    R.add("dve", lambda e: e.tensor_reduce(mx[:, 1:2], gqkrow[:, 128:256], AX.X, ALU.max,
                                           apply_absolute_value=True), ["gqkrow"], ["mx1"])
    ts("dve", mx[:, 2:3], mx[:, 0:1], mx[:, 1:2], ALU.mult, ["mx0", "mx1"], ["mx2"],
       s2=-math.sqrt(128.0), op1=ALU.mult)
    cp("dve", mx[:, 3:4], mx[:, 2:3], ["mx2"], ["mx3"])
    b0 = PS.alloc()
    mm_group(bank(b0)[:, 0:2], [(ones32[0:1, 0:128], mx[0:1, 2:4])], ["ones32", "mx2", "mx3"], pk(b0))
    cp("dve", negC[:, :], bank(b0)[:, 0:2], pk(b0), ["negC"])
    PS.release(b0)


    out_keys = []

    def dump(nm, t, keys):
        dd = dram("dbg_" + nm, list(t.shape), t.dtype, kind="ExternalOutput")
        dbg_out[nm] = dd
        dma(dd, t.ap(), keys, [("dbg", nm)])

    def finish():
        R.add("sp", None, [("dbg", nm) for nm in dbg_out] + out_keys, [])
        R.emit(nc)
        return nc, dbg_out

    def recip(t_ap, key):
        R.add("dve", lambda e: e.reciprocal(t_ap, t_ap), [key], [key])

    def mk(name, shape, dt, n=2):
        return [A.alloc(name, shape, dt) for _ in range(n)]

    class Loader:
        def __init__(self, stg_):
            self.stg, self.q, self.nd, self.ncast = stg_, [], 0, 0

        def enqueue(self, dst, src, dkey, extra=()):
            self.q.append((dst, src, dkey, tuple(extra)))

        def tick(self, n=1):
            for _ in range(n):
                if self.nd < len(self.q) and self.nd - self.ncast < 2:
                    dst, src, dkey, extra = self.q[self.nd]
                    sl = self.nd % 2
                    K_, n_ = dst.shape[1], dst.shape[2]
                    dma(self.stg[sl][:, 0:K_, 0:n_], src, [], [("stg", sl)])
                    self.nd += 1
                elif self.ncast < self.nd:
                    dst, src, dkey, extra = self.q[self.ncast]
                    sl = self.ncast % 2
                    K_, n_ = dst.shape[1], dst.shape[2]
                    cp("act", dst, self.stg[sl][:, 0:K_, 0:n_], [("stg", sl)], [dkey] + list(extra))
                    self.ncast += 1

        def flush(self):
            while self.ncast < len(self.q):
                self.tick()

    xTv = xT.rearrange("(k p) t -> p k t", p=128)
    w_in_v = w_in.rearrange("(k p) c -> p k c", p=128)
    w_go_v = w_go_d.rearrange("(k p) c -> p k c", p=128)
    w_do_v = w_do_d.rearrange("(k p) c -> p k c", p=128)
    w_o_v = w_o_d.rearrange("(k p) c -> p k c", p=128)

    A.reset(ARENA_B2)
    stgB = mk("stg", [128, 8, 128], F32)
    wB0_p = A.alloc("wB", [128, 8, 384], BF16)
    wz_p = A.alloc("wz", [128, 8, 128], BF16)
    B_PRE_END = A.mark()
    WL0 = Loader(stgB)
    WL0.enqueue(wz_p.ap(), w_in_v[:, :, OFF_ZD:OFF_ZD + 128], "wz")
    for (d0, s0, nm) in ((0, OFF_QD, "q"), (128, OFF_KD, "k"), (256, OFF_VD, "v")):
        WL0.enqueue(wB0_p[:, :, d0:d0 + 128], w_in_v[:, :, s0:s0 + 128], ("wB", 0, nm))
    ang_r = mk("ang", [128, 512], F32)
    kk_r = mk("kk", [128, 512], F32)
    xs_r = mk("xs", [128, 8, 512], F32, 3)
    sq_r = mk("sq", [128, 8, 512], BF16)
    rb_r = mk("rb", [128, 512], F32)

    def tab1(qq):
        ang, kk = ang_r[qq % 2], kk_r[qq % 2]
        ka, kkk = ("ang", qq % 2), ("kk", qq % 2)
        cs_ = slice(qq * 512, (qq + 1) * 512)
        dma(posi[:, :], pos[:, cs_].partition_broadcast(128)[:, 0, :], [], ["posi"])
        ts("dve", ang[:, :], posi[:, :], invf, ALU.mult, ["posi"], [ka])
        ts("dve", kk[:, :], ang[:, :], 1.0 / TWO_PI, ALU.mult, [ka], [kkk], s2=MAGIC, op1=ALU.add)
        ts("dve", kk[:, :], kk[:, :], MAGIC, ALU.subtract, [kkk], [kkk])
        stt(ang[:, :], kk[:, :], -CW1, ang[:, :], ALU.mult, ALU.add, [kkk, ka], [ka])
        stt(ang[:, :], kk[:, :], -CW2, ang[:, :], ALU.mult, ALU.add, [kkk, ka], [ka])
        ts("dve", ang[:, :], ang[:, :], math.pi, ALU.min, [ka], [ka], s2=-math.pi, op1=ALU.max)

    def tab2(qq):
        ang, kk = ang_r[qq % 2], kk_r[qq % 2]
        ka, kkk = ("ang", qq % 2), ("kk", qq % 2)
        cs_ = slice(qq * 512, (qq + 1) * 512)
        act(sinT[:, cs_], ang[:, :], AF.Sin, [ka], [("sinT", qq)])
        act(kk[:, :], ang[:, :], AF.Sin, [ka], [kkk], scale=0.5)

    def tab3(qq):
        kk = kk_r[qq % 2]
        kkk = ("kk", qq % 2)
        cs_ = slice(qq * 512, (qq + 1) * 512)
        tt("dve", kk[:, :], kk[:, :], kk[:, :], ALU.mult, [kkk], [kkk])
        ts("dve", cosT[:, cs_], kk[:, :], -2.0, ALU.mult, [kkk], [("cosT", qq)], s2=1.0, op1=ALU.add)

    def p0_x1(G):
        s, s3 = G % 2, G % 3
        xs, sq = xs_r[s3], sq_r[s]
        if G >= 2:
            dma(xs.ap(), xTv[:, :, G * 512:(G + 1) * 512], [("xs", (G - 2) % 3)], [("xs", s3)])
        act(sq.ap(), xs.ap(), AF.Square, [("xs", s3)], [("sq", s)])

    def p0_x2(G):
        s = G % 2
        sq, rb = sq_r[s], rb_r[s]
        b = PS.alloc()
        mm_group(bank(b), [(ones_b.ap(), sq[:, k, :]) for k in range(8)], [("sq", s)], pk(b))
        act(rb.ap(), bank(b), AF.Ln, pk(b), [("rb", s)], bias=epsc[:, 0:1], scale=1.0 / D)
        PS.release(b)
        act(rb.ap(), rb.ap(), AF.Exp, [("rb", s)], [("rb", s)], scale=-0.5)

    xg_r = mk("xg", [128, 512], F32)

    def p0_y(G):
        s, s3 = G % 2, G % 3
        xs, rb = xs_r[s3], rb_r[s]
        for k in range(6):
            stt(hT(k, G * 512, 512), xs[:, k, :], gx[:, k:k + 1], rb.ap(), ALU.mult, ALU.mult,
                [("xs", s3), ("rb", s)], [hkey(G * 512)])
        for k in (6, 7):
            xg = xg_r[k % 2]
            act(xg.ap(), xs[:, k, :], AF.Copy, [("xs", s3)], [("xg", k % 2)], scale=gx[:, k:k + 1])
            tt("pool", hT(k, G * 512, 512), xg.ap(), rb.ap(), ALU.mult, [("xg", k % 2), ("rb", s)], [hkey(G * 512)])

    dma(xs_r[0].ap(), xTv[:, :, 0:512], [], [("xs", 0)])
    dma(xs_r[1].ap(), xTv[:, :, 512:1024], [("xs", 0)], [("xs", 1)])
    R.barrier()
    tab1(0)
    p0_x1(0)
    p0_x2(0)
    tab1(1)
    for G in range(8):
        if G + 1 < 8:
            p0_x1(G + 1)
        tab2(G)
        p0_y(G)
        if G + 1 < 8:
            p0_x2(G + 1)
        if G == 1:
            WL0.tick(2)
        if G == 6:
            WL0.tick(4)
        tab3(G)
        if G + 2 < 8:
            tab1(G + 2)
    WL0.flush()
    if debug == "p0":
        dump("sinT", sinT, [("sinT", q) for q in range(8)])
        dump("cosT", cosT, [("cosT", q) for q in range(8)])
        dump("hTo", hTo, [hkey(t) for t in range(0, TALL, 512)])
        dump("hTp", hTp, [hkey(t) for t in range(0, TALL, 512)])
        return finish()

    R.barrier()
    A.reset(B_PRE_END)
    stg = stgB
    WL = Loader(stg)
    load_block = WL.enqueue

    wB0 = wB0_p
    wz = wz_p
    wB = [wB0, A.alloc("wB", [128, 8, 384], BF16)]
    qr = A.alloc("qr", [128, TOWN], BF16)
    kr = A.alloc("kr", [128, TALL], BF16)
    vt = A.alloc("vt", [128, 32 * 128], BF16)
    Oacc = A.alloc("Oacc", [128, TOWN], F32)
    Lacc = A.alloc("Lacc", [128, TOWN], F32)
    silz_r = mk("silz", [128, TOWN], BF16)
    nsq = mk("nsq", [128, 512], BF16, 4)
    nrs = mk("nrs", [128, 512], F32)
    kn = mk("kn", [128, 512], BF16, 3)
    t1 = mk("t1", [128, 512], F32)
    t2 = mk("t2", [128, 512], F32)
    pT = mk("pT", [128, 512], BF16, 6)

    def load_B(hd, g, st):
        n = g * 4 + hd
        for (d0, s0, nm) in ((0, OFF_QD + n * 128, "q"), (128, OFF_KD + n * 128, "k"), (256, OFF_VD + n * 128, "v")):
            load_block(wB[st][:, :, d0:d0 + 128], w_in_v[:, :, s0:s0 + 128], ("wB", st, nm))

    allh = [hkey(t) for t in range(0, TALL, 512)]
    wcnt = [0]
    pcnt = [0]

    def combine(hd):
        silz = silz_r[hd % 2]
        for G in range(4):
            gs = slice(G * 512, (G + 1) * 512)
            i2 = G % 2
            act(nrs[i2].ap(), Lacc[:, gs], AF.Ln, ["Lacc", ("nrs", i2)], [("nrs", i2)])
            act(nrs[i2].ap(), nrs[i2].ap(), AF.Exp, [("nrs", i2)], [("nrs", i2)], scale=-1.0)
            tt("dve", t1[i2].ap(), Oacc[:, gs], nrs[i2].ap(), ALU.mult, ["Oacc", ("nrs", i2)], [("t1", i2)])
            tt("dve", odT[:, hd, gs], t1[i2].ap(), silz[:, gs], ALU.mult, [("t1", i2), ("silz", hd % 2, G)], [("odT", hd, G)])

    for hd in range(4):
        silz = silz_r[hd % 2]
        WL.flush()
        for G in range(4):
            b = PS.alloc()
            mm_group(bank(b), [(wz[:, k, :], hT(k, TOWN + G * 512, 512)) for k in range(8)], ["wz", hkey(TOWN + G * 512)], pk(b))
            act(silz[:, G * 512:(G + 1) * 512], bank(b), AF.Silu, pk(b), [("silz", hd % 2, G)])
            PS.release(b)
        for g in range(3):
            st = wcnt[0] % 2
            wcnt[0] += 1
            W = wB[st]
            dil = 4 ** g
            Pn = 128 * dil
            nb = 16 // dil
            nxt = (hd, g + 1) if g < 2 else ((hd + 1, 0) if hd < 3 else None)
            WL.flush()
            if nxt is not None:
                load_B(nxt[0], nxt[1], wcnt[0] % 2)
            if g == 2 and hd < 3:
                load_block(wz.ap(), w_in_v[:, :, OFF_ZD + (hd + 1) * 128:OFF_ZD + (hd + 2) * 128], "wz")
            kp = [(TOWN - Pn, 128)] if Pn == 128 else [(TOWN - Pn + i * 512, 512) for i in range(Pn // 512)]
            kp += [(TOWN + i * 512, 512) for i in range(4)]
            pieces = [dict(t0=t0, n=n, dst=kr[:, t0:t0 + n], dkey=("kr", t0 // 128), c0=128, wk=("wB", st, "k"),
                           gcol=gqk[:, 1:2]) for (t0, n) in kp]
            pieces += [dict(t0=TOWN + i * 512, n=512, dst=qr[:, i * 512:(i + 1) * 512], dkey=("qr", i), c0=0,
                            wk=("wB", st, "q"), gcol=gqk[:, 0:1]) for i in range(4)]
            krkeys = [("kr", t // 128) for (t, n) in kp]
            qrkeys = [("qr", i) for i in range(4)]

            def st_a(p):
                p["id"] = pcnt[0]
                pcnt[0] += 1
                i3, n, t0 = p["id"] % 4, p["n"], p["t0"]
                p["bp"] = PS.alloc()
                mm_group(bank(p["bp"])[:, 0:n], [(W[:, k, p["c0"]:p["c0"] + 128], hT(k, t0, n)) for k in range(8)],
                         [p["wk"], hkey(t0)], pk(p["bp"]))
                act(nsq[i3][:, 0:n], bank(p["bp"])[:, 0:n], AF.Square, pk(p["bp"]), [("nsq", i3)])

            def st_b(p):
                i2, i3, ik, n, t0, bp = p["id"] % 2, p["id"] % 4, p["id"] % 3, p["n"], p["t0"], p["bp"]
                bs = PS.alloc()
                mm_group(bank(bs)[:, 0:n], [(ones_b.ap(), nsq[i3][:, 0:n])], [("nsq", i3)], pk(bs))
                act(nrs[i2][:, 0:n], bank(bs)[:, 0:n], AF.Ln, pk(bs), [("nrs", i2)], bias=epsc[:, 0:1], scale=1.0 / 128.0)
                PS.release(bs)
                act(nrs[i2][:, 0:n], nrs[i2][:, 0:n], AF.Exp, [("nrs", i2)], [("nrs", i2)], scale=-0.5)
                stt(kn[ik][:, 0:n], bank(bp)[:, 0:n], p["gcol"], nrs[i2][:, 0:n], ALU.mult, ALU.mult,
                    pk(bp) + [("nrs", i2)], [("kn", ik)])
                PS.release(bp)

            def st_c(p):
                i2, ik, n, t0 = p["id"] % 2, p["id"] % 3, p["n"], p["t0"]
                br = PS.alloc()
                mm_group(bank(br)[:, 0:n], [(RT_b, kn[ik][:, 0:n])], [("kn", ik)], pk(br))
                tt("pool", t1[i2][:, 0:n], kn[ik][:, 0:n], cosT[:, t0:t0 + n], ALU.mult, [("kn", ik)], [("t1", i2)])
                tt("dve", t2[i2][:, 0:n], bank(br)[:, 0:n], sinT[:, t0:t0 + n], ALU.mult, pk(br), [("t2", i2)])
                PS.release(br)
                tt("dve", p["dst"], t1[i2][:, 0:n], t2[i2][:, 0:n], ALU.add,
                   [("t1", i2), ("t2", i2)], [p["dkey"]])

            blocks = [(r, b) for r in range(dil) for b in range(-1, nb)]

            def v_chunk(i0):
                chunk = blocks[i0:i0 + 4]
                bv = PS.alloc()
                for qi, (r, b) in enumerate(chunk):
                    tau0 = TOWN + 128 * b * dil + r
                    mm_group(bank(bv)[:, qi * 128:(qi + 1) * 128],
                             [(hT(k, tau0, 128, dil), W[:, k, 256:384]) for k in range(8)],
                             [("wB", st, "v")] + allh, pk(bv))
                cp("act", vt[:, i0 * 128:(i0 + len(chunk)) * 128],
                   bank(bv)[:, 0:len(chunk) * 128], pk(bv), [("vt", i0 // 4)])
                PS.release(bv)

            v_list = list(range(0, len(blocks), 4))
            v_pos = [0]

            def v_some(n, keep=0):
                for _ in range(n):
                    if v_pos[0] < len(v_list) - keep:
                        v_chunk(v_list[v_pos[0]])
                        v_pos[0] += 1

            NP = len(pieces)
            for i in range(NP + 4):
                WL.tick(2)
                if i < NP:
                    st_a(pieces[i])
                else:
                    v_some(1, keep=1)
                if 0 <= i - 2 < NP:
                    st_b(pieces[i - 2])
                if 0 <= i - 4 < NP:
                    st_c(pieces[i - 4])
            v_some(len(v_list), keep=2)
            if g == 1:
                qblocks = [(r, b) for b in range(nb) for r in range(dil)]
            else:
                qblocks = [(r, b) for r in range(dil) for b in range(nb)]
            npair = len(qblocks) // 2
            sc = float(128.0 ** -0.5)
            state = {}

            def scores(i):
                bsT = PS.alloc()
                for u in range(2):
                    r, b = qblocks[2 * i + u]
                    q0 = 128 * b * dil + r
                    qs = qr[:, q0:q0 + 127 * dil + 1:dil]
                    tp = TOWN + 128 * (b - 1) * dil + r
                    tc = TOWN + 128 * b * dil + r
                    qk = [("qr", t // 512) for t in range(q0 - q0 % 512, q0 + 127 * dil + 1, 512)]
                    kk1 = [("kr", t0_ // 128) for (t0_, n_) in kp if t0_ < tp + 127 * dil + 1 and t0_ + n_ > tp]
                    kk2 = [("kr", t0_ // 128) for (t0_, n_) in kp if t0_ < tc + 127 * dil + 1 and t0_ + n_ > tc]
                    mm_group(bank(bsT)[:, u * 256:u * 256 + 128], [(kr[:, tp:tp + 127 * dil + 1:dil], qs)],
                             kk1 + qk, pk(bsT))
                    mm_group(bank(bsT)[:, u * 256 + 128:u * 256 + 256], [(kr[:, tc:tc + 127 * dil + 1:dil], qs)],
                             kk2 + qk, pk(bsT))
                p = pT[i % 6]
                act(p.ap(), bank(bsT), AF.Exp, pk(bsT), [("pT", i % 6)], bias=negC[:, 0:1], scale=sc)
                PS.release(bsT)
                f0 = qblocks[2 * i][1] == 0
                f1 = qblocks[2 * i + 1][1] == 0
                assert not (f1 and not f0)
                M = M_PP if (f0 and f1) else (M_PL if f0 else M_LL)
                tt("dve" if i % 4 else "pool", p.ap(), p.ap(), M.ap(), ALU.mult, [("pT", i % 6)], [("pT", i % 6)])

            def pv(i):
                p = pT[i % 6]
                for u in range(2):
                    j = 2 * i + u
                    r, b = qblocks[j]
                    qi = j % 4
                    if qi == 0:
                        state["bO"] = PS.alloc()
                        state["bL"] = PS.alloc()
                    bO, bL = state["bO"], state["bL"]
                    vp = (r * (nb + 1) + b) * 128
                    vc = (r * (nb + 1) + b + 1) * 128
                    mm_group(bank(bO)[:, qi * 128:(qi + 1) * 128],
                             [(vt[:, vp:vp + 128], p[:, u * 256:u * 256 + 128]),
                              (vt[:, vc:vc + 128], p[:, u * 256 + 128:u * 256 + 256])],
                             [("vt", (vp // 128) // 4), ("vt", (vc // 128) // 4), ("pT", i % 6)], pk(bO))
                    mm_group(bank(bL)[:, qi * 128:(qi + 1) * 128],
                             [(ones_b.ap(), p[:, u * 256:u * 256 + 128]), (ones_b.ap(), p[:, u * 256 + 128:u * 256 + 256])],
                             [("pT", i % 6)], pk(bL))
                    if qi == 3:
                        r0, b0_ = qblocks[j - 3]
                        if g == 0:
                            ov = Oacc[:, b0_ * 128:b0_ * 128 + 512]
                            lv = Lacc[:, b0_ * 128:b0_ * 128 + 512]
                            so, sl = bank(bO), bank(bL)
                        elif g == 1:
                            ov = Oacc[:, b0_ * 512:(b0_ + 1) * 512].rearrange("p (i r) -> p r i", r=4)
                            lv = Lacc[:, b0_ * 512:(b0_ + 1) * 512].rearrange("p (i r) -> p r i", r=4)
                            so = bank(bO).rearrange("p (r i) -> p r i", r=4)
                            sl = bank(bL).rearrange("p (r i) -> p r i", r=4)
                        else:
                            ov = Oacc.ap().rearrange("p (i r) -> p r i", r=16)[:, r0:r0 + 4, :]
                            lv = Lacc.ap().rearrange("p (i r) -> p r i", r=16)[:, r0:r0 + 4, :]
                            so = bank(bO).rearrange("p (r i) -> p r i", r=4)
                            sl = bank(bL).rearrange("p (r i) -> p r i", r=4)
                        if g == 0:
                            cp("dve", ov, so, pk(bO), ["Oacc"])
                            cp("act", lv, sl, pk(bL), ["Lacc"])
                        else:
                            tt("dve", ov, so, ov, ALU.add, pk(bO) + ["Oacc"], ["Oacc"])
                            tt("dve", lv, sl, lv, ALU.add, pk(bL) + ["Lacc"], ["Lacc"])
                        PS.release(bO)
                        PS.release(bL)

            LOOK = 4
            for i in range(min(LOOK, npair)):
                scores(i)
            v_some(2)
            for i in range(npair):
                if i + LOOK < npair:
                    scores(i + LOOK)
                pv(i)
                if i == 0 and g == 0 and hd >= 1:
                    combine(hd - 1)
    combine(3)
    if debug == "pB":
        dump("odT", odT, [("odT", c, g) for c in range(4) for g in range(4)])
        return finish()

    R.barrier()
    A.reset(ARENA_A)
    stg = mk("stg", [128, 8, 128], F32)
    WL = Loader(stg)
    load_block = WL.enqueue

    WA0_OFF = A.mark()
    wA = mk("wA", [128, 8, 768], BF16)
    wC0 = nc.alloc_sbuf_tensor_at("wC0_alias", [128, 28, 128], BF16, offset=WA0_OFF)
    alrT = A.alloc("alrT", [128, TALL], BF16)
    walr = A.alloc("walr", [128, 8, 128], BF16)
    wa2p = A.alloc("wa2p", [128, 512], BF16)
    nba = A.alloc("nba", [128, 4], F32)
    e32 = A.alloc("e32", [128, 512], F32)
    cum = A.alloc("cum", [128, 512], F32)
    eb = A.alloc("eb", [128, 512], F32)
    enb = A.alloc("enb", [128, 512], F32)
    decs = A.alloc("decs", [128, 4 * 32], F32)
    kTt = mk("kTt", [128, 512], BF16)
    qTt = mk("qTt", [128, 512], BF16)
    ktok = mk("ktok", [128, 512], BF16)
    vtok = mk("vtok", [128, 1024], BF16)
    silr = mk("silr", [128, 1024], BF16)
    attn_sb = mk("attn", [128, 128], BF16, 4)
    osq = A.alloc("osq", [128, 1024], BF16)
    rso = A.alloc("rso", [128, 512], F32)
    rs = A.alloc("rs", [128, 1024], F32)
    Tst = mk("Tst", [128, 256], F32)
    Sbf = mk("Sbf", [128, 256], BF16, 5)

    R.add("pool", lambda e: e.memset(walr.ap(), 0.0), [], ["walr"])
    R.add("pool", lambda e: e.memset(wa2p.ap(), 0.0), [], ["wa2p"])
    cp("dve", wa2p[0:16, :], wa2s[0:16, :], ["wa2p"], ["wa2p"])
    dma(nba.ap(), b_aT_d[:, :], [], ["nba"])
    ts("dve", nba.ap(), nba.ap(), -1.0, ALU.mult, ["nba"], ["nba"])
    def load_head(h):
        st = h % 2
        for (d0, s0, nm) in ((0, OFF_QA + h * 128, "q"), (128, OFF_KA + h * 128, "k"),
                             (256, OFF_VA + h * 256, "v0"), (384, OFF_VA + h * 256 + 128, "v1"),
                             (512, OFF_RA + h * 256, "r0"), (640, OFF_RA + h * 256 + 128, "r1")):
            load_block(wA[st][:, :, d0:d0 + 128], w_in_v[:, :, s0:s0 + 128], ("wA", st, nm))

    load_block(walr[:, :, 0:16], w_in_v[:, :, OFF_ALR:OFF_ALR + 16], "walr")
    load_head(0)
    WL.tick(3)
    for G in range(8):
        WL.tick(2)
        b = PS.alloc()
        mm_group(bank(b), [(walr[:, k, :], hT(k, G * 512, 512)) for k in range(8)],
                 ["walr", hkey(G * 512)], pk(b))
        act(alrT[:, G * 512:(G + 1) * 512], bank(b), AF.Copy, pk(b), [("alrT", G)])
        PS.release(b)
    WL.flush()

    def stage1(h, G):
        st, s, own, t0 = h % 2, G % 2, G >= 4, G * 512
        W = wA[st]
        hk = hkey(t0)
        bz = PS.alloc()
        mm_group(bank(bz), [(wa2p[:, h * 128:(h + 1) * 128], alrT[:, t0:t0 + 512])], [("alrT", G), "wa2p"], pk(bz))
        act(e32.ap(), bank(bz), AF.Exp, pk(bz) + ["nba"], ["e32"], scale=-1.0, bias=nba[:, h:h + 1])
        PS.release(bz)
        act(e32.ap(), e32.ap(), AF.Ln, ["e32"], ["e32"], bias=onec[:, 0:1])
        R.add("dve", lambda e: e.tensor_tensor_scan(cum.ap(), scanm.ap(), e32.ap(), 0.0, ALU.mult, ALU.add),
              ["e32"], ["cum"])
        bk = PS.alloc()
        mm_group(bank(bk), [(W[:, k, 128:256], hT(k, t0, 512)) for k in range(8)], [("wA", st, "k"), hk], pk(bk))
        act(enb.ap(), cum.ap(), AF.Exp, ["cum"], ["enb"], scale=1.0 / 16.0)
        if own:
            act(eb.ap(), cum.ap(), AF.Exp, ["cum"], ["eb"], scale=-1.0 / 16.0)
        act(decs[:, h * 32 + G * 4:h * 32 + G * 4 + 4], cum[:, 127:512:128], AF.Exp, ["cum"], [("dec", h, G)],
            scale=-1.0 / 16.0)
        tt("dve", kTt[s].ap(), bank(bk), enb.ap(), ALU.mult, pk(bk) + ["enb"], [("kTt", s)])
        PS.release(bk)
        yield
        if own:
            bq = PS.alloc()
            mm_group(bank(bq), [(W[:, k, 0:128], hT(k, t0, 512)) for k in range(8)], [("wA", st, "q"), hk], pk(bq))
            stt(qTt[s].ap(), bank(bq), float(128.0 ** -0.5), eb.ap(), ALU.mult, ALU.mult, pk(bq) + ["eb"], [("qTt", s)])
            PS.release(bq)
            for c in range(2):
                br = PS.alloc()
                mm_group(bank(br), [(W[:, k, 512 + c * 128:640 + c * 128], hT(k, t0, 512)) for k in range(8)],
                         [("wA", st, "r%d" % c), hk], pk(br))
                act(silr[s][:, c * 512:(c + 1) * 512], bank(br), AF.Silu, pk(br), [("silr", s, c)])
                PS.release(br)
                if c == 0:
                    yield
        else:
            yield
        yield
        for jj in range(2):
            bv = PS.alloc()
            for j in (2 * jj, 2 * jj + 1):
                mm_group(bank(bv)[:, (j % 2) * 256:(j % 2 + 1) * 256],
                         [(hT(k, t0 + j * 128, 128), W[:, k, 256:512]) for k in range(8)],
                         [("wA", st, "v0"), ("wA", st, "v1"), hk], pk(bv))
            act(vtok[s][:, jj * 512:(jj + 1) * 512], bank(bv), AF.Copy, pk(bv), [("vtok", s, jj)])
            PS.release(bv)
            if jj == 0:
                yield

    def front(h, G):
        s, own, t0 = G % 2, G >= 4, G * 512
        if G == 0:
            R.add("pool", lambda e: e.memset(Sbf[0].ap(), 0.0), [], [("Sbf", 0)])
        bt = PS.alloc()

        def ftr(e):
            ins = None
            for j in range(4):
                ins = e.transpose(bankbf(bt)[:, j * 128:(j + 1) * 128], kTt[s][:, j * 128:(j + 1) * 128], ident_b)
            return ins
        R.add("pe", ftr, [("kTt", s)], pk(bt))
        cp("dve", ktok[s].ap(), bankbf(bt)[:, 0:512], pk(bt), [("ktok", s)])
        PS.release(bt)
        if own:
            ba_ = PS.alloc()
            for j in range(4):
                js = slice(j * 128, (j + 1) * 128)
                mm_group(bank(ba_)[:, js], [(kTt[s][:, js], qTt[s][:, js])], [("kTt", s), ("qTt", s)], pk(ba_))
            for j in range(4):
                js = slice(j * 128, (j + 1) * 128)
                tt("dve", attn_sb[j].ap(), bank(ba_)[:, js], triU32, ALU.mult, pk(ba_), [("attn", j)])
            PS.release(ba_)
        bkv = PS.alloc(2)
        for j in range(4):
            js = slice(j * 128, (j + 1) * 128)
            mm_group(bank(bkv, 2)[:, j * 256:(j + 1) * 256], [(ktok[s][:, js], vtok[s][:, j * 256:(j + 1) * 256])],
                     [("ktok", s), ("vtok", s, j // 2)], pk(bkv + j // 2))
        for j in range(4):
            n = G * 4 + j
            kvs = bank(bkv, 2)[:, j * 256:(j + 1) * 256]
            if n == 0:
                cp("dve", Tst[0].ap(), kvs, pk(bkv + j // 2), [("Tst", 0)])
            else:
                stt(Tst[n % 2].ap(), Tst[(n - 1) % 2].ap(), decs[:, h * 32 + n - 1:h * 32 + n], kvs,
                    ALU.mult, ALU.add, [("Tst", (n - 1) % 2), ("dec", h, (n - 1) // 4)] + pk(bkv + j // 2), [("Tst", n % 2)])
            if n < 31:
                ts("pool", Sbf[(n + 1) % 5].ap(), Tst[n % 2].ap(), decs[:, h * 32 + n:h * 32 + n + 1], ALU.mult,
                   [("Tst", n % 2), ("dec", h, G)], [("Sbf", (n + 1) % 5)], s2=1.0, op1=ALU.mult)
        PS.release(bkv, 2)

    bo_of = {}

    def mid(h, G, gen):
        s, own = G % 2, G >= 4
        if own:
            bo_of[(h, G)] = PS.alloc(2)
            bo = bo_of[(h, G)]
        for j in range(4):
            n = G * 4 + j
            js = slice(j * 128, (j + 1) * 128)
            WL.tick()
            if gen is not None:
                next(gen, None)
            if own:
                for c in range(2):
                    mm_group(bank(bo + c)[:, js],
                             [(vtok[s][:, j * 256 + c * 128:j * 256 + (c + 1) * 128], attn_sb[j].ap()),
                              (Sbf[n % 5][:, c * 128:(c + 1) * 128], qTt[s][:, js])],
                             [("vtok", s, j // 2), ("attn", j), ("Sbf", n % 5), ("qTt", s)], pk(bo + c))
        if gen is not None:
            for _ in gen:
                pass

    def tail(h, G):
        s, own = G % 2, G >= 4
        if not own:
            return
        bo = bo_of.pop((h, G))
        act(osq.ap(), bank(bo, 2), AF.Square, pk(bo, 2), ["osq"])
        bs = PS.alloc()
        mm_group(bank(bs), [(ones_b.ap(), osq[:, 0:512]), (ones_b.ap(), osq[:, 512:1024])], ["osq"], pk(bs))
        act(rso.ap(), bank(bs), AF.Ln, pk(bs), ["rso"], bias=epsc[:, 0:1], scale=1.0 / 256.0)
        PS.release(bs)
        act(rso.ap(), rso.ap(), AF.Exp, ["rso"], ["rso"], scale=-0.5)
        for c in range(2):
            cs = slice(c * 512, (c + 1) * 512)
            tt("dve", rs[:, cs], silr[s][:, cs], rso.ap(), ALU.mult, [("silr", s, c), "rso"], [("rs", c)])
            stt(oaT[:, 2 * h + c, (G - 4) * 512:(G - 3) * 512], bank(bo + c), go[:, c:c + 1], rs[:, cs],
                ALU.mult, ALU.mult, pk(bo + c) + [("rs", c)], [("oaT", 2 * h + c, G - 4)])
        PS.release(bo, 2)

    order = [(h, G) for h in range(4) for G in range(8)]
    for _ in stage1(0, 0):
        pass
    front(0, 0)
    for idx, (h, G) in enumerate(order):
        if G == 0 and h + 1 < 4:
            load_head(h + 1)
        if h == 3 and G == 1:
            war = [("wA", 0, nm_) for nm_ in ("q", "k", "v0", "v1", "r0", "r1")]
            load_block(wC0[:, 0:8, :], w_in_v[:, :, OFF_GA:OFF_GA + 128], ("wC", 0, 0), war)
            load_block(wC0[:, 8:16, :], w_in_v[:, :, OFF_GD:OFF_GD + 128], ("wC", 0, 1), war)
            load_block(wC0[:, 16:24, :], w_go_v[:, :, 0:128], ("wC", 0, 2), war)
            load_block(wC0[:, 24:28, :], w_do_v[:, :, 0:128], ("wC", 0, 3), war)
        if G == 7:
            WL.flush()
        gen = stage1(*order[idx + 1]) if idx + 1 < len(order) else None
        mid(h, G, gen)
        if idx + 1 < len(order):
            front(*order[idx + 1])
        tail(h, G)
    WL.flush()
    if debug == "pA":
        dump("oaT", oaT, [("oaT", c, g) for c in range(8) for g in range(4)])
        return finish()

    R.barrier()
    A.reset(ARENA_A)
    yT = nc.alloc_sbuf_tensor_at("yT_alias", [128, 8, TOWN], BF16, offset=HTP_OFF)
    A.reset(WA0_OFF + 28 * 128 * 2)
    wC = [wC0, A.alloc("wC", [128, 28, 128], BF16)]
    sga = mk("sga", [128, 512], F32)
    sgd = mk("sgd", [128, 512], F32)
    u1 = mk("u1", [128, 512], F32)
    u2 = mk("u2", [128, 512], F32)
    wo = mk("wo", [128, 8, 128], BF16, 3)
    xr = mk("xr", [128, TOWN], F32, 3)
    ob = mk("ob", [128, 512], F32)
    it = 0

    def pre_c(c):
        load_block(wo[c % 3].ap(), w_o_v[:, :, c * 128:(c + 1) * 128], ("wo", c % 3))
        dma(xr[c % 3].ap(), xT[c * 128:(c + 1) * 128, TOWN:TALL], [], [("xr", c % 3)])

    def load_c(c):
        s = c % 2
        cs = slice(c * 128, (c + 1) * 128)
        load_block(wC[s][:, 0:8, :], w_in_v[:, :, OFF_GA + c * 128:OFF_GA + (c + 1) * 128], ("wC", s, 0))
        load_block(wC[s][:, 8:16, :], w_in_v[:, :, OFF_GD + c * 128:OFF_GD + (c + 1) * 128], ("wC", s, 1))
        load_block(wC[s][:, 16:24, :], w_go_v[:, :, cs], ("wC", s, 2))
        load_block(wC[s][:, 24:28, :], w_do_v[:, :, cs], ("wC", s, 3))

    for c in range(8):
        s = c % 2
        cs = slice(c * 128, (c + 1) * 128)
        WL.flush()
        if c + 1 < 8:
            load_c(c + 1)
        if c == 6:
            pre_c(0)
        if c == 7:
            pre_c(1)
        for G in range(4):
            WL.tick(2)
            gs = slice(G * 512, (G + 1) * 512)
            i2 = it % 2
            it += 1
            hk = hkey(TOWN + G * 512)
            bga = PS.alloc()
            mm_group(bank(bga), [(wC[s][:, k, :], hT(k, TOWN + G * 512, 512)) for k in range(8)], [("wC", s, 0), hk], pk(bga))
            bgd = PS.alloc()
            mm_group(bank(bgd), [(wC[s][:, 8 + k, :], hT(k, TOWN + G * 512, 512)) for k in range(8)], [("wC", s, 1), hk], pk(bgd))
            bya = PS.alloc()
            mm_group(bank(bya), [(wC[s][:, 16 + k, :], oaT[:, k, gs]) for k in range(8)],
                     [("wC", s, 2)] + [("oaT", k, G) for k in range(8)], pk(bya))
            byd = PS.alloc()
            mm_group(bank(byd), [(wC[s][:, 24 + k, :], odT[:, k, gs]) for k in range(4)],
                     [("wC", s, 3)] + [("odT", k, G) for k in range(4)], pk(byd))
            act(sga[i2].ap(), bank(bga), AF.Sigmoid, pk(bga), [("sga", i2)])
            act(sgd[i2].ap(), bank(bgd), AF.Sigmoid, pk(bgd), [("sgd", i2)])
            PS.release(bga)
            PS.release(bgd)
            tt("dve", u1[i2].ap(), bank(bya), sga[i2].ap(), ALU.mult, pk(bya) + [("sga", i2)], [("u1", i2)])
            tt("dve", u2[i2].ap(), bank(byd), sgd[i2].ap(), ALU.mult, pk(byd) + [("sgd", i2)], [("u2", i2)])
            PS.release(bya)
            PS.release(byd)
            tt("pool", yT[:, c, gs], u1[i2].ap(), u2[i2].ap(), ALU.add, [("u1", i2), ("u2", i2)], [("yT", c, G)])
    it = 0
    for c in range(8):
        s = c % 3
        cs = slice(c * 128, (c + 1) * 128)
        WL.flush()
        for G in range(4):
            WL.tick()
            gs = slice(G * 512, (G + 1) * 512)
            i2 = it % 2
            it += 1
            bo = PS.alloc()
            mm_group(bank(bo), [(wo[s][:, k, :], yT[:, k, gs]) for k in range(8)],
                     [("wo", s)] + [("yT", k, G) for k in range(8)], pk(bo))
            tt("dve", ob[i2].ap(), bank(bo), xr[s][:, gs], ALU.add, pk(bo) + [("xr", s)], [("ob", i2)])
            PS.release(bo)
            dma(outT[cs, gs], ob[i2].ap(), [("ob", i2)], [("out", c, G)], eng="act")
            out_keys.append(("out", c, G))
            if G == 0 and c + 2 < 8:
                pre_c(c + 2)
    return finish()


def make_consts():
    c = np.zeros((128, 648), np.float32)
    j = np.arange(128)[:, None]
    i = np.arange(128)[None, :]
    c[:, 0:128] = (j == i)
    c[:, 128:256] = (j <= i)
    c[:, 256:384] = (j >= i)
    rt = np.zeros((128, 128), np.float32)
    for m in range(64):
        rt[m + 64, m] = -1.0
        rt[m, m + 64] = 1.0
    c[:, 384:512] = rt
    c[:, 512:640] = (j <= i) * (-1.0 / 16.0)
    half = 64
    inv = (10000.0 ** (-(np.arange(half, dtype=np.float32)) / np.float32(half))).astype(np.float32)
    c[:, 640] = np.concatenate([inv, inv])
    return c


def prep_inputs(x, positions, norm_gain, w_in, gla_w_a2, gla_b_a, gla_out_gain,
                dil_q_gain, dil_k_gain, w_gla_out, w_dil_out, w_o):
    consts = make_consts()
    common = {
        "w_in": np.ascontiguousarray(w_in[0]),
        "w_a2": np.ascontiguousarray(gla_w_a2[0]),
        "b_a": np.ascontiguousarray(gla_b_a[0].reshape(1, 512)),
        "b_aT": np.ascontiguousarray(gla_b_a[0].reshape(4, 128).T),
        "gx": np.ascontiguousarray(norm_gain[0].reshape(8, 128).T),
        "go": np.ascontiguousarray(gla_out_gain[0].reshape(2, 128).T),
        "gqk": np.ascontiguousarray(np.stack([dil_q_gain[0], dil_k_gain[0]], axis=1)),
        "gqk_row": np.ascontiguousarray(np.concatenate([dil_q_gain[0], dil_k_gain[0]]).reshape(1, 256)),
        "w_gla_out": np.ascontiguousarray(w_gla_out[0]),
        "w_dil_out": np.ascontiguousarray(w_dil_out[0]),
        "w_o": np.ascontiguousarray(w_o[0]),
        "consts": consts,
    }
    in_maps = []
    for b in range(NB):
        xb = np.asarray(x[b], np.float32)
        pb = np.asarray(positions[b], np.int32)
        for half in range(2):
            xt = np.zeros((D, TALL), np.float32)
            pp = np.zeros((1, TALL), np.int32)
            if half == 0:
                xt[:, TOWN:] = xb[0:TOWN].T
                pp[0, TOWN:] = pb[0:TOWN]
            else:
                xt[:, :] = xb.T
                pp[0, :] = pb
            m = dict(common)
            m["xT"] = xt
            m["pos"] = pp
            m["flag"] = np.full((128, 1), float(half), np.float32)
            in_maps.append(m)
    return in_maps


_NC_CACHE = {}


def kernel(**inputs):
    inputs = {k: np.asarray(v) for k, v in inputs.items()}
    in_maps = prep_inputs(**inputs)
    if "nc" not in _NC_CACHE:
        _NC_CACHE["nc"] = build_program()[0]
    nc = _NC_CACHE["nc"]
    res = run_bass_kernel_spmd(nc, in_maps, core_ids=list(range(8)))
    out = np.zeros((NB, SEQ, D), np.float32)
    for b in range(NB):
        for half in range(2):
            o = np.asarray(res.results[b * 2 + half]["outT"], np.float32)
            out[b, half * TOWN:(half + 1) * TOWN, :] = o.T
    return out
```

```python
import math
import numpy as np
import concourse.bass as bass
import concourse.mybir as mybir
from concourse.bass_utils import run_bass_kernel_spmd

F32 = mybir.dt.float32
BF16 = mybir.dt.bfloat16
I32 = mybir.dt.int32
AF = mybir.ActivationFunctionType
ALU = mybir.AluOpType
AX = mybir.AxisListType

D = 1024
SEQ = 4096
NB = 4
TOWN = 2048
TALL = 4096
INW = 10256
EPS = 1e-6
OFF_QA, OFF_KA, OFF_VA, OFF_RA, OFF_ALR = 0, 512, 1024, 2048, 3072
OFF_QD, OFF_KD, OFF_VD, OFF_ZD, OFF_GA, OFF_GD = 3088, 4624, 6160, 7696, 8208, 9232
NDSEM = 24
TWO_PI = 2.0 * math.pi
CW1 = 6.28125
CW2 = TWO_PI - CW1
MAGIC = 12582912.0


class Rec:
    def __init__(self):
        self.ops = []
        self.lastw = {}
        self.readers = {}
        self.bar = set()

    def barrier(self):
        last = {}
        dmas = []
        for i, o in enumerate(self.ops):
            if o["dma"]:
                dmas.append(i)
            elif o["fn"] is not None:
                last[o["eng"]] = i
        self.bar = set(last.values()) | set(dmas[-NDSEM:])

    def add(self, eng, fn, r=(), w=(), dma=False):
        i = len(self.ops)
        deps = set(self.bar)
        for k in r:
            j = self.lastw.get(k)
            if j is not None:
                deps.add(j)
        for k in w:
            j = self.lastw.get(k)
            if j is not None:
                deps.add(j)
            for j in self.readers.get(k, ()):
                deps.add(j)
        wset = set(w)
        for k in w:
            self.lastw[k] = i
            self.readers[k] = []
        for k in r:
            if k not in wset:
                self.readers.setdefault(k, []).append(i)
        self.ops.append(dict(eng=eng, fn=fn, deps=deps, dma=dma, sig=False))
        return i

    def emit(self, nc):
        ops = self.ops
        sems = {e: nc.alloc_semaphore("sem_" + e) for e in ("pe", "act", "dve", "pool")}
        dsems = [nc.alloc_semaphore("dsem%d" % i) for i in range(NDSEM)]
        for o in ops:
            for j in o["deps"]:
                pj = ops[j]
                if pj["dma"]:
                    continue
                if pj["eng"] == "pe" and o["eng"] == "pe" and not o["dma"]:
                    continue
                pj["sig"] = True
        cnt = {e: 0 for e in sems}
        nd = 0
        for o in ops:
            if o["dma"]:
                o["dsem"] = nd % NDSEM
                o["dval"] = 16 * (nd // NDSEM + 1)
                nd += 1
            elif o["sig"]:
                cnt[o["eng"]] += 1
                o["sval"] = cnt[o["eng"]]

        def make(engname):
            def body(e):
                waited = {}
                for o in ops:
                    if o["eng"] != engname:
                        continue
                    need = {}
                    for j in o["deps"]:
                        pj = ops[j]
                        if pj["dma"]:
                            key = ("d", pj["dsem"])
                            val = pj["dval"]
                        else:
                            if pj["eng"] == "pe" and engname == "pe" and not o["dma"]:
                                continue
                            key = ("e", pj["eng"])
                            val = pj["sval"]
                        if need.get(key, 0) < val:
                            need[key] = val
                    if o["dma"] and o["dval"] > 16:
                        key = ("d", o["dsem"])
                        if need.get(key, 0) < o["dval"] - 16:
                            need[key] = o["dval"] - 16
                    for key, val in need.items():
                        if waited.get(key, 0) >= val:
                            continue
                        waited[key] = val
                        sem = dsems[key[1]] if key[0] == "d" else sems[key[1]]
                        e.wait_ge(sem, val)
                    ins = o["fn"](e) if o["fn"] is not None else None
                    if o["dma"]:
                        ins.then_inc(dsems[o["dsem"]], 16)
                    elif o["sig"]:
                        ins.then_inc(sems[o["eng"]], 1)

            return body

        with nc.Block() as block:
            block.tensor(make("pe"))
            block.scalar(make("act"))
            block.vector(make("dve"))
            block.gpsimd(make("pool"))
            block.sync(make("sp"))


class Arena:
    def __init__(self, nc, base, top):
        self.nc, self.base, self.top, self.cur = nc, base, top, base
        self.n = 0

    def alloc(self, name, shape, dtype):
        nbytes = int(np.prod(shape[1:])) * (4 if dtype in (F32, I32) else 2)
        nbytes = (nbytes + 31) // 32 * 32
        off = self.cur
        self.cur += nbytes
        assert self.cur <= self.top, ("SBUF arena overflow", name, self.cur, self.top)
        self.n += 1
        return self.nc.alloc_sbuf_tensor_at("%s_%d" % (name, self.n), list(shape), dtype, offset=off)

    def mark(self):
        return self.cur

    def reset(self, m):
        self.cur = m


class PsumPool:
    def __init__(self, ps):
        self.ps = ps
        self.free = list(range(8))

    def alloc(self, n=1):
        if n == 1:
            assert self.free, "out of PSUM banks"
            return self.free.pop(0)
        for b in list(self.free):
            if all((b + i) in self.free for i in range(n)):
                for i in range(n):
                    self.free.remove(b + i)
                return b
        raise AssertionError("out of adjacent PSUM banks")

    def release(self, b, n=1):
        for i in range(n):
            self.free.append(b + i)


def build_program(debug=None):
    nc = bass.Bass("TRN2", target_bir_lowering=False)
    R = Rec()

    def dram(name, shape, dt=F32, kind="ExternalInput"):
        return nc.dram_tensor(name, list(shape), dt, kind=kind).ap()

    xT = dram("xT", [D, TALL])
    pos = dram("pos", [1, TALL], I32)
    flag_d = dram("flag", [128, 1])
    w_in = dram("w_in", [D, INW])
    w_a2_d = dram("w_a2", [16, 512])
    b_a_d = dram("b_a", [1, 512])
    b_aT_d = dram("b_aT", [128, 4])
    gx_d = dram("gx", [128, 8])
    go_d = dram("go", [128, 2])
    gqk_d = dram("gqk", [128, 2])
    gqk_row_d = dram("gqk_row", [1, 256])
    w_go_d = dram("w_gla_out", [1024, 1024])
    w_do_d = dram("w_dil_out", [512, 1024])
    w_o_d = dram("w_o", [1024, 1024])
    consts_d = dram("consts", [128, 648])
    outT = dram("outT", [D, TOWN], F32, kind="ExternalOutput")
    dbg_out = {}

    A = Arena(nc, 16512, 229344)
    ps_t = nc.alloc_psum_tensor("ps", [128, 4096], F32)
    PS = PsumPool(ps_t)

    def bank(b, n=1):
        return ps_t[:, b * 512:(b + n) * 512]

    def bankbf(b):
        return ps_t[:, b * 512:(b + 1) * 512].bitcast(BF16)

    def pk(b, n=1):
        return [("ps", b + i) for i in range(n)]

    hTo = A.alloc("hTo", [128, 8, TOWN], BF16)
    HTP_OFF = A.mark()
    hTp = A.alloc("hTp", [128, 8, TOWN], BF16)
    cst32 = A.alloc("cst32", [128, 648], F32)
    cstb = A.alloc("cstb", [128, 512], BF16)
    ones_b = A.alloc("ones_b", [128, 128], BF16)
    maskP = A.alloc("maskP", [128, 128], BF16)
    gx = A.alloc("gx", [128, 8], F32)
    go = A.alloc("go", [128, 2], F32)
    gqk = A.alloc("gqk", [128, 2], F32)
    flag = A.alloc("flag", [128, 1], F32)
    negC = A.alloc("negC", [128, 2], F32)
    ones32 = A.alloc("ones32", [1, 128], F32)
    wa2s = A.alloc("wa2s", [16, 512], F32)
    gqkrow = A.alloc("gqkrow", [1, 256], F32)
    mx = A.alloc("mx", [1, 4], F32)
    epsc = A.alloc("epsc", [128, 1], F32)
    posi = A.alloc("posi", [128, 512], I32)
    scanm = A.alloc("scanm", [128, 512], F32)
    onec = A.alloc("onec", [128, 1], F32)
    M_PP = A.alloc("M_PP", [128, 512], BF16)
    M_PL = A.alloc("M_PL", [128, 512], BF16)
    M_LL = A.alloc("M_LL", [128, 512], BF16)
    odT = A.alloc("odT", [128, 4, TOWN], BF16)
    ARENA_B = A.mark()
    oaT = A.alloc("oaT", [128, 8, TOWN], BF16)
    ARENA_A = A.mark()
    A.reset(ARENA_B)
    cosT = A.alloc("cosT", [128, TALL], BF16)
    sinT = A.alloc("sinT", [128, TALL], BF16)
    ARENA_B2 = A.mark()

    ident32 = cst32[:, 0:128]
    triU32 = cst32[:, 128:256]
    triL32 = cst32[:, 256:384]
    triN32 = cst32[:, 512:640]
    invf = cst32[:, 640:641]
    ident_b = cstb[:, 0:128]
    triU_b = cstb[:, 128:256]
    triL_b = cstb[:, 256:384]
    RT_b = cstb[:, 384:512]

    def hT(k, t0, n, step=1):
        last = t0 + (n - 1) * step
        if t0 >= TOWN:
            a = t0 - TOWN
            return hTo[:, k, a:a + (n - 1) * step + 1:step] if step > 1 else hTo[:, k, a:a + n]
        assert last < TOWN
        return hTp[:, k, t0:t0 + (n - 1) * step + 1:step] if step > 1 else hTp[:, k, t0:t0 + n]

    def hkey(t0):
        return ("hT", t0 // 512)

    def dma(out, in_, r, w, eng="sp"):
        return R.add(eng, lambda e: e.dma_start(out=out, in_=in_), r, w, dma=True)

    def mm_group(out, pairs, r, w):
        n = len(pairs)

        def fn(e):
            ins = None
            for i, (l, rr) in enumerate(pairs):
                ins = e.matmul(out, l, rr, start=(i == 0), stop=(i == n - 1))
            return ins

        return R.add("pe", fn, r, w)

    def act(out, in_, func, r, w, bias=None, scale=None):
        kw = {}
        if bias is not None:
            kw["bias"] = bias
        if scale is not None:
            kw["scale"] = scale
        return R.add("act", lambda e: e.activation(out, in_, func, **kw), r, w)

    def tt(eng, out, in0, in1, op, r, w):
        return R.add(eng, lambda e: e.tensor_tensor(out, in0, in1, op), r, w)

    def ts(eng, out, in0, s1, op0, r, w, s2=None, op1=None):
        if op1 is None:
            return R.add(eng, lambda e: e.tensor_scalar(out, in0, s1, None, op0), r, w)
        return R.add(eng, lambda e: e.tensor_scalar(out, in0, s1, s2, op0, op1), r, w)

    def stt(out, in0, scalar, in1, op0, op1, r, w):
        return R.add("dve", lambda e: e.scalar_tensor_tensor(out, in0, scalar, in1, op0, op1), r, w)

    def cp(eng, out, in_, r, w):
        if eng == "act":
            return R.add(eng, lambda e: e.activation(out, in_, AF.Copy), r, w)
        return R.add(eng, lambda e: e.tensor_copy(out, in_), r, w)

    dma(cst32[:, :], consts_d[:, :], [], ["cst32"])
    dma(gx[:, :], gx_d[:, :], [], ["gx"])
    dma(go[:, :], go_d[:, :], [], ["go"])
    dma(gqk[:, :], gqk_d[:, :], [], ["gqk"])
    dma(flag[:, :], flag_d[:, :], [], ["flag"])
    dma(wa2s[:, :], w_a2_d[:, :], [], ["wa2s"])
    dma(gqkrow[:, :], gqk_row_d[:, :], [], ["gqkrow"])
    cp("dve", cstb[:, :], cst32[:, 0:512], ["cst32"], ["cstb"])
    R.add("dve", lambda e: e.memset(ones_b[:, :], 1.0), [], ["ones_b"])
    R.add("dve", lambda e: e.memset(ones32[:, :], 1.0), [], ["ones32"])
    R.add("dve", lambda e: e.memset(epsc[:, :], EPS), [], ["epsc"])
    R.add("dve", lambda e: e.memset(scanm[:, :], 1.0), [], ["scanm"])
    for j in range(4):
        R.add("dve", lambda e, j=j: e.memset(scanm[:, j * 128:j * 128 + 1], 0.0), ["scanm"], ["scanm"])
    R.add("dve", lambda e: e.memset(onec[:, :], 1.0), [], ["onec"])
    ts("dve", maskP[:, :], triL32, flag[:, 0:1], ALU.mult, ["cst32", "flag"], ["maskP"])
    for (M, first, second) in ((M_PP, "P", "P"), (M_PL, "P", "L"), (M_LL, "L", "L")):
        for u, kind in enumerate((first, second)):
            if kind == "P":
                ts("dve", M[:, u * 256:u * 256 + 128], triL32, flag[:, 0:1], ALU.mult, ["cst32", "flag"], ["masks"])
            else:
                cp("dve", M[:, u * 256:u * 256 + 128], triL32, ["cst32"], ["masks"])
            cp("dve", M[:, u * 256 + 128:u * 256 + 256], triU32, ["cst32"], ["masks"])
    R.add("dve", lambda e: e.tensor_reduce(mx[:, 0:1], gqkrow[:, 0:128], AX.X, ALU.max,
                                           apply_absolute_value=True), ["gqkrow"], ["mx0"])
    R.add("dve", lambda e: e.tensor_reduce(mx[:, 1:2], gqkrow[:, 128:256], AX.X, ALU.max,
                                           apply_absolute_value=True), ["gqkrow"], ["mx1"])
    ts("dve", mx[:, 2:3], mx[:, 0:1], mx[:, 1:2], ALU.mult, ["mx0", "mx1"], ["mx2"],
       s2=-math.sqrt(128.0), op1=ALU.mult)
    cp("dve", mx[:, 3:4], mx[:, 2:3], ["mx2"], ["mx3"])
    b0 = PS.alloc()
    mm_group(bank(b0)[:, 0:2], [(ones32[0:1, 0:128], mx[0:1, 2:4])], ["ones32", "mx2", "mx3"], pk(b0))
    cp("dve", negC[:, :], bank(b0)[:, 0:2], pk(b0), ["negC"])
    PS.release(b0)


    out_keys = []

    def dump(nm, t, keys):
        dd = dram("dbg_" + nm, list(t.shape), t.dtype, kind="ExternalOutput")
        dbg_out[nm] = dd
        dma(dd, t.ap(), keys, [("dbg", nm)])

    def finish():
        R.add("sp", None, [("dbg", nm) for nm in dbg_out] + out_keys, [])
        R.emit(nc)
        return nc, dbg_out

    def recip(t_ap, key):
        R.add("dve", lambda e: e.reciprocal(t_ap, t_ap), [key], [key])

    def mk(name, shape, dt, n=2):
        return [A.alloc(name, shape, dt) for _ in range(n)]

    class Loader:
        def __init__(self, stg_):
            self.stg, self.q, self.nd, self.ncast = stg_, [], 0, 0

        def enqueue(self, dst, src, dkey, extra=()):
            self.q.append((dst, src, dkey, tuple(extra)))

        def tick(self, n=1):
            for _ in range(n):
                if self.nd < len(self.q) and self.nd - self.ncast < 2:
                    dst, src, dkey, extra = self.q[self.nd]
                    sl = self.nd % 2
                    K_, n_ = dst.shape[1], dst.shape[2]
                    dma(self.stg[sl][:, 0:K_, 0:n_], src, [], [("stg", sl)])
                    self.nd += 1
                elif self.ncast < self.nd:
                    dst, src, dkey, extra = self.q[self.ncast]
                    sl = self.ncast % 2
                    K_, n_ = dst.shape[1], dst.shape[2]
                    cp("act", dst, self.stg[sl][:, 0:K_, 0:n_], [("stg", sl)], [dkey] + list(extra))
                    self.ncast += 1

        def flush(self):
            while self.ncast < len(self.q):
                self.tick()

    xTv = xT.rearrange("(k p) t -> p k t", p=128)
    w_in_v = w_in.rearrange("(k p) c -> p k c", p=128)
    w_go_v = w_go_d.rearrange("(k p) c -> p k c", p=128)
    w_do_v = w_do_d.rearrange("(k p) c -> p k c", p=128)
    w_o_v = w_o_d.rearrange("(k p) c -> p k c", p=128)

    A.reset(ARENA_B2)
    stgB = mk("stg", [128, 8, 128], F32)
    wB0_p = A.alloc("wB", [128, 8, 384], BF16)
    wz_p = A.alloc("wz", [128, 8, 128], BF16)
    B_PRE_END = A.mark()
    WL0 = Loader(stgB)
    WL0.enqueue(wz_p.ap(), w_in_v[:, :, OFF_ZD:OFF_ZD + 128], "wz")
    for (d0, s0, nm) in ((0, OFF_QD, "q"), (128, OFF_KD, "k"), (256, OFF_VD, "v")):
        WL0.enqueue(wB0_p[:, :, d0:d0 + 128], w_in_v[:, :, s0:s0 + 128], ("wB", 0, nm))
    ang_r = mk("ang", [128, 512], F32)
    kk_r = mk("kk", [128, 512], F32)
    xs_r = mk("xs", [128, 8, 512], F32, 3)
    sq_r = mk("sq", [128, 8, 512], BF16)
    rb_r = mk("rb", [128, 512], F32)

    def tab1(qq):
        ang, kk = ang_r[qq % 2], kk_r[qq % 2]
        ka, kkk = ("ang", qq % 2), ("kk", qq % 2)
        cs_ = slice(qq * 512, (qq + 1) * 512)
        dma(posi[:, :], pos[:, cs_].partition_broadcast(128)[:, 0, :], [], ["posi"])
        ts("dve", ang[:, :], posi[:, :], invf, ALU.mult, ["posi"], [ka])
        ts("dve", kk[:, :], ang[:, :], 1.0 / TWO_PI, ALU.mult, [ka], [kkk], s2=MAGIC, op1=ALU.add)
        ts("dve", kk[:, :], kk[:, :], MAGIC, ALU.subtract, [kkk], [kkk])
        stt(ang[:, :], kk[:, :], -CW1, ang[:, :], ALU.mult, ALU.add, [kkk, ka], [ka])
        stt(ang[:, :], kk[:, :], -CW2, ang[:, :], ALU.mult, ALU.add, [kkk, ka], [ka])
        ts("dve", ang[:, :], ang[:, :], math.pi, ALU.min, [ka], [ka], s2=-math.pi, op1=ALU.max)

    def tab2(qq):
        ang, kk = ang_r[qq % 2], kk_r[qq % 2]
        ka, kkk = ("ang", qq % 2), ("kk", qq % 2)
        cs_ = slice(qq * 512, (qq + 1) * 512)
        act(sinT[:, cs_], ang[:, :], AF.Sin, [ka], [("sinT", qq)])
        act(kk[:, :], ang[:, :], AF.Sin, [ka], [kkk], scale=0.5)

    def tab3(qq):
        kk = kk_r[qq % 2]
        kkk = ("kk", qq % 2)
        cs_ = slice(qq * 512, (qq + 1) * 512)
        tt("dve", kk[:, :], kk[:, :], kk[:, :], ALU.mult, [kkk], [kkk])
        ts("dve", cosT[:, cs_], kk[:, :], -2.0, ALU.mult, [kkk], [("cosT", qq)], s2=1.0, op1=ALU.add)

    def p0_x1(G):
        s, s3 = G % 2, G % 3
        xs, sq = xs_r[s3], sq_r[s]
        if G >= 2:
            dma(xs.ap(), xTv[:, :, G * 512:(G + 1) * 512], [("xs", (G - 2) % 3)], [("xs", s3)])
        act(sq.ap(), xs.ap(), AF.Square, [("xs", s3)], [("sq", s)])

    def p0_x2(G):
        s = G % 2
        sq, rb = sq_r[s], rb_r[s]
        b = PS.alloc()
        mm_group(bank(b), [(ones_b.ap(), sq[:, k, :]) for k in range(8)], [("sq", s)], pk(b))
        act(rb.ap(), bank(b), AF.Ln, pk(b), [("rb", s)], bias=epsc[:, 0:1], scale=1.0 / D)
        PS.release(b)
        act(rb.ap(), rb.ap(), AF.Exp, [("rb", s)], [("rb", s)], scale=-0.5)

    xg_r = mk("xg", [128, 512], F32)

    def p0_y(G):
        s, s3 = G % 2, G % 3
        xs, rb = xs_r[s3], rb_r[s]
        for k in range(6):
            stt(hT(k, G * 512, 512), xs[:, k, :], gx[:, k:k + 1], rb.ap(), ALU.mult, ALU.mult,
                [("xs", s3), ("rb", s)], [hkey(G * 512)])
        for k in (6, 7):
            xg = xg_r[k % 2]
            act(xg.ap(), xs[:, k, :], AF.Copy, [("xs", s3)], [("xg", k % 2)], scale=gx[:, k:k + 1])
            tt("pool", hT(k, G * 512, 512), xg.ap(), rb.ap(), ALU.mult, [("xg", k % 2), ("rb", s)], [hkey(G * 512)])

    dma(xs_r[0].ap(), xTv[:, :, 0:512], [], [("xs", 0)])
    dma(xs_r[1].ap(), xTv[:, :, 512:1024], [("xs", 0)], [("xs", 1)])
    R.barrier()
    tab1(0)
    p0_x1(0)
    p0_x2(0)
    tab1(1)
    for G in range(8):
        if G + 1 < 8:
            p0_x1(G + 1)
        tab2(G)
        p0_y(G)
        if G + 1 < 8:
            p0_x2(G + 1)
        if G == 1:
            WL0.tick(2)
        if G == 6:
            WL0.tick(4)
        tab3(G)
        if G + 2 < 8:
            tab1(G + 2)
    WL0.flush()
    if debug == "p0":
        dump("sinT", sinT, [("sinT", q) for q in range(8)])
        dump("cosT", cosT, [("cosT", q) for q in range(8)])
        dump("hTo", hTo, [hkey(t) for t in range(0, TALL, 512)])
        dump("hTp", hTp, [hkey(t) for t in range(0, TALL, 512)])
        return finish()

    R.barrier()
    A.reset(B_PRE_END)
    stg = stgB
    WL = Loader(stg)
    load_block = WL.enqueue

    wB0 = wB0_p
    wz = wz_p
    wB = [wB0, A.alloc("wB", [128, 8, 384], BF16)]
    qr = A.alloc("qr", [128, TOWN], BF16)
    kr = A.alloc("kr", [128, TALL], BF16)
    vt = A.alloc("vt", [128, 32 * 128], BF16)
    Oacc = A.alloc("Oacc", [128, TOWN], F32)
    Lacc = A.alloc("Lacc", [128, TOWN], F32)
    silz_r = mk("silz", [128, TOWN], BF16)
    nsq = mk("nsq", [128, 512], BF16, 4)
    nrs = mk("nrs", [128, 512], F32)
    kn = mk("kn", [128, 512], BF16, 3)
    t1 = mk("t1", [128, 512], F32)
    t2 = mk("t2", [128, 512], F32)
    pT = mk("pT", [128, 512], BF16, 6)

    def load_B(hd, g, st):
        n = g * 4 + hd
        for (d0, s0, nm) in ((0, OFF_QD + n * 128, "q"), (128, OFF_KD + n * 128, "k"), (256, OFF_VD + n * 128, "v")):
            load_block(wB[st][:, :, d0:d0 + 128], w_in_v[:, :, s0:s0 + 128], ("wB", st, nm))

    allh = [hkey(t) for t in range(0, TALL, 512)]
    wcnt = [0]
    pcnt = [0]

    def combine(hd):
        silz = silz_r[hd % 2]
        for G in range(4):
            gs = slice(G * 512, (G + 1) * 512)
            i2 = G % 2
            act(nrs[i2].ap(), Lacc[:, gs], AF.Ln, ["Lacc", ("nrs", i2)], [("nrs", i2)])
            act(nrs[i2].ap(), nrs[i2].ap(), AF.Exp, [("nrs", i2)], [("nrs", i2)], scale=-1.0)
            tt("dve", t1[i2].ap(), Oacc[:, gs], nrs[i2].ap(), ALU.mult, ["Oacc", ("nrs", i2)], [("t1", i2)])
            tt("dve", odT[:, hd, gs], t1[i2].ap(), silz[:, gs], ALU.mult, [("t1", i2), ("silz", hd % 2, G)], [("odT", hd, G)])

    for hd in range(4):
        silz = silz_r[hd % 2]
        WL.flush()
        for G in range(4):
            b = PS.alloc()
            mm_group(bank(b), [(wz[:, k, :], hT(k, TOWN + G * 512, 512)) for k in range(8)], ["wz", hkey(TOWN + G * 512)], pk(b))
            act(silz[:, G * 512:(G + 1) * 512], bank(b), AF.Silu, pk(b), [("silz", hd % 2, G)])
            PS.release(b)
        for g in range(3):
            st = wcnt[0] % 2
            wcnt[0] += 1
            W = wB[st]
            dil = 4 ** g
            Pn = 128 * dil
            nb = 16 // dil
            nxt = (hd, g + 1) if g < 2 else ((hd + 1, 0) if hd < 3 else None)
            WL.flush()
            if nxt is not None:
                load_B(nxt[0], nxt[1], wcnt[0] % 2)
            if g == 2 and hd < 3:
                load_block(wz.ap(), w_in_v[:, :, OFF_ZD + (hd + 1) * 128:OFF_ZD + (hd + 2) * 128], "wz")
            kp = [(TOWN - Pn, 128)] if Pn == 128 else [(TOWN - Pn + i * 512, 512) for i in range(Pn // 512)]
            kp += [(TOWN + i * 512, 512) for i in range(4)]
            pieces = [dict(t0=t0, n=n, dst=kr[:, t0:t0 + n], dkey=("kr", t0 // 128), c0=128, wk=("wB", st, "k"),
                           gcol=gqk[:, 1:2]) for (t0, n) in kp]
            pieces += [dict(t0=TOWN + i * 512, n=512, dst=qr[:, i * 512:(i + 1) * 512], dkey=("qr", i), c0=0,
                            wk=("wB", st, "q"), gcol=gqk[:, 0:1]) for i in range(4)]
            krkeys = [("kr", t // 128) for (t, n) in kp]
            qrkeys = [("qr", i) for i in range(4)]

            def st_a(p):
                p["id"] = pcnt[0]
                pcnt[0] += 1
                i3, n, t0 = p["id"] % 4, p["n"], p["t0"]
                p["bp"] = PS.alloc()
                mm_group(bank(p["bp"])[:, 0:n], [(W[:, k, p["c0"]:p["c0"] + 128], hT(k, t0, n)) for k in range(8)],
                         [p["wk"], hkey(t0)], pk(p["bp"]))
                act(nsq[i3][:, 0:n], bank(p["bp"])[:, 0:n], AF.Square, pk(p["bp"]), [("nsq", i3)])

            def st_b(p):
                i2, i3, ik, n, t0, bp = p["id"] % 2, p["id"] % 4, p["id"] % 3, p["n"], p["t0"], p["bp"]
                bs = PS.alloc()
                mm_group(bank(bs)[:, 0:n], [(ones_b.ap(), nsq[i3][:, 0:n])], [("nsq", i3)], pk(bs))
                act(nrs[i2][:, 0:n], bank(bs)[:, 0:n], AF.Ln, pk(bs), [("nrs", i2)], bias=epsc[:, 0:1], scale=1.0 / 128.0)
                PS.release(bs)
                act(nrs[i2][:, 0:n], nrs[i2][:, 0:n], AF.Exp, [("nrs", i2)], [("nrs", i2)], scale=-0.5)
                stt(kn[ik][:, 0:n], bank(bp)[:, 0:n], p["gcol"], nrs[i2][:, 0:n], ALU.mult, ALU.mult,
                    pk(bp) + [("nrs", i2)], [("kn", ik)])
                PS.release(bp)

            def st_c(p):
                i2, ik, n, t0 = p["id"] % 2, p["id"] % 3, p["n"], p["t0"]
                br = PS.alloc()
                mm_group(bank(br)[:, 0:n], [(RT_b, kn[ik][:, 0:n])], [("kn", ik)], pk(br))
                tt("pool", t1[i2][:, 0:n], kn[ik][:, 0:n], cosT[:, t0:t0 + n], ALU.mult, [("kn", ik)], [("t1", i2)])
                tt("dve", t2[i2][:, 0:n], bank(br)[:, 0:n], sinT[:, t0:t0 + n], ALU.mult, pk(br), [("t2", i2)])
                PS.release(br)
                tt("dve", p["dst"], t1[i2][:, 0:n], t2[i2][:, 0:n], ALU.add,
                   [("t1", i2), ("t2", i2)], [p["dkey"]])

            blocks = [(r, b) for r in range(dil) for b in range(-1, nb)]

            def v_chunk(i0):
                chunk = blocks[i0:i0 + 4]
                bv = PS.alloc()
                for qi, (r, b) in enumerate(chunk):
                    tau0 = TOWN + 128 * b * dil + r
                    mm_group(bank(bv)[:, qi * 128:(qi + 1) * 128],
                             [(hT(k, tau0, 128, dil), W[:, k, 256:384]) for k in range(8)],
                             [("wB", st, "v")] + allh, pk(bv))
                cp("act", vt[:, i0 * 128:(i0 + len(chunk)) * 128],
                   bank(bv)[:, 0:len(chunk) * 128], pk(bv), [("vt", i0 // 4)])
                PS.release(bv)

            v_list = list(range(0, len(blocks), 4))
            v_pos = [0]

            def v_some(n, keep=0):
                for _ in range(n):
                    if v_pos[0] < len(v_list) - keep:
                        v_chunk(v_list[v_pos[0]])
                        v_pos[0] += 1

            NP = len(pieces)
            for i in range(NP + 4):
                WL.tick(2)
                if i < NP:
                    st_a(pieces[i])
                else:
                    v_some(1, keep=1)
                if 0 <= i - 2 < NP:
                    st_b(pieces[i - 2])
                if 0 <= i - 4 < NP:
                    st_c(pieces[i - 4])
            v_some(len(v_list), keep=2)
            if g == 1:
                qblocks = [(r, b) for b in range(nb) for r in range(dil)]
            else:
                qblocks = [(r, b) for r in range(dil) for b in range(nb)]
            npair = len(qblocks) // 2
            sc = float(128.0 ** -0.5)
            state = {}

            def scores(i):
                bsT = PS.alloc()
                for u in range(2):
                    r, b = qblocks[2 * i + u]
                    q0 = 128 * b * dil + r
                    qs = qr[:, q0:q0 + 127 * dil + 1:dil]
                    tp = TOWN + 128 * (b - 1) * dil + r
                    tc = TOWN + 128 * b * dil + r
                    qk = [("qr", t // 512) for t in range(q0 - q0 % 512, q0 + 127 * dil + 1, 512)]
                    kk1 = [("kr", t0_ // 128) for (t0_, n_) in kp if t0_ < tp + 127 * dil + 1 and t0_ + n_ > tp]
                    kk2 = [("kr", t0_ // 128) for (t0_, n_) in kp if t0_ < tc + 127 * dil + 1 and t0_ + n_ > tc]
                    mm_group(bank(bsT)[:, u * 256:u * 256 + 128], [(kr[:, tp:tp + 127 * dil + 1:dil], qs)],
                             kk1 + qk, pk(bsT))
                    mm_group(bank(bsT)[:, u * 256 + 128:u * 256 + 256], [(kr[:, tc:tc + 127 * dil + 1:dil], qs)],
                             kk2 + qk, pk(bsT))
                p = pT[i % 6]
                act(p.ap(), bank(bsT), AF.Exp, pk(bsT), [("pT", i % 6)], bias=negC[:, 0:1], scale=sc)
                PS.release(bsT)
                f0 = qblocks[2 * i][1] == 0
                f1 = qblocks[2 * i + 1][1] == 0
                assert not (f1 and not f0)
                M = M_PP if (f0 and f1) else (M_PL if f0 else M_LL)
                tt("dve" if i % 4 else "pool", p.ap(), p.ap(), M.ap(), ALU.mult, [("pT", i % 6)], [("pT", i % 6)])

            def pv(i):
                p = pT[i % 6]
                for u in range(2):
                    j = 2 * i + u
                    r, b = qblocks[j]
                    qi = j % 4
                    if qi == 0:
                        state["bO"] = PS.alloc()
                        state["bL"] = PS.alloc()
                    bO, bL = state["bO"], state["bL"]
                    vp = (r * (nb + 1) + b) * 128
                    vc = (r * (nb + 1) + b + 1) * 128
                    mm_group(bank(bO)[:, qi * 128:(qi + 1) * 128],
                             [(vt[:, vp:vp + 128], p[:, u * 256:u * 256 + 128]),
                              (vt[:, vc:vc + 128], p[:, u * 256 + 128:u * 256 + 256])],
                             [("vt", (vp // 128) // 4), ("vt", (vc // 128) // 4), ("pT", i % 6)], pk(bO))
                    mm_group(bank(bL)[:, qi * 128:(qi + 1) * 128],
                             [(ones_b.ap(), p[:, u * 256:u * 256 + 128]), (ones_b.ap(), p[:, u * 256 + 128:u * 256 + 256])],
                             [("pT", i % 6)], pk(bL))
                    if qi == 3:
                        r0, b0_ = qblocks[j - 3]
                        if g == 0:
                            ov = Oacc[:, b0_ * 128:b0_ * 128 + 512]
                            lv = Lacc[:, b0_ * 128:b0_ * 128 + 512]
                            so, sl = bank(bO), bank(bL)
                        elif g == 1:
                            ov = Oacc[:, b0_ * 512:(b0_ + 1) * 512].rearrange("p (i r) -> p r i", r=4)
                            lv = Lacc[:, b0_ * 512:(b0_ + 1) * 512].rearrange("p (i r) -> p r i", r=4)
                            so = bank(bO).rearrange("p (r i) -> p r i", r=4)
                            sl = bank(bL).rearrange("p (r i) -> p r i", r=4)
                        else:
                            ov = Oacc.ap().rearrange("p (i r) -> p r i", r=16)[:, r0:r0 + 4, :]
                            lv = Lacc.ap().rearrange("p (i r) -> p r i", r=16)[:, r0:r0 + 4, :]
                            so = bank(bO).rearrange("p (r i) -> p r i", r=4)
                            sl = bank(bL).rearrange("p (r i) -> p r i", r=4)
                        if g == 0:
                            cp("dve", ov, so, pk(bO), ["Oacc"])
                            cp("act", lv, sl, pk(bL), ["Lacc"])
                        else:
                            tt("dve", ov, so, ov, ALU.add, pk(bO) + ["Oacc"], ["Oacc"])
                            tt("dve", lv, sl, lv, ALU.add, pk(bL) + ["Lacc"], ["Lacc"])
                        PS.release(bO)
                        PS.release(bL)

            LOOK = 4
            for i in range(min(LOOK, npair)):
                scores(i)
            v_some(2)
            for i in range(npair):
                if i + LOOK < npair:
                    scores(i + LOOK)
                pv(i)
                if i == 0 and g == 0 and hd >= 1:
                    combine(hd - 1)
    combine(3)
    if debug == "pB":
        dump("odT", odT, [("odT", c, g) for c in range(4) for g in range(4)])
        return finish()

    R.barrier()
    A.reset(ARENA_A)
    stg = mk("stg", [128, 8, 128], F32)
    WL = Loader(stg)
    load_block = WL.enqueue

    WA0_OFF = A.mark()
    wA = mk("wA", [128, 8, 768], BF16)
    wC0 = nc.alloc_sbuf_tensor_at("wC0_alias", [128, 28, 128], BF16, offset=WA0_OFF)
    alrT = A.alloc("alrT", [128, TALL], BF16)
    walr = A.alloc("walr", [128, 8, 128], BF16)
    wa2p = A.alloc("wa2p", [128, 512], BF16)
    nba = A.alloc("nba", [128, 4], F32)
    e32 = A.alloc("e32", [128, 512], F32)
    cum = A.alloc("cum", [128, 512], F32)
    eb = A.alloc("eb", [128, 512], F32)
    enb = A.alloc("enb", [128, 512], F32)
    decs = A.alloc("decs", [128, 4 * 32], F32)
    kTt = mk("kTt", [128, 512], BF16)
    qTt = mk("qTt", [128, 512], BF16)
    ktok = mk("ktok", [128, 512], BF16)
    vtok = mk("vtok", [128, 1024], BF16)
    silr = mk("silr", [128, 1024], BF16)
    attn_sb = mk("attn", [128, 128], BF16, 4)
    osq = A.alloc("osq", [128, 1024], BF16)
    rso = A.alloc("rso", [128, 512], F32)
    rs = A.alloc("rs", [128, 1024], F32)
    Tst = mk("Tst", [128, 256], F32)
    Sbf = mk("Sbf", [128, 256], BF16, 5)

    R.add("pool", lambda e: e.memset(walr.ap(), 0.0), [], ["walr"])
    R.add("pool", lambda e: e.memset(wa2p.ap(), 0.0), [], ["wa2p"])
    cp("dve", wa2p[0:16, :], wa2s[0:16, :], ["wa2p"], ["wa2p"])
    dma(nba.ap(), b_aT_d[:, :], [], ["nba"])
    ts("dve", nba.ap(), nba.ap(), -1.0, ALU.mult, ["nba"], ["nba"])
    def load_head(h):
        st = h % 2
        for (d0, s0, nm) in ((0, OFF_QA + h * 128, "q"), (128, OFF_KA + h * 128, "k"),
                             (256, OFF_VA + h * 256, "v0"), (384, OFF_VA + h * 256 + 128, "v1"),
                             (512, OFF_RA + h * 256, "r0"), (640, OFF_RA + h * 256 + 128, "r1")):
            load_block(wA[st][:, :, d0:d0 + 128], w_in_v[:, :, s0:s0 + 128], ("wA", st, nm))

    load_block(walr[:, :, 0:16], w_in_v[:, :, OFF_ALR:OFF_ALR + 16], "walr")
    load_head(0)
    WL.tick(3)
    for G in range(8):
        WL.tick(2)
        b = PS.alloc()
        mm_group(bank(b), [(walr[:, k, :], hT(k, G * 512, 512)) for k in range(8)],
                 ["walr", hkey(G * 512)], pk(b))
        act(alrT[:, G * 512:(G + 1) * 512], bank(b), AF.Copy, pk(b), [("alrT", G)])
        PS.release(b)
    WL.flush()

    def stage1(h, G):
        st, s, own, t0 = h % 2, G % 2, G >= 4, G * 512
        W = wA[st]
        hk = hkey(t0)
        bz = PS.alloc()
        mm_group(bank(bz), [(wa2p[:, h * 128:(h + 1) * 128], alrT[:, t0:t0 + 512])], [("alrT", G), "wa2p"], pk(bz))
        act(e32.ap(), bank(bz), AF.Exp, pk(bz) + ["nba"], ["e32"], scale=-1.0, bias=nba[:, h:h + 1])
        PS.release(bz)
        act(e32.ap(), e32.ap(), AF.Ln, ["e32"], ["e32"], bias=onec[:, 0:1])
        R.add("dve", lambda e: e.tensor_tensor_scan(cum.ap(), scanm.ap(), e32.ap(), 0.0, ALU.mult, ALU.add),
              ["e32"], ["cum"])
        bk = PS.alloc()
        mm_group(bank(bk), [(W[:, k, 128:256], hT(k, t0, 512)) for k in range(8)], [("wA", st, "k"), hk], pk(bk))
        act(enb.ap(), cum.ap(), AF.Exp, ["cum"], ["enb"], scale=1.0 / 16.0)
        if own:
            act(eb.ap(), cum.ap(), AF.Exp, ["cum"], ["eb"], scale=-1.0 / 16.0)
        act(decs[:, h * 32 + G * 4:h * 32 + G * 4 + 4], cum[:, 127:512:128], AF.Exp, ["cum"], [("dec", h, G)],
            scale=-1.0 / 16.0)
        tt("dve", kTt[s].ap(), bank(bk), enb.ap(), ALU.mult, pk(bk) + ["enb"], [("kTt", s)])
        PS.release(bk)
        yield
        if own:
            bq = PS.alloc()
            mm_group(bank(bq), [(W[:, k, 0:128], hT(k, t0, 512)) for k in range(8)], [("wA", st, "q"), hk], pk(bq))
            stt(qTt[s].ap(), bank(bq), float(128.0 ** -0.5), eb.ap(), ALU.mult, ALU.mult, pk(bq) + ["eb"], [("qTt", s)])
            PS.release(bq)
            for c in range(2):
                br = PS.alloc()
                mm_group(bank(br), [(W[:, k, 512 + c * 128:640 + c * 128], hT(k, t0, 512)) for k in range(8)],
                         [("wA", st, "r%d" % c), hk], pk(br))
                act(silr[s][:, c * 512:(c + 1) * 512], bank(br), AF.Silu, pk(br), [("silr", s, c)])
                PS.release(br)
                if c == 0:
                    yield
        else:
            yield
        yield
        for jj in range(2):
            bv = PS.alloc()
            for j in (2 * jj, 2 * jj + 1):
                mm_group(bank(bv)[:, (j % 2) * 256:(j % 2 + 1) * 256],
                         [(hT(k, t0 + j * 128, 128), W[:, k, 256:512]) for k in range(8)],
                         [("wA", st, "v0"), ("wA", st, "v1"), hk], pk(bv))
            act(vtok[s][:, jj * 512:(jj + 1) * 512], bank(bv), AF.Copy, pk(bv), [("vtok", s, jj)])
            PS.release(bv)
            if jj == 0:
                yield

    def front(h, G):
        s, own, t0 = G % 2, G >= 4, G * 512
        if G == 0:
            R.add("pool", lambda e: e.memset(Sbf[0].ap(), 0.0), [], [("Sbf", 0)])
        bt = PS.alloc()

        def ftr(e):
            ins = None
            for j in range(4):
                ins = e.transpose(bankbf(bt)[:, j * 128:(j + 1) * 128], kTt[s][:, j * 128:(j + 1) * 128], ident_b)
            return ins
        R.add("pe", ftr, [("kTt", s)], pk(bt))
        cp("dve", ktok[s].ap(), bankbf(bt)[:, 0:512], pk(bt), [("ktok", s)])
        PS.release(bt)
        if own:
            ba_ = PS.alloc()
            for j in range(4):
                js = slice(j * 128, (j + 1) * 128)
                mm_group(bank(ba_)[:, js], [(kTt[s][:, js], qTt[s][:, js])], [("kTt", s), ("qTt", s)], pk(ba_))
            for j in range(4):
                js = slice(j * 128, (j + 1) * 128)
                tt("dve", attn_sb[j].ap(), bank(ba_)[:, js], triU32, ALU.mult, pk(ba_), [("attn", j)])
            PS.release(ba_)
        bkv = PS.alloc(2)
        for j in range(4):
            js = slice(j * 128, (j + 1) * 128)
            mm_group(bank(bkv, 2)[:, j * 256:(j + 1) * 256], [(ktok[s][:, js], vtok[s][:, j * 256:(j + 1) * 256])],
                     [("ktok", s), ("vtok", s, j // 2)], pk(bkv + j // 2))
        for j in range(4):
            n = G * 4 + j
            kvs = bank(bkv, 2)[:, j * 256:(j + 1) * 256]
            if n == 0:
                cp("dve", Tst[0].ap(), kvs, pk(bkv + j // 2), [("Tst", 0)])
            else:
                stt(Tst[n % 2].ap(), Tst[(n - 1) % 2].ap(), decs[:, h * 32 + n - 1:h * 32 + n], kvs,
                    ALU.mult, ALU.add, [("Tst", (n - 1) % 2), ("dec", h, (n - 1) // 4)] + pk(bkv + j // 2), [("Tst", n % 2)])
            if n < 31:
                ts("pool", Sbf[(n + 1) % 5].ap(), Tst[n % 2].ap(), decs[:, h * 32 + n:h * 32 + n + 1], ALU.mult,
                   [("Tst", n % 2), ("dec", h, G)], [("Sbf", (n + 1) % 5)], s2=1.0, op1=ALU.mult)
        PS.release(bkv, 2)

    bo_of = {}

    def mid(h, G, gen):
        s, own = G % 2, G >= 4
        if own:
            bo_of[(h, G)] = PS.alloc(2)
            bo = bo_of[(h, G)]
        for j in range(4):
            n = G * 4 + j
            js = slice(j * 128, (j + 1) * 128)
            WL.tick()
            if gen is not None:
                next(gen, None)
            if own:
                for c in range(2):
                    mm_group(bank(bo + c)[:, js],
                             [(vtok[s][:, j * 256 + c * 128:j * 256 + (c + 1) * 128], attn_sb[j].ap()),
                              (Sbf[n % 5][:, c * 128:(c + 1) * 128], qTt[s][:, js])],
                             [("vtok", s, j // 2), ("attn", j), ("Sbf", n % 5), ("qTt", s)], pk(bo + c))
        if gen is not None:
            for _ in gen:
                pass

    def tail(h, G):
        s, own = G % 2, G >= 4
        if not own:
            return
        bo = bo_of.pop((h, G))
        act(osq.ap(), bank(bo, 2), AF.Square, pk(bo, 2), ["osq"])
        for c in range(2):
            cs = slice(c * 512, (c + 1) * 512)
            stt(rs[:, cs], bank(bo + c), go[:, c:c + 1], silr[s][:, cs], ALU.mult, ALU.mult,
                pk(bo + c) + [("silr", s, c)], [("rs", c)])
        PS.release(bo, 2)
        bs = PS.alloc()
        mm_group(bank(bs), [(ones_b.ap(), osq[:, 0:512]), (ones_b.ap(), osq[:, 512:1024])], ["osq"], pk(bs))
        act(rso.ap(), bank(bs), AF.Ln, pk(bs), ["rso"], bias=epsc[:, 0:1], scale=1.0 / 256.0)
        PS.release(bs)
        act(rso.ap(), rso.ap(), AF.Exp, ["rso"], ["rso"], scale=-0.5)
        for c in range(2):
            cs = slice(c * 512, (c + 1) * 512)
            tt("dve", oaT[:, 2 * h + c, (G - 4) * 512:(G - 3) * 512], rs[:, cs], rso.ap(), ALU.mult,
               [("rs", c), "rso"], [("oaT", 2 * h + c, G - 4)])

    order = [(h, G) for h in range(4) for G in range(8)]
    for _ in stage1(0, 0):
        pass
    front(0, 0)
    for idx, (h, G) in enumerate(order):
        if G == 0 and h + 1 < 4:
            load_head(h + 1)
        if h == 3 and G == 1:
            war = [("wA", 0, nm_) for nm_ in ("q", "k", "v0", "v1", "r0", "r1")]
            load_block(wC0[:, 0:8, :], w_in_v[:, :, OFF_GA:OFF_GA + 128], ("wC", 0, 0), war)
            load_block(wC0[:, 8:16, :], w_in_v[:, :, OFF_GD:OFF_GD + 128], ("wC", 0, 1), war)
            load_block(wC0[:, 16:24, :], w_go_v[:, :, 0:128], ("wC", 0, 2), war)
            load_block(wC0[:, 24:28, :], w_do_v[:, :, 0:128], ("wC", 0, 3), war)
        if G == 7:
            WL.flush()
        gen = stage1(*order[idx + 1]) if idx + 1 < len(order) else None
        mid(h, G, gen)
        if idx + 1 < len(order):
            front(*order[idx + 1])
        tail(h, G)
    WL.flush()
    if debug == "pA":
        dump("oaT", oaT, [("oaT", c, g) for c in range(8) for g in range(4)])
        return finish()

    R.barrier()
    A.reset(ARENA_A)
    yT = nc.alloc_sbuf_tensor_at("yT_alias", [128, 8, TOWN], BF16, offset=HTP_OFF)
    A.reset(WA0_OFF + 28 * 128 * 2)
    wC = [wC0, A.alloc("wC", [128, 28, 128], BF16)]
    sga = mk("sga", [128, 512], F32)
    sgd = mk("sgd", [128, 512], F32)
    u1 = mk("u1", [128, 512], F32)
    u2 = mk("u2", [128, 512], F32)
    wo = mk("wo", [128, 8, 128], BF16, 3)
    xr = mk("xr", [128, TOWN], F32, 3)
    ob = mk("ob", [128, 512], F32)
    it = 0

    def pre_c(c):
        load_block(wo[c % 3].ap(), w_o_v[:, :, c * 128:(c + 1) * 128], ("wo", c % 3))
        dma(xr[c % 3].ap(), xT[c * 128:(c + 1) * 128, TOWN:TALL], [], [("xr", c % 3)])

    def load_c(c):
        s = c % 2
        cs = slice(c * 128, (c + 1) * 128)
        load_block(wC[s][:, 0:8, :], w_in_v[:, :, OFF_GA + c * 128:OFF_GA + (c + 1) * 128], ("wC", s, 0))
        load_block(wC[s][:, 8:16, :], w_in_v[:, :, OFF_GD + c * 128:OFF_GD + (c + 1) * 128], ("wC", s, 1))
        load_block(wC[s][:, 16:24, :], w_go_v[:, :, cs], ("wC", s, 2))
        load_block(wC[s][:, 24:28, :], w_do_v[:, :, cs], ("wC", s, 3))

    for c in range(8):
        s = c % 2
        cs = slice(c * 128, (c + 1) * 128)
        WL.flush()
        if c + 1 < 8:
            load_c(c + 1)
        if c == 6:
            pre_c(0)
        if c == 7:
            pre_c(1)
        for G in range(4):
            WL.tick(2)
            gs = slice(G * 512, (G + 1) * 512)
            i2 = it % 2
            it += 1
            hk = hkey(TOWN + G * 512)
            bga = PS.alloc()
            mm_group(bank(bga), [(wC[s][:, k, :], hT(k, TOWN + G * 512, 512)) for k in range(8)], [("wC", s, 0), hk], pk(bga))
            bgd = PS.alloc()
            mm_group(bank(bgd), [(wC[s][:, 8 + k, :], hT(k, TOWN + G * 512, 512)) for k in range(8)], [("wC", s, 1), hk], pk(bgd))
            bya = PS.alloc()
            mm_group(bank(bya), [(wC[s][:, 16 + k, :], oaT[:, k, gs]) for k in range(8)],
                     [("wC", s, 2)] + [("oaT", k, G) for k in range(8)], pk(bya))
            byd = PS.alloc()
            mm_group(bank(byd), [(wC[s][:, 24 + k, :], odT[:, k, gs]) for k in range(4)],
                     [("wC", s, 3)] + [("odT", k, G) for k in range(4)], pk(byd))
            act(sga[i2].ap(), bank(bga), AF.Sigmoid, pk(bga), [("sga", i2)])
            act(sgd[i2].ap(), bank(bgd), AF.Sigmoid, pk(bgd), [("sgd", i2)])
            PS.release(bga)
            PS.release(bgd)
            tt("dve", u1[i2].ap(), bank(bya), sga[i2].ap(), ALU.mult, pk(bya) + [("sga", i2)], [("u1", i2)])
            tt("dve", u2[i2].ap(), bank(byd), sgd[i2].ap(), ALU.mult, pk(byd) + [("sgd", i2)], [("u2", i2)])
            PS.release(bya)
            PS.release(byd)
            tt("pool", yT[:, c, gs], u1[i2].ap(), u2[i2].ap(), ALU.add, [("u1", i2), ("u2", i2)], [("yT", c, G)])
    it = 0
    for c in range(8):
        s = c % 3
        cs = slice(c * 128, (c + 1) * 128)
        WL.flush()
        for G in range(4):
            WL.tick()
            gs = slice(G * 512, (G + 1) * 512)
            i2 = it % 2
            it += 1
            bo = PS.alloc()
            mm_group(bank(bo), [(wo[s][:, k, :], yT[:, k, gs]) for k in range(8)],
                     [("wo", s)] + [("yT", k, G) for k in range(8)], pk(bo))
            tt("dve", ob[i2].ap(), bank(bo), xr[s][:, gs], ALU.add, pk(bo) + [("xr", s)], [("ob", i2)])
            PS.release(bo)
            dma(outT[cs, gs], ob[i2].ap(), [("ob", i2)], [("out", c, G)], eng="act")
            out_keys.append(("out", c, G))
            if G == 0 and c + 2 < 8:
                pre_c(c + 2)
    return finish()


def make_consts():
    c = np.zeros((128, 648), np.float32)
    j = np.arange(128)[:, None]
    i = np.arange(128)[None, :]
    c[:, 0:128] = (j == i)
    c[:, 128:256] = (j <= i)
    c[:, 256:384] = (j >= i)
    rt = np.zeros((128, 128), np.float32)
    for m in range(64):
        rt[m + 64, m] = -1.0
        rt[m, m + 64] = 1.0
    c[:, 384:512] = rt
    c[:, 512:640] = (j <= i) * (-1.0 / 16.0)
    half = 64
    inv = (10000.0 ** (-(np.arange(half, dtype=np.float32)) / np.float32(half))).astype(np.float32)
    c[:, 640] = np.concatenate([inv, inv])
    return c


def prep_inputs(x, positions, norm_gain, w_in, gla_w_a2, gla_b_a, gla_out_gain,
                dil_q_gain, dil_k_gain, w_gla_out, w_dil_out, w_o):
    consts = make_consts()
    common = {
        "w_in": np.ascontiguousarray(w_in[0]),
        "w_a2": np.ascontiguousarray(gla_w_a2[0]),
        "b_a": np.ascontiguousarray(gla_b_a[0].reshape(1, 512)),
        "b_aT": np.ascontiguousarray(gla_b_a[0].reshape(4, 128).T),
        "gx": np.ascontiguousarray(norm_gain[0].reshape(8, 128).T),
        "go": np.ascontiguousarray(gla_out_gain[0].reshape(2, 128).T),
        "gqk": np.ascontiguousarray(np.stack([dil_q_gain[0], dil_k_gain[0]], axis=1)),
        "gqk_row": np.ascontiguousarray(np.concatenate([dil_q_gain[0], dil_k_gain[0]]).reshape(1, 256)),
        "w_gla_out": np.ascontiguousarray(w_gla_out[0]),
        "w_dil_out": np.ascontiguousarray(w_dil_out[0]),
        "w_o": np.ascontiguousarray(w_o[0]),
        "consts": consts,
    }
    in_maps = []
    for b in range(NB):
        xb = np.asarray(x[b], np.float32)
        pb = np.asarray(positions[b], np.int32)
        for half in range(2):
            xt = np.zeros((D, TALL), np.float32)
            pp = np.zeros((1, TALL), np.int32)
            if half == 0:
                xt[:, TOWN:] = xb[0:TOWN].T
                pp[0, TOWN:] = pb[0:TOWN]
            else:
                xt[:, :] = xb.T
                pp[0, :] = pb
            m = dict(common)
            m["xT"] = xt
            m["pos"] = pp
            m["flag"] = np.full((128, 1), float(half), np.float32)
            in_maps.append(m)
    return in_maps


_NC_CACHE = {}


def kernel(**inputs):
    inputs = {k: np.asarray(v) for k, v in inputs.items()}
    in_maps = prep_inputs(**inputs)
    if "nc" not in _NC_CACHE:
        _NC_CACHE["nc"] = build_program()[0]
    nc = _NC_CACHE["nc"]
    res = run_bass_kernel_spmd(nc, in_maps, core_ids=list(range(8)))
    out = np.zeros((NB, SEQ, D), np.float32)
    for b in range(NB):
        for half in range(2):
            o = np.asarray(res.results[b * 2 + half]["outT"], np.float32)
            out[b, half * TOWN:(half + 1) * TOWN, :] = o.T
    return out
```
